# Optimizing a Trainium2 kernel written in Bass

```python
import math
import jax
import jax.numpy as jnp
from jax import lax
import numpy as np

D_MODEL = 2048
BATCH = 16
SEQ = 2048
DEPTH = 4

N_MIXERS = 3
HEAD_DIM = 128
ROPE_THETA = 500000.0
ROPE_FRACTION = 4
QUERY_BLOCK = 128
NEG_FILL = -1e30

A_HEADS = D_MODEL // HEAD_DIM
A_QK_DIM = HEAD_DIM // 2

B_HEADS = D_MODEL // HEAD_DIM
B_KV_GROUPS = 4
B_CMP_LEN = 32
B_CMP_STRIDE = 16
B_SEL_LEN = 64
B_SEL_TOPN = 16
B_WINDOW = 512
B_SEL_CHUNK = 8
B_N_KV = 6

C_PATTERNS = ((128, 1), (512, 4), (2048, 16))
C_HEADS_PER_GROUP = D_MODEL // (2 * HEAD_DIM)

MOE_GROUPS = 4
MOE_EXPERTS_PER_GROUP = 8
MOE_EXPERTS = MOE_GROUPS * MOE_EXPERTS_PER_GROUP
MOE_TOPK = 2
MOE_D_FF = D_MODEL // 4
MOE_ROW_BLOCK = 256

DN_ALPHA = (2 * DEPTH) ** 0.25
DN_BETA = (8 * DEPTH) ** -0.25
NORM_EPS = 1e-5

kernel_name = 'hybrid_diff_nsa_dilated_hmoe_deepnorm'


def layer_norm(x, g, b):
    xf = x.astype(jnp.float32)
    mu = jnp.mean(xf, axis=-1, keepdims=True)
    var = jnp.mean(jnp.square(xf - mu), axis=-1, keepdims=True)
    return ((xf - mu) * lax.rsqrt(var + NORM_EPS) * g + b).astype(x.dtype)


def rms_norm(x, g):
    xf = x.astype(jnp.float32)
    return (xf * lax.rsqrt(jnp.mean(xf * xf, axis=-1, keepdims=True) + NORM_EPS) * g).astype(x.dtype)


def rope_angles(pos, dim):
    rot = dim // ROPE_FRACTION
    inv_freq = ROPE_THETA ** (-jnp.arange(0, rot, 2, dtype=jnp.float32) / rot)
    ang = pos.astype(jnp.float32)[:, None] * inv_freq[None, :]
    return jnp.cos(ang), jnp.sin(ang)


def partial_rope(x, cos, sin):
    half = cos.shape[-1]
    x1 = x[..., :half].astype(jnp.float32)
    x2 = x[..., half:2 * half].astype(jnp.float32)
    r1 = (x1 * cos - x2 * sin).astype(x.dtype)
    r2 = (x2 * cos + x1 * sin).astype(x.dtype)
    return jnp.concatenate([r1, r2, x[..., 2 * half:]], axis=-1)


def banded_attention(q, k, v, span, block):
    B, G, R, L, hd = q.shape
    nb = L // block
    pad = -(-span // block) * block
    kp = jnp.pad(k, ((0, 0), (0, 0), (pad, 0), (0, 0)))
    vp = jnp.pad(v, ((0, 0), (0, 0), (pad, 0), (0, 0)))
    qb = jnp.moveaxis(q.reshape(B, G, R, nb, block, hd), 3, 0)
    scale = hd ** -0.5
    offs_q = jnp.arange(block)
    offs_k = jnp.arange(pad + block) - pad

    def one(args):
        qi, i = args
        kb = lax.dynamic_slice_in_dim(kp, i * block, pad + block, axis=2)
        vb = lax.dynamic_slice_in_dim(vp, i * block, pad + block, axis=2)
        s = jnp.einsum('bgrqd,bgkd->bgrqk', qi, kb).astype(jnp.float32) * scale
        qpos = i * block + offs_q
        kpos = i * block + offs_k
        diff = qpos[:, None] - kpos[None, :]
        mask = (diff >= 0) & (diff <= span) & (kpos >= 0)[None, :]
        s = jnp.where(mask, s, -jnp.inf)
        m = jnp.max(s, axis=-1, keepdims=True)
        e = jnp.exp(s - m)
        den = jnp.sum(e, axis=-1, keepdims=True)
        o = jnp.einsum('bgrqk,bgkd->bgrqd', (e / den).astype(v.dtype), vb)
        return o, (m + jnp.log(den))[..., 0]

    o, lse = lax.map(one, (qb, jnp.arange(nb)))
    o = jnp.moveaxis(o, 0, 3).reshape(B, G, R, L, hd)
    lse = jnp.moveaxis(lse, 0, 3).reshape(B, G, R, L)
    return o, lse


def causal_diff_attention(q1, q2, k1, k2, v, lam):
    B, H, S, dq = q1.shape
    blk = math.gcd(QUERY_BLOCK, S)
    nb = S // blk
    scale = dq ** -0.5
    kpos = jnp.arange(S)

    def to_blocks(t):
        return jnp.moveaxis(t.reshape(B, H, nb, blk, dq), 2, 0)

    def one(args):
        q1b, q2b, i = args
        mask = kpos[None, :] <= (i * blk + jnp.arange(blk))[:, None]

        def probs(qb, kk):
            s = jnp.einsum('bhqd,bhkd->bhqk', qb, kk).astype(jnp.float32) * scale
            return jax.nn.softmax(jnp.where(mask, s, -jnp.inf), axis=-1)

        a = probs(q1b, k1) - lam * probs(q2b, k2)
        return jnp.einsum('bhqk,bhkd->bhqd', a.astype(v.dtype), v)

    o = lax.map(one, (to_blocks(q1), to_blocks(q2), jnp.arange(nb)))
    return jnp.moveaxis(o, 0, 2).reshape(B, H, S, v.shape[-1])


def diff_attention_mixer(x, w_in, lam, subln_g, w_out, layer_idx):
    B, S, _ = x.shape
    H, dq, dv = A_HEADS, A_QK_DIM, HEAD_DIM
    q, k, v = jnp.split(x @ w_in, 3, axis=-1)
    q = q.reshape(B, S, H, 2, dq).transpose(3, 0, 2, 1, 4)
    k = k.reshape(B, S, H, 2, dq).transpose(3, 0, 2, 1, 4)
    v = v.reshape(B, S, H, dv).transpose(0, 2, 1, 3)
    cos, sin = rope_angles(jnp.arange(S), dq)
    q = partial_rope(q, cos, sin)
    k = partial_rope(k, cos, sin)
    lam_init = 0.8 - 0.6 * math.exp(-0.3 * layer_idx)
    lf = lam.astype(jnp.float32)
    lam_full = jnp.exp(jnp.sum(lf[0] * lf[1])) - jnp.exp(jnp.sum(lf[2] * lf[3])) + lam_init
    o = causal_diff_attention(q[0], q[1], k[0], k[1], v, lam_full)
    o = rms_norm(o, subln_g) * (1.0 - lam_init)
    return o.transpose(0, 2, 1, 3).reshape(B, S, H * dv) @ w_out


def nsa_mixer(x, w_in, cmp_pos, cmp_w1, cmp_w2, w_out):
    B, S, _ = x.shape
    H, G, hd = B_HEADS, B_KV_GROUPS, HEAD_DIM
    R = H // G
    scale = hd ** -0.5
    pos = jnp.arange(S)
    proj = x @ w_in
    q_w, kv_w = H * hd, B_N_KV * G * hd
    q = proj[..., :q_w].reshape(B, S, G, R, hd).transpose(0, 2, 3, 1, 4)
    kvs = proj[..., q_w:q_w + kv_w].reshape(B, S, B_N_KV, G, hd).transpose(2, 0, 3, 1, 4)
    gates = jax.nn.sigmoid(proj[..., q_w + kv_w:].astype(jnp.float32))
    gates = gates.reshape(B, S, G, R, 3).transpose(4, 0, 2, 3, 1)[..., None]
    k_cmp, v_cmp, k_sel, v_sel, k_win, v_win = kvs
    cos, sin = rope_angles(pos, hd)
    q = partial_rope(q, cos, sin)
    k_sel = partial_rope(k_sel, cos, sin)
    k_win = partial_rope(k_win, cos, sin)

    n_chunk = S // B_CMP_STRIDE
    per = B_CMP_LEN // B_CMP_STRIDE
    n_cmp = n_chunk - per + 1

    def compress(t, j):
        c = t.reshape(B, G, n_chunk, B_CMP_STRIDE, hd)
        blocks = jnp.concatenate([c[:, :, m:m + n_cmp] for m in range(per)], axis=3)
        blocks = (blocks + cmp_pos[j]).reshape(B, G, n_cmp, B_CMP_LEN * hd)
        return jax.nn.gelu(blocks @ cmp_w1[j]) @ cmp_w2[j]

    cmp_start = jnp.arange(n_cmp) * B_CMP_STRIDE
    cmp_end = cmp_start + B_CMP_LEN - 1
    cos_c, sin_c = rope_angles(cmp_end, hd)
    kc = partial_rope(compress(k_cmp, 0), cos_c, sin_c)
    vc = compress(v_cmp, 1)
    cmask = cmp_end[None, :] <= pos[:, None]
    s = jnp.einsum('bgrsd,bgcd->bgrsc', q, kc).astype(jnp.float32) * scale
    p_cmp = jnp.where(cmask, jax.nn.softmax(jnp.where(cmask, s, NEG_FILL), axis=-1), 0.0)
    o_cmp = jnp.einsum('bgrsc,bgcd->bgrsd', p_cmp.astype(vc.dtype), vc)

    n_sel = S // B_SEL_LEN
    sel_start = jnp.arange(n_sel) * B_SEL_LEN
    cover = ((cmp_start[:, None] < sel_start[None, :] + B_SEL_LEN)
             & (cmp_end[:, None] >= sel_start[None, :])).astype(jnp.float32)
    importance = jnp.einsum('bgrsc,cn->bgsn', p_cmp, cover)
    cur = pos // B_SEL_LEN
    blk = jnp.arange(n_sel)
    forced = (blk[None, :] == 0) | (blk[None, :] == cur[:, None]) | (blk[None, :] == cur[:, None] - 1)
    score = jnp.where(forced, jnp.inf, importance)
    score = jnp.where(sel_start[None, :] <= pos[:, None], score, -jnp.inf)
    n_top = min(B_SEL_TOPN, n_sel)
    _, sel_idx = lax.top_k(score, n_top)

    chunk = math.gcd(B_SEL_CHUNK, S)
    n_chunks = S // chunk
    kb = k_sel.reshape(B, G, n_sel, B_SEL_LEN, hd)
    vb = v_sel.reshape(B, G, n_sel, B_SEL_LEN, hd)
    qc = jnp.moveaxis(q.reshape(B, G, R, n_chunks, chunk, hd), 3, 0)
    ic = jnp.moveaxis(sel_idx.reshape(B, G, n_chunks, chunk, n_top), 2, 0)
    gather = jax.vmap(jax.vmap(lambda blocks, ix: blocks[ix]))
    offs = jnp.arange(B_SEL_LEN)

    def sel_one(args):
        qi, ii, c = args
        kg = gather(kb, ii)
        vg = gather(vb, ii)
        sc = jnp.einsum('bgrcd,bgcnld->bgrcnl', qi, kg).astype(jnp.float32) * scale
        qpos = c * chunk + jnp.arange(chunk)
        kpos = ii[..., None] * B_SEL_LEN + offs
        mask = kpos <= qpos[:, None, None]
        sc = jnp.where(mask[:, :, None], sc, -jnp.inf).reshape(B, G, R, chunk, n_top * B_SEL_LEN)
        pr = jax.nn.softmax(sc, axis=-1).reshape(B, G, R, chunk, n_top, B_SEL_LEN)
        return jnp.einsum('bgrcnl,bgcnld->bgrcd', pr.astype(vg.dtype), vg)

    o_sel = lax.map(sel_one, (qc, ic, jnp.arange(n_chunks)))
    o_sel = jnp.moveaxis(o_sel, 0, 3).reshape(B, G, R, S, hd)

    o_win, _ = banded_attention(q, k_win, v_win, B_WINDOW - 1, math.gcd(QUERY_BLOCK, S))

    o = (gates[0] * o_cmp + gates[1] * o_sel + gates[2] * o_win).astype(x.dtype)
    return o.transpose(0, 3, 1, 2, 4).reshape(B, S, H * hd) @ w_out


def dilated_mixer(x, w_in, w_out):
    B, S, _ = x.shape
    Hg, hd, P = C_HEADS_PER_GROUP, HEAD_DIM, len(C_PATTERNS)
    proj = (x @ w_in).reshape(B, S, 3, P, Hg, hd).transpose(2, 3, 0, 4, 1, 5)
    cos, sin = rope_angles(jnp.arange(S), hd)
    outs, lses = [], []
    for g, (window, dil) in enumerate(C_PATTERNS):
        L = S // dil

        def to_sub(t):
            return t.reshape(B, Hg, L, dil, hd).transpose(0, 1, 3, 2, 4).reshape(B, Hg * dil, L, hd)

        q = to_sub(partial_rope(proj[0, g], cos, sin))
        k = to_sub(partial_rope(proj[1, g], cos, sin))
        v = to_sub(proj[2, g])
        o, lse = banded_attention(q[:, :, None], k, v, window // dil, math.gcd(QUERY_BLOCK, L))
        outs.append(o[:, :, 0].reshape(B, Hg, dil, L, hd).transpose(0, 1, 3, 2, 4).reshape(B, Hg, S, hd))
        lses.append(lse[:, :, 0].reshape(B, Hg, dil, L).transpose(0, 1, 3, 2).reshape(B, Hg, S))
    w = jax.nn.softmax(jnp.stack(lses), axis=0)
    o = jnp.sum(w[..., None] * jnp.stack(outs).astype(jnp.float32), axis=0).astype(x.dtype)
    return o.transpose(0, 2, 1, 3).reshape(B, S, Hg * hd) @ w_out


def hierarchical_moe(x, router_w, router_b, w_in, w_out):
    B, S, D = x.shape
    T = B * S
    K, E, R = MOE_TOPK, MOE_EXPERTS, MOE_ROW_BLOCK
    xt = x.reshape(T, D)
    logits = (xt @ router_w).astype(jnp.float32) + router_b
    g_prob = jax.nn.softmax(logits[:, :MOE_GROUPS], axis=-1)
    g_w, g_idx = lax.top_k(g_prob, 1)
    e_logits = logits[:, MOE_GROUPS:].reshape(T, MOE_GROUPS, MOE_EXPERTS_PER_GROUP)
    e_in = jnp.take_along_axis(e_logits, g_idx[:, :, None], axis=1)[:, 0]
    e_val, e_idx = lax.top_k(e_in, K)
    gate = jax.nn.softmax(e_val, axis=-1) * g_w
    flat_e = (g_idx * MOE_EXPERTS_PER_GROUP + e_idx).reshape(-1)
    flat_w = gate.reshape(-1)
    flat_tok = jnp.repeat(jnp.arange(T, dtype=jnp.int32), K)
    A = T * K
    n_blk = -(-A // R) + E
    order = jnp.argsort(flat_e)
    se = flat_e[order]
    sizes = jnp.bincount(flat_e, length=E)
    padded = (sizes + R - 1) // R * R
    pad_end = jnp.cumsum(padded)
    pad_start = pad_end - padded
    start = jnp.cumsum(sizes) - sizes
    dest = pad_start[se] + jnp.arange(A, dtype=jnp.int32) - start[se]
    slot_tok = jnp.full((n_blk * R,), T, jnp.int32).at[dest].set(flat_tok[order])
    slot_w = jnp.zeros((n_blk * R,), jnp.float32).at[dest].set(flat_w[order])
    blk_e = jnp.minimum(jnp.searchsorted(pad_end, jnp.arange(n_blk) * R, side='right'), E - 1)
    xpad = jnp.concatenate([xt, jnp.zeros((1, D), xt.dtype)], axis=0)
    xb = xpad[slot_tok].reshape(n_blk, R, D)

    def expert_block(args):
        xr, e = args
        gu = xr @ w_in[e]
        g, u = jnp.split(gu, 2, axis=-1)
        return (jax.nn.silu(g) * u) @ w_out[e]

    yb = lax.map(expert_block, (xb, blk_e)).reshape(n_blk * R, D)
    y = jnp.zeros((T + 1, D), jnp.float32).at[slot_tok].add(yb.astype(jnp.float32) * slot_w[:, None])
    return y[:T].astype(x.dtype).reshape(B, S, D)


def setup_inputs(seed: int = 0) -> dict:
    key = jax.random.key(seed)
    keys = iter(jax.random.split(key, 128))
    D, hd = D_MODEL, HEAD_DIM

    def normal(shape, fan_in, scale=1.0):
        return jax.random.normal(next(keys), shape, jnp.float32) * (scale * fan_in ** -0.5)

    def noise(shape, std):
        return jax.random.normal(next(keys), shape, jnp.float32) * std

    params = {'x': jax.random.normal(next(keys), (BATCH, SEQ, D), jnp.float32)}
    for i in range(DEPTH):
        kind = i % N_MIXERS
        p = 'l%d_' % i
        if kind == 0:
            w = A_HEADS * hd
            params[p + 'a_w_in'] = jnp.concatenate(
                [normal((D, w), D), normal((D, w), D), normal((D, w), D, DN_BETA)], axis=1)
            params[p + 'a_lam'] = noise((4, A_QK_DIM), 0.1)
            params[p + 'a_subln'] = 1.0 + noise((hd,), 0.02)
            params[p + 'a_w_out'] = normal((w, D), w, DN_BETA)
        elif kind == 1:
            kv = B_KV_GROUPS * hd
            cols = [normal((D, B_HEADS * hd), D)]
            cols += [normal((D, kv), D, DN_BETA if j % 2 else 1.0) for j in range(B_N_KV)]
            cols.append(normal((D, B_HEADS * 3), D))
            params[p + 'b_w_in'] = jnp.concatenate(cols, axis=1)
            params[p + 'b_cmp_pos'] = noise((2, B_CMP_LEN, hd), 0.02)
            params[p + 'b_cmp_w1'] = normal((2, B_CMP_LEN * hd, hd), B_CMP_LEN * hd)
            params[p + 'b_cmp_w2'] = normal((2, hd, hd), hd)
            params[p + 'b_w_out'] = normal((B_HEADS * hd, D), B_HEADS * hd, DN_BETA)
        else:
            w = len(C_PATTERNS) * C_HEADS_PER_GROUP * hd
            params[p + 'c_w_in'] = jnp.concatenate(
                [normal((D, w), D), normal((D, w), D), normal((D, w), D, DN_BETA)], axis=1)
            params[p + 'c_w_out'] = normal((C_HEADS_PER_GROUP * hd, D), C_HEADS_PER_GROUP * hd, DN_BETA)
        params[p + 'ln'] = jnp.stack([1.0 + noise((D,), 0.02), noise((D,), 0.02),
                                      1.0 + noise((D,), 0.02), noise((D,), 0.02)])
        n_route = MOE_GROUPS + MOE_EXPERTS
        params[p + 'router_w'] = normal((D, n_route), D)
        params[p + 'router_b'] = noise((n_route,), 0.01)
        params[p + 'moe_w_in'] = jnp.concatenate(
            [normal((MOE_EXPERTS, D, MOE_D_FF), D), normal((MOE_EXPERTS, D, MOE_D_FF), D, DN_BETA)], axis=-1)
        params[p + 'moe_w_out'] = normal((MOE_EXPERTS, MOE_D_FF, D), MOE_D_FF, DN_BETA)
    return params


def reference(x,
              l0_a_w_in, l0_a_lam, l0_a_subln, l0_a_w_out,
              l0_ln, l0_router_w, l0_router_b, l0_moe_w_in, l0_moe_w_out,
              l1_b_w_in, l1_b_cmp_pos, l1_b_cmp_w1, l1_b_cmp_w2, l1_b_w_out,
              l1_ln, l1_router_w, l1_router_b, l1_moe_w_in, l1_moe_w_out,
              l2_c_w_in, l2_c_w_out,
              l2_ln, l2_router_w, l2_router_b, l2_moe_w_in, l2_moe_w_out,
              l3_a_w_in, l3_a_lam, l3_a_subln, l3_a_w_out,
              l3_ln, l3_router_w, l3_router_b, l3_moe_w_in, l3_moe_w_out):
    mixer_args = [(l0_a_w_in, l0_a_lam, l0_a_subln, l0_a_w_out),
                  (l1_b_w_in, l1_b_cmp_pos, l1_b_cmp_w1, l1_b_cmp_w2, l1_b_w_out),
                  (l2_c_w_in, l2_c_w_out),
                  (l3_a_w_in, l3_a_lam, l3_a_subln, l3_a_w_out)]
    ffn_args = [(l0_ln, l0_router_w, l0_router_b, l0_moe_w_in, l0_moe_w_out),
                (l1_ln, l1_router_w, l1_router_b, l1_moe_w_in, l1_moe_w_out),
                (l2_ln, l2_router_w, l2_router_b, l2_moe_w_in, l2_moe_w_out),
                (l3_ln, l3_router_w, l3_router_b, l3_moe_w_in, l3_moe_w_out)]
    h = x
    for i in range(DEPTH):
        kind = i % N_MIXERS
        ln, router_w, router_b, moe_w_in, moe_w_out = ffn_args[i]
        if kind == 0:
            m = diff_attention_mixer(h, *mixer_args[i], layer_idx=i)
        elif kind == 1:
            m = nsa_mixer(h, *mixer_args[i])
        else:
            m = dilated_mixer(h, *mixer_args[i])
        h = layer_norm(DN_ALPHA * h + m, ln[0], ln[1])
        f = hierarchical_moe(h, router_w, router_b, moe_w_in, moe_w_out)
        h = layer_norm(DN_ALPHA * h + f, ln[2], ln[3])
    return h
```

```python
import math
import numpy as np
from contextlib import ExitStack
import concourse.bass as bass
import concourse.mybir as mybir
from concourse.bass_utils import run_bass_kernel_spmd

F32 = mybir.dt.float32
BF16 = mybir.dt.bfloat16
I32 = mybir.dt.int32
AF = mybir.ActivationFunctionType
ALU = mybir.AluOpType
AX = mybir.AxisListType

D = 2048
SEQ = 2048
NT = 16
DEPTH = 4
ALPHA = (2 * DEPTH) ** 0.25
EPS = 1e-5
THETA = 500000.0
NEG = -30000.0


class Res:
    __slots__ = ("w", "r", "nowaw")

    def __init__(self, nowaw=False):
        self.w = None
        self.r = {}
        self.nowaw = nowaw


class Sched:
    SEM_LIMIT = 30000

    def __init__(self, nc, stack, n_dma_sems=12):
        self.nc = nc
        self.stack = stack
        self.engs = {"pe": nc.tensor, "dve": nc.vector, "act": nc.scalar,
                     "pool": nc.gpsimd, "sp": nc.sync}
        self.sems = {}
        self.cur = {}
        self.epoch = {}
        for e in self.engs:
            self.epoch[e] = 0
            self._new_epoch(e)
        self.waited = {e: {} for e in self.engs}
        self.dma_pool = {}
        self.n_dma_sems = n_dma_sems
        self.dma_rr = {}
        self.n_instr = 0

    def _new_epoch(self, e):
        k = "%s_%d" % (e, self.epoch[e])
        self.epoch[e] += 1
        self.sems[k] = self.stack.enter_context(self.nc.semaphore(k))
        self.cur[e] = [k, 0]

    def _wait(self, e, tok):
        if tok is None:
            return
        k, v = tok
        if self.waited[e].get(k, 0) >= v:
            return
        if e == "pe" and k.startswith("pe_"):
            return
        self.engs[e].wait_ge(self.sems[k], v)
        self.waited[e][k] = v

    def _deps(self, e, reads, writes, skip_key=None):
        for t in reads:
            self._wait(e, t.w)
        for t in writes:
            if t.nowaw and t.w is not None and t.w[0].startswith("d"):
                pass
            elif not (skip_key is not None and t.w is not None and t.w[0] == skip_key):
                self._wait(e, t.w)
            for k, v in t.r.items():
                self._wait(e, (k, v))

    def _mark(self, tok, reads, writes):
        for t in reads:
            t.r[tok[0]] = tok[1]
        for t in writes:
            t.w = tok
            t.r = {}

    def op(self, e, ins_fn, reads=(), writes=()):
        self._deps(e, reads, writes)
        ins = ins_fn(self.engs[e])
        c = self.cur[e]
        c[1] += 1
        ins.then_inc(self.sems[c[0]], 1)
        tok = (c[0], c[1])
        self._mark(tok, reads, writes)
        if c[1] >= self.SEM_LIMIT:
            self._new_epoch(e)
        self.n_instr += 1
        return tok

    def dma(self, q, out, in_, reads=(), writes=(), indirect=None, grp=None, **kw):
        pool = self.dma_pool.setdefault(q, [])
        if grp is not None and grp.get("ent") is not None:
            ent = grp["ent"]
        elif len(pool) < (self.n_dma_sems if q != "pool" else 20):
            k = "d%s_%d" % (q, len(pool))
            self.sems[k] = self.stack.enter_context(self.nc.semaphore(k))
            pool.append([k, 0])
            ent = pool[-1]
        else:
            i = self.dma_rr.get(q, 0)
            ent = pool[i]
            self.dma_rr[q] = (i + 1) % len(pool)
            self._wait(q, (ent[0], ent[1]))
        skip_key = None
        if grp is not None:
            if grp.get("ent") is not None:
                skip_key = ent[0]
            grp["ent"] = ent
        self._deps(q, reads, writes, skip_key)
        if indirect is not None:
            ins = self.engs[q].indirect_dma_start(out, indirect.get("out_offset"), in_,
                                                  indirect.get("in_offset"), **kw)
        else:
            ins = self.engs[q].dma_start(out=out, in_=in_, **kw)
        ent[1] += 16
        ins.then_inc(self.sems[ent[0]], 16)
        tok = (ent[0], ent[1])
        self._mark(tok, reads, writes)
        self.n_instr += 1
        return tok

    def barrier(self):
        toks = []
        for e, c in self.cur.items():
            if c[1] > 0:
                toks.append((c[0], c[1]))
        for q, pool in self.dma_pool.items():
            for ent in pool:
                if ent[1] > 0:
                    toks.append((ent[0], ent[1]))
        for e in self.engs:
            for tok in toks:
                self._wait(e, tok)

    def mm(self, out, lhsT, rhs, start, stop, reads, writes):
        return self.op("pe", lambda e: e.matmul(out, lhsT, rhs, start=start, stop=stop), reads, writes)

    def tr(self, out, in_, ident, reads, writes):
        return self.op("pe", lambda e: e.transpose(out, in_, ident), reads, writes)

    def act(self, out, in_, func, reads, writes, **kw):
        return self.op("act", lambda e: e.activation(out, in_, func, **kw), reads, writes)

    def tt(self, eng, out, in0, in1, op, reads, writes):
        return self.op(eng, lambda e: e.tensor_tensor(out, in0, in1, op), reads, writes)

    def ts(self, eng, out, in0, s1, s2, op0, op1, reads, writes, **kw):
        if s2 is None:
            return self.op(eng, lambda e: e.tensor_scalar(out, in0, s1, None, op0, **kw), reads, writes)
        return self.op(eng, lambda e: e.tensor_scalar(out, in0, s1, s2, op0, op1, **kw), reads, writes)

    def stt(self, out, in0, scalar, in1, op0, op1, reads, writes):
        return self.op("dve", lambda e: e.scalar_tensor_tensor(out, in0, scalar, in1, op0, op1), reads, writes)

    def copy(self, eng, out, in_, reads, writes):
        if eng == "act":
            return self.op("act", lambda e: e.copy(out, in_), reads, writes)
        return self.op(eng, lambda e: e.tensor_copy(out, in_), reads, writes)


class Ring:
    def __init__(self, items):
        self.items = items
        self.i = 0

    def next(self):
        it = self.items[self.i]
        self.i = (self.i + 1) % len(self.items)
        return it


MASK_IDS = {}


def _build_masks():
    k = np.arange(128)[:, None]
    q = np.arange(128)[None, :]
    masks = []

    def add(name, m):
        MASK_IDS[name] = len(masks)
        masks.append(m.astype(np.float32))

    add("causal", k <= q)
    add("win4", k > q)
    add("d1_0", k <= q)
    add("d1_1", (128 + q - k) <= 128)
    for t in range(5):
        dd = 128 * t + q - k
        add("d4_%d" % t, (dd >= 0) & (dd % 4 == 0) & (dd <= 512))
    for t in range(16):
        dd = 128 * t + q - k
        add("d16_%d" % t, (dd >= 0) & (dd % 16 == 0) & (dd <= 2048))
    for i in range(16):
        add("cmp_%d" % i, (16 * k + 31 <= 128 * i + q) & (k < 127))
    return np.stack(masks)


def _rope_tables(kind):
    pos = np.arange(SEQ, dtype=np.float32)
    C = np.ones((128, SEQ), np.float32)
    Sn = np.zeros((128, SEQ), np.float32)
    P = np.zeros((128, 128), np.float32)
    if kind == "A":
        dim, bases = 64, (0, 64)
    else:
        dim, bases = 128, (0,)
    rot = dim // 4
    half = rot // 2
    inv = (THETA ** (-np.arange(0, rot, 2, dtype=np.float32) / rot)).astype(np.float32)
    ang = pos[None, :] * inv[:, None]
    for b in bases:
        for i in range(half):
            C[b + i] = np.cos(ang[i]); C[b + half + i] = np.cos(ang[i])
            Sn[b + i] = np.sin(ang[i]); Sn[b + half + i] = np.sin(ang[i])
            P[b + half + i, b + i] = -1.0
            P[b + i, b + half + i] = 1.0
    return C, Sn, P


def _host_consts():
    c = {}
    c["c_ident"] = np.eye(128, dtype=np.float32)
    c["c_masks"] = np.ascontiguousarray(_build_masks().transpose(1, 0, 2))
    CA, SA, PA = _rope_tables("A")
    CB, SB, PB = _rope_tables("B")
    c["c_ropeA"] = np.stack([CA, SA], axis=1)
    c["c_ropeB"] = np.stack([CB, SB], axis=1)
    c["c_perm"] = np.stack([PA, PB], axis=1)
    cend = np.arange(127) * 16 + 31
    kc = np.zeros((128, 2, 128), np.float32)
    kc[:, 0, :] = 1.0
    kc[:, 0, :127] = CB[:, cend]
    kc[:, 1, :127] = SB[:, cend]
    c["c_ropeKC"] = kc
    cs = np.arange(127) * 16
    ss = np.arange(32) * 64
    cover = ((cs[:, None] < ss[None, :] + 64) & (cs[:, None] + 31 >= ss[None, :])).astype(np.float32)
    cv = np.zeros((128, 32), np.float32)
    cv[:127] = cover
    c["c_cover"] = cv
    kk = np.arange(SEQ)
    c["c_expand"] = (kk[None, :] // 64 == np.arange(32)[:, None]).astype(np.float32)
    posq = np.arange(SEQ)
    cur = posq // 64
    blk = np.arange(32)
    forced = (blk[None, :] == 0) | (blk[None, :] == cur[:, None]) | (blk[None, :] == cur[:, None] - 1)
    valid = ss[None, :] <= posq[:, None]
    sa = np.where(valid, np.where(forced, 1e9, 0.0), -1e9).astype(np.float32)
    c["c_seladd"] = np.ascontiguousarray(sa.reshape(16, 128, 32).transpose(1, 0, 2))
    tri = (np.arange(128)[:, None] < np.arange(128)[None, :]).astype(np.float32)
    c["c_tri"] = tri
    c["c_iota"] = np.stack([np.arange(128, dtype=np.float32)] +
                           [np.full(128, 0.0, np.float32)], axis=1)
    return c


CONST_SHAPES = None


KINDS = ["A", "B", "C", "A"]
LAM_INIT = [0.8 - 0.6 * math.exp(-0.3 * i) for i in range(DEPTH)]


def weight_names(i):
    p = "l%d_" % i
    k = KINDS[i]
    if k == "A":
        m = {p + "a_w_in": [2048, 6144], p + "a_lam": [4, 64], p + "a_subln": [128], p + "a_w_out": [2048, 2048]}
    elif k == "B":
        m = {p + "b_w_in": [2048, 5168], p + "b_cmp_pos": [2, 32, 128], p + "b_cmp_w1": [2, 4096, 128],
             p + "b_cmp_w2": [2, 128, 128], p + "b_w_out": [2048, 2048]}
    else:
        m = {p + "c_w_in": [2048, 9216], p + "c_w_out": [1024, 2048]}
    m.update({p + "ln": [4, 2048], p + "router_w": [2048, 36], p + "router_b": [36],
              p + "moe_w_in": [32, 2048, 1024], p + "moe_w_out": [32, 512, 2048]})
    return m


class AttnPipe:
    def __init__(self, prog, st, n_s=2, n_p=5, depth=3, extra_banks=()):
        self.prog = prog
        self.S = prog.S
        self.sring = Ring([(prog.ps(st, "Sps%d" % i, [128, 512], F32), Res()) for i in range(n_s)] + list(extra_banks))
        self.pring = Ring([(prog.sb(st, "Pt%d" % i, [128, 512], BF16), Res()) for i in range(n_p)])
        self.q = []
        self.depth = depth

    def push(self, items, scale, after=None):
        S = self.S
        sp, rsp = self.sring.next()
        for j, it in enumerate(items):
            ex = it.get("extra")
            S.mm(sp[:, j * 128:(j + 1) * 128], it["k"], it["q"], start=True, stop=(ex is None),
                 reads=list(it["kr"]) + list(it["qr"]), writes=[rsp])
            if ex is not None:
                S.mm(sp[:, j * 128:(j + 1) * 128], ex[0], ex[1], start=False, stop=True, reads=list(ex[2]), writes=[rsp])
        P, rP = self.pring.next()
        w = len(items) * 128
        S.act(P[:, 0:w], sp[:, 0:w], AF.Exp, reads=[rsp], writes=[rP], scale=scale)
        for j, it in enumerate(items):
            if it.get("mask") is not None:
                S.tt("dve", P[:, j * 128:(j + 1) * 128], P[:, j * 128:(j + 1) * 128],
                     self.prog.masks[:, it["mask"], :], ALU.mult, reads=[rP, self.prog.rC], writes=[rP])
        self.q.append((items, P, rP, after))
        while len(self.q) > self.depth:
            self._pv()

    def _pv(self):
        S = self.S
        items, P, rP, after = self.q.pop(0)
        for j, it in enumerate(items):
            S.mm(it["O"], P[:, j * 128:(j + 1) * 128], it["v"], start=it["start"], stop=it["stop"],
                 reads=[rP] + list(it["vr"]), writes=[it["rO"]])
        if after is not None:
            after()

    def run(self, items, scale, after=None):
        n = len(items)
        for i in range(0, n, 4):
            self.push(items[i:i + 4], scale, after if i + 4 >= n else None)

    def flush(self):
        while self.q:
            self._pv()


class Prog:
    def __init__(self, n_seq, layers, dbg=False):
        self.n_seq = n_seq
        self.layers = layers
        self.NTT = n_seq * NT
        self.NTOK = n_seq * SEQ
        self.RB = 256
        self.NBLK = self.NTOK * 2 // self.RB + 32
        self.dbg = dbg
        nc = bass.Bass("TRN2", target_bir_lowering=False)
        self.nc = nc
        self.din = {}
        self.din["x"] = nc.dram_tensor("x", [self.NTOK, D], F32, kind="ExternalInput").ap()
        for i in layers:
            for nm, shp in weight_names(i).items():
                self.din[nm] = nc.dram_tensor(nm, shp, F32, kind="ExternalInput").ap()
        self.consts = _host_consts()
        self.consts["c_blk"] = np.ascontiguousarray(
            np.broadcast_to(float(self.RB) * np.arange(self.NBLK, dtype=np.float32)[None, :], (128, self.NBLK)))
        for nm, arr in self.consts.items():
            self.din[nm] = nc.dram_tensor(nm, list(arr.shape), F32, kind="ExternalInput").ap()
        self.out = nc.dram_tensor("y", [self.NTOK, D], F32, kind="ExternalOutput").ap()
        sk = "ExternalOutput" if dbg else "Internal"
        self.H = nc.dram_tensor("sH", [self.NTOK, D], F32, kind=sk).ap()
        self.H1 = nc.dram_tensor("sH1", [self.NTOK, D], F32, kind=sk).ap()
        self.HT = nc.dram_tensor("sHT", [n_seq, 128, 16, SEQ], BF16, kind=sk).ap()
        self.OT = nc.dram_tensor("sOT", [n_seq, 128, 16, SEQ], BF16, kind=sk).ap()
        self.XB = nc.dram_tensor("sXB", [self.NTOK, D], BF16, kind=sk).ap()
        self.XS = nc.dram_tensor("sXS", [self.NBLK * self.RB, D], BF16, kind=sk).ap()
        self.YS = nc.dram_tensor("sYS", [self.NBLK * self.RB, D], F32, kind=sk).ap()
        self.rH = [Res() for _ in range(self.NTT)]
        self.rH1 = [Res() for _ in range(self.NTT)]
        self.rHT = [Res(True) for _ in range(n_seq)]
        self.rOT = [Res(True) for _ in range(n_seq)]
        self.rXB = [Res() for _ in range(self.NTT)]
        self.rXS = Res(True)
        self.rYS = Res(True)
        self.rOUT = Res(True)

    def sb(self, st, name, shape, dt):
        self.uid = getattr(self, "uid", 0) + 1
        return st.enter_context(self.nc.sbuf_tensor("%s_u%d" % (name, self.uid), shape, dt))

    def ps(self, st, name, shape, dt):
        self.uid = getattr(self, "uid", 0) + 1
        return st.enter_context(self.nc.psum_tensor("%s_u%d" % (name, self.uid), shape, dt))

    def build(self):
        nc = self.nc
        with ExitStack() as st:
            S = Sched(nc, st)
            self.S = S
            self.ident_bf = self.sb(st, "ident_bf", [128, 128], BF16)
            self.ident_f = self.sb(st, "ident_f", [128, 128], F32)
            NM = self.consts["c_masks"].shape[1]
            self.masks = self.sb(st, "masks", [128, NM, 128], BF16)
            self.tri = self.sb(st, "tri", [128, 128], BF16)
            self.ones_bf = self.sb(st, "ones_bf", [128, 128], BF16)
            self.iota = self.sb(st, "iota", [128, 2], F32)
            self.OH = self.sb(st, "OH", [128, self.NTT, 2, 32], BF16)
            self.GW = self.sb(st, "GW", [128, self.NTT, 2], F32)
            self.DST = self.sb(st, "DST", [128, self.NTT, 2], I32)
            self.IDX = self.sb(st, "IDX", [128, 2, self.NBLK], I32)
            self.rIDX = Res()
            self.rC = Res()
            self.rOH = [Res() for _ in range(self.NTT)]
            self.rDST = [Res() for _ in range(self.NTT)]
            S.dma("pool", self.ident_bf[:], self.din["c_ident"], writes=[self.rC])
            S.dma("sp", self.ident_f[:], self.din["c_ident"], writes=[self.rC])
            S.dma("pool", self.masks[:], self.din["c_masks"], writes=[self.rC])
            S.dma("pool", self.tri[:], self.din["c_tri"], writes=[self.rC])
            S.dma("sp", self.iota[:], self.din["c_iota"], writes=[self.rC])
            S.op("dve", lambda e: e.memset(self.ones_bf[:], 1.0), writes=[self.rC])
            self.nhalf = self.sb(st, "nhalf", [128, 1], F32)
            S.op("dve", lambda e: e.memset(self.nhalf[:], -0.5), writes=[self.rC])
            self.eps_t = self.sb(st, "eps_t", [128, 1], F32)
            S.op("dve", lambda e: e.memset(self.eps_t[:], EPS), writes=[self.rC])
            S.barrier()

            first = True
            for li, L in enumerate(self.layers):
                kind = KINDS[L]
                last = (li == len(self.layers) - 1)
                for s in range(self.n_seq):
                    if first:
                        self.phase_x_to_ht(s)
                    if kind == "A":
                        self.mixer_A(L, s)
                    elif kind == "B":
                        self.mixer_B(L, s)
                    else:
                        self.mixer_C(L, s)
                    self.phase_outproj_ln1(L, s, src_is_x=first)
                first = False
                self.phase_dispatch(L)
                self.phase_blocks(L)
                self.phase_combine(L, last)
            S.barrier()
        return nc

    def emit_hT(self, st_res, xb, rxb, s, t):
        S = self.S
        ptA, rptA, ptB, rptB, ht_ring = st_res
        ht, rht = ht_ring.next()
        for half, (pt, rpt) in enumerate(((ptA, rptA), (ptB, rptB))):
            for j in range(8):
                c = half * 8 + j
                S.tr(pt[:, j * 128:(j + 1) * 128], xb[:, c * 128:(c + 1) * 128], self.ident_bf[:],
                     reads=[rxb, self.rC], writes=[rpt])
            S.copy("dve" if half == 0 else "act", ht[:, half * 8:(half + 1) * 8, :],
                   pt[:].rearrange("p (j n) -> p j n", n=128), reads=[rpt], writes=[rht])
        S.dma("sp", self.HT[s, :, :, t * 128:(t + 1) * 128], ht[:], reads=[rht], writes=[self.rHT[s]])

    def alloc_hT_emit(self, st):
        ptA = self.ps(st, "ptA", [128, 1024], BF16)
        ptB = self.ps(st, "ptB", [128, 1024], BF16)
        hts = [(self.sb(st, "htile%d" % i, [128, 16, 128], BF16), Res()) for i in range(2)]
        return (ptA, Res(), ptB, Res(), Ring(hts))

    def phase_x_to_ht(self, s):
        S = self.S
        S.barrier()
        with ExitStack() as st:
            em = self.alloc_hT_emit(st)
            xs = Ring([(self.sb(st, "p0xs%d" % i, [128, D], F32), Res()) for i in range(2)])
            xb = Ring([(self.sb(st, "p0xb%d" % i, [128, D], BF16), Res()) for i in range(2)])
            for t in range(NT):
                g = s * NT + t
                x_, rx = xs.next()
                b_, rb = xb.next()
                S.dma("sp", x_[:], self.din["x"][g * 128:(g + 1) * 128, :], writes=[rx])
                S.copy("act", b_[:], x_[:], reads=[rx], writes=[rb])
                self.emit_hT(em, b_, rb, s, t)
            S.barrier()

    def layer_norm_tile(self, a, ra, gb, rgb, gi, outt, rout, stats, rstats, tmp, rtmp):
        S = self.S
        st6, mv, rstd = stats
        for j in range(4):
            S.op("dve", lambda e, j=j: e.bn_stats(st6[:, j, :], a[:, j * 512:(j + 1) * 512]),
                 reads=[ra], writes=[rstats])
        S.op("dve", lambda e: e.bn_aggr(mv[:], st6[:].rearrange("p a b -> p (a b)")), reads=[rstats], writes=[rstats])
        S.act(rstd[:], mv[:, 1:2], AF.Sqrt, reads=[rstats], writes=[rstats], bias=self.eps_t[:, 0:1], scale=1.0)
        S.op("dve", lambda e: e.reciprocal(rstd[:], rstd[:]), reads=[rstats], writes=[rstats])
        S.ts("dve", tmp[:], a[:], mv[:, 0:1], rstd[:, 0:1], ALU.subtract, ALU.mult, reads=[ra, rstats], writes=[rtmp])
        S.tt("pool", tmp[:], tmp[:], gb[:, gi, :], ALU.mult, reads=[rtmp, rgb], writes=[rtmp])
        S.tt("pool", outt[:], tmp[:], gb[:, gi + 1, :], ALU.add, reads=[rtmp, rgb], writes=[rout])

    def alloc_ln(self, st, L, tag, row0):
        S = self.S
        gb = self.sb(st, "lngb" + tag, [128, 2, D], F32)
        rgb = Res()
        ln = self.din["l%d_ln" % L]
        src = ln[row0:row0 + 2, :].partition_broadcast(128)
        S.dma("sp", gb[:], src, writes=[rgb])
        stats = (self.sb(st, "lnst6" + tag, [128, 4, 6], F32), self.sb(st, "lnmv" + tag, [128, 2], F32),
                 self.sb(st, "lnrstd" + tag, [128, 1], F32))
        return gb, rgb, stats, Res()

    def alloc_attn(self, st):
        sb = [(self.ps(st, "Sps%d" % i, [128, 512], F32), Res()) for i in range(2)]
        pt = [(self.sb(st, "Pt%d" % i, [128, 512], BF16), Res()) for i in range(3)]
        return Ring(sb), Ring(pt)

    def attn(self, actx, q_ap, q_reads, items, O, rO, ncols, scale):
        S = self.S
        sring, pring = actx
        groups = [items[i:i + 4] for i in range(0, len(items), 4)]
        n_items = len(items)
        done = [0]

        def pv(grp, P, rP):
            for j, it in enumerate(grp):
                S.mm(O[:, 0:ncols], P[:, j * 128:(j + 1) * 128], it[2], start=(done[0] == 0),
                     stop=(done[0] == n_items - 1), reads=[rP] + list(it[3]), writes=[rO])
                done[0] += 1

        prev = None
        for grp in groups:
            sp, rsp = sring.next()
            for j, it in enumerate(grp):
                ex = it[5]
                S.mm(sp[:, j * 128:(j + 1) * 128], it[0], q_ap, start=True, stop=(ex is None),
                     reads=list(it[1]) + list(q_reads), writes=[rsp])
                if ex is not None:
                    S.mm(sp[:, j * 128:(j + 1) * 128], ex[0], ex[1], start=False, stop=True,
                         reads=list(ex[2]), writes=[rsp])
            P, rP = pring.next()
            w = len(grp) * 128
            S.act(P[:, 0:w], sp[:, 0:w], AF.Exp, reads=[rsp], writes=[rP], scale=scale)
            for j, it in enumerate(grp):
                if it[4] is not None:
                    S.tt("dve", P[:, j * 128:(j + 1) * 128], P[:, j * 128:(j + 1) * 128],
                         self.masks[:, it[4], :], ALU.mult, reads=[rP, self.rC], writes=[rP])
            if prev is not None:
                pv(*prev)
            prev = (grp, P, rP)
        pv(*prev)

    def load_w(self, slot, rslot, w_ap, col0, ncols, kc=16, dst_col=0):
        src = w_ap.rearrange("(c p) n -> p c n", p=128)[:, :, col0:col0 + ncols]
        self.S.dma("pool", slot[:, 0:kc, dst_col:dst_col + ncols], src, writes=[rslot])

    def proj_fm(self, pp, rpp, slot, rslot, c0, hT, rhT, tok0, ntok=512, kc=16):
        for c in range(kc):
            self.S.mm(pp[:, 0:ntok], slot[:, c, c0:c0 + 128], hT[:, c, tok0:tok0 + ntok],
                      start=(c == 0), stop=(c == kc - 1), reads=[rslot, rhT], writes=[rpp])

    def proj_tm(self, pp, rpp, slot, rslot, c0, ncols, hT, rhT, t, kc=16):
        for c in range(kc):
            self.S.mm(pp[:, 0:ncols], hT[:, c, t * 128:(t + 1) * 128], slot[:, c, c0:c0 + ncols],
                      start=(c == 0), stop=(c == kc - 1), reads=[rslot, rhT], writes=[rpp])

    def defer(self, fn, key="rope"):
        d = self.__dict__.setdefault("_deferred", {})
        prev = d.get(key)
        d[key] = fn
        if prev is not None:
            prev()

    def run_deferred(self, key="rope"):
        d = self.__dict__.setdefault("_deferred", {})
        prev = d.get(key)
        d[key] = None
        if prev is not None:
            prev()

    def rope_fm2(self, dst, rdst, pring, perm_ap, rope, rrope, tok0, tring, ntok=512):
        def go():
            pr, rpr = pring.next()
            tmp, rtmp = tring.next()
            self.rope_fm(dst, rdst, pr, rpr, perm_ap, rope, rrope, tok0, tmp, rtmp, ntok)
        self.defer(go)

    def rope_fm(self, dst, rdst, pr, rpr, perm_ap, rope, rrope, tok0, tmp, rtmp, ntok=512):
        S = self.S
        S.mm(pr[:, 0:ntok], perm_ap, dst, start=True, stop=True, reads=[rdst, rrope], writes=[rpr])
        S.tt("pool", tmp[:, 0:ntok], dst, rope[:, 0, tok0:tok0 + ntok], ALU.mult, reads=[rdst, rrope], writes=[rtmp])
        S.tt("dve", dst, pr[:, 0:ntok], rope[:, 1, tok0:tok0 + ntok], ALU.mult, reads=[rpr, rrope], writes=[rdst])
        S.tt("dve", dst, dst, tmp[:, 0:ntok], ALU.add, reads=[rtmp], writes=[rdst])

    def load_hT(self, st, s):
        hT = self.sb(st, "hT", [128, 16, SEQ], BF16)
        rhT = Res()
        g_ = {}
        for c in range(0, 16, 4):
            self.S.dma("sp", hT[:, c:c + 4, :], self.HT[s, :, c:c + 4, :], reads=[self.rHT[s]], writes=[rhT], grp=g_)
        return hT, rhT

    def load_rope(self, st, kind):
        S = self.S
        rope = self.sb(st, "rope", [128, 2, SEQ], F32)
        perm = self.sb(st, "perm", [128, 128], BF16)
        rr = Res()
        S.dma("sp", rope[:], self.din["c_rope" + ("A" if kind == "A" else "B")], writes=[rr])
        S.dma("pool", perm[:], self.din["c_perm"][:, 0 if kind == "A" else 1, :], writes=[rr])
        return rope, perm, rr

    def mixer_A_old(self, L, s):
        S = self.S
        S.barrier()
        p = "l%d_" % L
        w_in = self.din[p + "a_w_in"]
        with ExitStack() as st:
            hT, rhT = self.load_hT(st, s)
            rope, perm, rr = self.load_rope(st, "A")
            actx = self.alloc_attn(st)
            slots = Ring([(self.sb(st, "wslot%d" % i, [128, 16, 256], BF16), Res()) for i in range(5)])
            pproj = Ring([(self.ps(st, "pproj%d" % i, [128, 512], F32), Res()) for i in range(2)])
            pr, rpr = self.ps(st, "prope", [128, 512], F32), Res()
            O1, rO1 = self.ps(st, "O1", [128, 512], F32), Res()
            O2, rO2 = self.ps(st, "O2", [128, 512], F32), Res()
            ptr, rptr = self.ps(st, "ptrA", [128, 1024], BF16), Res()
            qT = self.sb(st, "qT", [128, 2, SEQ], BF16)
            kT = self.sb(st, "kT", [128, 2, SEQ], BF16)
            rqk = [[Res() for _ in range(2)] for _ in range(2)]
            V = self.sb(st, "Vaug", [128, NT, 2, 132], BF16)
            rV = Res()
            tmp, rtmp = self.sb(st, "ropetmp", [128, 512], F32), Res()
            oT = self.sb(st, "oTg", [128, 2, SEQ], BF16)
            roT = Res()
            lam = self.sb(st, "lam", [128, 256], F32)
            lsc = self.sb(st, "lsc", [128, 8], F32)
            gv = self.sb(st, "gv", [128, 128], F32)
            rl = Res()
            S.dma("sp", lam[:], self.din[p + "a_lam"].rearrange("a b -> (a b)").partition_broadcast(128), writes=[rl])
            S.dma("sp", gv[:], self.din[p + "a_subln"].partition_broadcast(128), writes=[rl])
            S.tt("dve", lam[:, 0:64], lam[:, 0:64], lam[:, 64:128], ALU.mult, reads=[rl], writes=[rl])
            S.tt("dve", lam[:, 128:192], lam[:, 128:192], lam[:, 192:256], ALU.mult, reads=[rl], writes=[rl])
            S.op("dve", lambda e: e.reduce_sum(lsc[:, 0:1], lam[:, 0:64], axis=AX.X), reads=[rl], writes=[rl])
            S.op("dve", lambda e: e.reduce_sum(lsc[:, 1:2], lam[:, 128:192], axis=AX.X), reads=[rl], writes=[rl])
            S.act(lsc[:, 2:4], lsc[:, 0:2], AF.Exp, reads=[rl], writes=[rl])
            S.tt("dve", lsc[:, 4:5], lsc[:, 3:4], lsc[:, 2:3], ALU.subtract, reads=[rl], writes=[rl])
            S.ts("dve", lsc[:, 4:5], lsc[:, 4:5], -LAM_INIT[L], None, ALU.add, None, reads=[rl], writes=[rl])
            S.ts("dve", gv[:], gv[:], 1.0 - LAM_INIT[L], None, ALU.mult, None, reads=[rl], writes=[rl])
            S.op("dve", lambda e: e.memset(V[:, :, :, 128:129], 1.0), writes=[rV])
            osb = Ring([(self.sb(st, "osb%d" % i, [128, 128], F32), Res()) for i in range(2)])
            obf = Ring([(self.sb(st, "obf%d" % i, [128, 128], BF16), Res()) for i in range(2)])
            sm = Ring([(self.sb(st, "smA%d" % i, [128, 8], F32), Res()) for i in range(2)])
            junk = self.sb(st, "junkA", [128, 128], F32)
            for hg in range(8):
                wq, rwq = slots.next(); wk, rwk = slots.next(); wv, rwv = slots.next()
                self.load_w(wq, rwq, w_in, hg * 256, 256)
                self.load_w(wk, rwk, w_in, 2048 + hg * 256, 256)
                self.load_w(wv, rwv, w_in, 4096 + hg * 256, 256)
                for hh in range(2):
                    for (dst, w_, rw_, which) in ((qT, wq, rwq, 0), (kT, wk, rwk, 1)):
                        for tc4 in range(4):
                            pp, rpp = pproj.next()
                            self.proj_fm(pp, rpp, w_, rw_, hh * 128, hT, rhT, tc4 * 512)
                            d_ = dst[:, hh, tc4 * 512:(tc4 + 1) * 512]
                            S.copy("act", d_, pp[:], reads=[rpp], writes=[rqk[which][hh]])
                            self.rope_fm(d_, rqk[which][hh], pr, rpr, perm[:], rope, rr, tc4 * 512, tmp, rtmp)
                for t in range(NT):
                    pp, rpp = pproj.next()
                    self.proj_tm(pp, rpp, wv, rwv, 0, 256, hT, rhT, t)
                    S.copy("act", V[:, t, :, 0:128], pp[:, 0:256].rearrange("p (h n) -> p h n", n=128),
                           reads=[rpp], writes=[rV])
                for hh in range(2):
                    for i in range(NT):
                        for m, (O, rO) in enumerate(((O1, rO1), (O2, rO2))):
                            items = []
                            for j in range(i + 1):
                                items.append((kT[m * 64:(m + 1) * 64, hh, j * 128:(j + 1) * 128], [rqk[1][hh]],
                                              V[:, j, hh, 0:129], [rV],
                                              MASK_IDS["causal"] if j == i else None, None))
                            self.attn(actx, qT[m * 64:(m + 1) * 64, hh, i * 128:(i + 1) * 128], [rqk[0][hh]],
                                      items, O, rO, 129, 0.125)
                        sm_, rsm = sm.next()
                        o_, ro = osb.next()
                        ob, rob = obf.next()
                        S.op("dve", lambda e: e.reciprocal(sm_[:, 0:1], O1[:, 128:129]), reads=[rO1], writes=[rsm])
                        S.op("dve", lambda e: e.reciprocal(sm_[:, 1:2], O2[:, 128:129]), reads=[rO2], writes=[rsm])
                        S.tt("dve", sm_[:, 1:2], sm_[:, 1:2], lsc[:, 4:5], ALU.mult, reads=[rsm, rl], writes=[rsm])
                        S.ts("dve", o_[:], O1[:, 0:128], sm_[:, 0:1], None, ALU.mult, None, reads=[rO1, rsm], writes=[ro])
                        S.stt(o_[:], O2[:, 0:128], sm_[:, 1:2], o_[:], ALU.mult, ALU.add, reads=[rO2, rsm, ro], writes=[ro])
                        S.op("dve", lambda e: e.memset(sm_[:, 2:3], 0.0), writes=[rsm])
                        S.act(junk[:], o_[:], AF.Square, reads=[ro], writes=[rsm], accum_out=sm_[:, 2:3])
                        S.act(sm_[:, 3:4], sm_[:, 2:3], AF.Sqrt, reads=[rsm], writes=[rsm], bias=self.eps_t[:, 0:1],
                              scale=1.0 / 128.0)
                        S.op("dve", lambda e: e.reciprocal(sm_[:, 3:4], sm_[:, 3:4]), reads=[rsm], writes=[rsm])
                        S.stt(ob[:], o_[:], sm_[:, 3:4], gv[:], ALU.mult, ALU.mult, reads=[ro, rsm, rl], writes=[rob])
                        S.tr(ptr[:, 0:128], ob[:], self.ident_bf[:], reads=[rob, self.rC], writes=[rptr])
                        S.copy("dve", oT[:, hh, i * 128:(i + 1) * 128], ptr[:, 0:128], reads=[rptr], writes=[roT])
                S.dma("sp", self.OT[s, :, hg * 2:hg * 2 + 2, :], oT[:], reads=[roT], writes=[self.rOT[s]])
            S.barrier()

    def mixer_C_old(self, L, s):
        S = self.S
        S.barrier()
        p = "l%d_" % L
        w_in = self.din[p + "c_w_in"]
        with ExitStack() as st:
            hT, rhT = self.load_hT(st, s)
            rope, perm, rr = self.load_rope(st, "B")
            actx = self.alloc_attn(st)
            slots = Ring([(self.sb(st, "wslotC%d" % i, [128, 16, 128], BF16), Res()) for i in range(12)])
            pproj = Ring([(self.ps(st, "pproj%d" % i, [128, 512], F32), Res()) for i in range(2)])
            pr, rpr = self.ps(st, "prope", [128, 512], F32), Res()
            O, rO = self.ps(st, "OC_", [128, 512], F32), Res()
            ptr, rptr = self.ps(st, "ptrC", [128, 1024], BF16), Res()
            qT = self.sb(st, "qTC", [128, 3, SEQ], BF16)
            kT = self.sb(st, "kTC", [128, 3, SEQ], BF16)
            rq = [Res() for _ in range(3)]
            rk = [Res() for _ in range(3)]
            V = self.sb(st, "VaugC", [128, 3, NT, 132], BF16)
            rV = [Res() for _ in range(3)]
            tmp, rtmp = self.sb(st, "ropetmp", [128, 512], F32), Res()
            oT, roT = self.sb(st, "oTC", [128, SEQ], BF16), Res()
            S.op("dve", lambda e: e.memset(V[:, :, :, 128:129], 1.0), writes=rV)
            ob = Ring([(self.sb(st, "obC%d" % i, [128, 128], BF16), Res()) for i in range(2)])
            sm = Ring([(self.sb(st, "smC%d" % i, [128, 2], F32), Res()) for i in range(2)])
            for h in range(8):
                for g in range(3):
                    for which, (dst, rd) in enumerate(((qT, rq), (kT, rk))):
                        w_, rw_ = slots.next()
                        self.load_w(w_, rw_, w_in, which * 3072 + g * 1024 + h * 128, 128)
                        for tc4 in range(4):
                            pp, rpp = pproj.next()
                            self.proj_fm(pp, rpp, w_, rw_, 0, hT, rhT, tc4 * 512)
                            d_ = dst[:, g, tc4 * 512:(tc4 + 1) * 512]
                            S.copy("act", d_, pp[:], reads=[rpp], writes=[rd[g]])
                            self.rope_fm(d_, rd[g], pr, rpr, perm[:], rope, rr, tc4 * 512, tmp, rtmp)
                    w_, rw_ = slots.next()
                    self.load_w(w_, rw_, w_in, 2 * 3072 + g * 1024 + h * 128, 128)
                    for t4 in range(0, NT, 4):
                        pp, rpp = pproj.next()
                        for tt_ in range(4):
                            t = t4 + tt_
                            for c in range(16):
                                S.mm(pp[:, tt_ * 128:(tt_ + 1) * 128], hT[:, c, t * 128:(t + 1) * 128], w_[:, c, 0:128],
                                     start=(c == 0), stop=(c == 15), reads=[rw_, rhT], writes=[rpp])
                        S.copy("act", V[:, g, t4:t4 + 4, 0:128], pp[:].rearrange("p (t n) -> p t n", n=128),
                               reads=[rpp], writes=[rV[g]])
                for i in range(NT):
                    items = []
                    for g, (nm, maxd) in enumerate((("d1_", 1), ("d4_", 4), ("d16_", 15))):
                        for j in range(max(0, i - maxd), i + 1):
                            items.append((kT[:, g, j * 128:(j + 1) * 128], [rk[g]], V[:, g, j, 0:129], [rV[g]],
                                          MASK_IDS[nm + str(i - j)], None))
                    first = True
                    self.attn_multi(actx, [(qT[:, g, i * 128:(i + 1) * 128], [rq[g]]) for g in range(3)],
                                    items, [len([1 for j in range(max(0, i - md), i + 1)]) for md in (1, 4, 15)],
                                    O, rO, 129, 128 ** -0.5)
                    sm_, rsm = sm.next()
                    ob_, rob = ob.next()
                    S.op("dve", lambda e: e.reciprocal(sm_[:, 0:1], O[:, 128:129]), reads=[rO], writes=[rsm])
                    S.ts("dve", ob_[:], O[:, 0:128], sm_[:, 0:1], None, ALU.mult, None, reads=[rO, rsm], writes=[rob])
                    S.tr(ptr[:, 0:128], ob_[:], self.ident_bf[:], reads=[rob, self.rC], writes=[rptr])
                    S.copy("dve", oT[:, i * 128:(i + 1) * 128], ptr[:, 0:128], reads=[rptr], writes=[roT])
                S.dma("sp", self.OT[s, :, h, :], oT[:], reads=[roT], writes=[self.rOT[s]])
            S.barrier()

    def attn_multi(self, actx, qs, items, counts, O, rO, ncols, scale):
        S = self.S
        sring, pring = actx
        qidx = []
        for gi, cnt in enumerate(counts):
            qidx += [gi] * cnt
        n_items = len(items)
        groups = [list(range(i, min(i + 4, n_items))) for i in range(0, n_items, 4)]
        done = [0]

        def pv(grp, P, rP):
            for j, ii in enumerate(grp):
                it = items[ii]
                S.mm(O[:, 0:ncols], P[:, j * 128:(j + 1) * 128], it[2], start=(done[0] == 0),
                     stop=(done[0] == n_items - 1), reads=[rP] + list(it[3]), writes=[rO])
                done[0] += 1

        prev = None
        for grp in groups:
            sp, rsp = sring.next()
            for j, ii in enumerate(grp):
                it = items[ii]
                q_ap, q_reads = qs[qidx[ii]]
                S.mm(sp[:, j * 128:(j + 1) * 128], it[0], q_ap, start=True, stop=True,
                     reads=list(it[1]) + list(q_reads), writes=[rsp])
            P, rP = pring.next()
            w = len(grp) * 128
            S.act(P[:, 0:w], sp[:, 0:w], AF.Exp, reads=[rsp], writes=[rP], scale=scale)
            for j, ii in enumerate(grp):
                it = items[ii]
                if it[4] is not None:
                    S.tt("dve", P[:, j * 128:(j + 1) * 128], P[:, j * 128:(j + 1) * 128],
                         self.masks[:, it[4], :], ALU.mult, reads=[rP, self.rC], writes=[rP])
            if prev is not None:
                pv(*prev)
            prev = (grp, P, rP)
        pv(*prev)

    def mixer_B(self, L, s):
        S = self.S
        S.barrier()
        p = "l%d_" % L
        w_in = self.din[p + "b_w_in"]
        sc = 128 ** -0.5
        with ExitStack() as st:
            hT, rhT = self.load_hT(st, s)
            rope, perm, rr = self.load_rope(st, "B")
            big = Ring([(self.sb(st, "wbig%d" % i, [128, 16, 512], BF16), Res()) for i in range(1)])
            small = Ring([(self.sb(st, "wsm%d" % i, [128, 16, 128], BF16), Res()) for i in range(4)])
            pproj = Ring([(self.ps(st, "pproj%d" % i, [128, 512], F32), Res()) for i in range(3)])
            tring = Ring([(self.sb(st, "ropetmpB%d" % i, [128, 512], F32), Res()) for i in range(2)])
            pipe = AttnPipe(self, st, extra_banks=pproj.items)
            Oa, rOa = self.ps(st, "Oa", [128, 512], F32), Res()
            Ob, rOb = self.ps(st, "Ob", [128, 512], F32), Res()
            ptr = self.ps(st, "ptrB", [128, 1024], BF16)
            rptr = [Res(), Res(), Res()]
            qT = self.sb(st, "qTB", [128, 4, SEQ], BF16)
            rq = [[Res() for _ in range(4)] for _ in range(4)]
            kv = self.sb(st, "kvT", [128, 4, SEQ], BF16)
            rkv = [[Res() for _ in range(4)] for _ in range(4)]
            Vs = self.sb(st, "VsB", [128, 2, NT, 132], BF16)
            rVs = [Res() for _ in range(2)]
            oTr = Ring([(self.sb(st, "oTB%d" % i, [128, 4, 128], BF16), Res()) for i in range(2)])
            gate = self.sb(st, "gateB", [128, NT, 48], F32)
            rgate = Res()
            ropekc = self.sb(st, "ropekc", [128, 2, 128], F32)
            cover = self.sb(st, "coverB", [128, 32], BF16)
            expand = self.sb(st, "expandB", [32, SEQ], BF16)
            seladd = self.sb(st, "seladdB", [128, 16, 32], F32)
            rcst = Res()
            S.dma("sp", ropekc[:], self.din["c_ropeKC"], writes=[rcst])
            S.dma("pool", cover[:], self.din["c_cover"], writes=[rcst])
            S.dma("pool", expand[:], self.din["c_expand"], writes=[rcst])
            S.dma("sp", seladd[:], self.din["c_seladd"], writes=[rcst])
            S.op("dve", lambda e: e.memset(Vs[:, :, :, 128:129], 1.0), writes=rVs)
            w1 = self.sb(st, "cw1", [128, 32, 128], BF16)
            rw1 = Res()
            w2 = self.sb(st, "cw2", [128, 2, 128], BF16)
            posf = self.sb(st, "cposf", [128, 2, 32], F32)
            posb = self.sb(st, "cposb", [128, 2, 32], BF16)
            cb = self.sb(st, "cbias", [128, 2], F32)
            rcw = Res()
            for j in range(2):
                S.dma("pool", w2[:, j, :], self.din[p + "b_cmp_w2"][j], writes=[rcw])
                S.dma("sp", posf[:, j, :], self.din[p + "b_cmp_pos"][j].rearrange("l d -> d l"), writes=[rcw],
                      allow_slow_non_contiguous=True)
            S.copy("dve", posb[:], posf[:], reads=[rcw], writes=[rcw])
            wg, rwg = small.next()
            self.load_w(wg, rwg, w_in, 5120, 48)
            for t in range(NT):
                pp, rpp = pproj.next()
                self.proj_tm(pp, rpp, wg, rwg, 0, 48, hT, rhT, t)
                S.act(gate[:, t, :], pp[:, 0:48], AF.Sigmoid, reads=[rpp], writes=[rgate])
            kcT = self.sb(st, "kcT", [128, 128], BF16)
            vca = self.sb(st, "vcaug", [128, 192], BF16)
            rkc, rvc = Res(), Res()
            cu = self.sb(st, "cmp_u", [128, 128], F32)
            ct = self.sb(st, "cmp_t", [128, 128], F32)
            cg = self.sb(st, "cmp_g", [128, 128], BF16)
            rcu = Res()
            oacc = self.sb(st, "oaccB", [128, 4, 128], F32)
            roacc = [Res() for _ in range(4)]
            imp = self.sb(st, "impB", [128, 4, 32], F32)
            m8 = self.sb(st, "m8B", [128, 16], F32)
            amb = self.sb(st, "ambB", [128, 32], BF16)
            rimp = Res()
            sm = Ring([(self.sb(st, "smB%d" % i, [128, 4], F32), Res()) for i in range(3)])
            ob = Ring([(self.sb(st, "obB%d" % i, [128, 128], BF16), Res()) for i in range(3)])
            for g in range(4):
                wq, rwq = big.next()
                self.load_w(wq, rwq, w_in, g * 512, 512)
                for r in range(4):
                    for tc4 in range(4):
                        pp, rpp = pproj.next()
                        self.proj_fm(pp, rpp, wq, rwq, r * 128, hT, rhT, tc4 * 512)
                        d_ = qT[:, r, tc4 * 512:(tc4 + 1) * 512]
                        S.copy("act", d_, pp[:], reads=[rpp], writes=[rq[r][tc4]])
                        self.rope_fm2(d_, rq[r][tc4], pproj, perm[:], rope, rr, tc4 * 512, tring)
                for idx, (jj, do_rope) in enumerate(((2, True), (4, True), (0, False), (1, False))):
                    w_, rw_ = small.next()
                    self.load_w(w_, rw_, w_in, 2048 + jj * 512 + g * 128, 128)
                    for tc4 in range(4):
                        pp, rpp = pproj.next()
                        self.proj_fm(pp, rpp, w_, rw_, 0, hT, rhT, tc4 * 512)
                        d_ = kv[:, idx, tc4 * 512:(tc4 + 1) * 512]
                        S.copy("act", d_, pp[:], reads=[rpp], writes=[rkv[idx][tc4]])
                        if do_rope:
                            self.rope_fm2(d_, rkv[idx][tc4], pproj, perm[:], rope, rr, tc4 * 512, tring)
                for idx, jj in enumerate((3, 5)):
                    w_, rw_ = small.next()
                    self.load_w(w_, rw_, w_in, 2048 + jj * 512 + g * 128, 128)
                    for t4 in range(0, NT, 4):
                        pp, rpp = pproj.next()
                        for tt_ in range(4):
                            t = t4 + tt_
                            for c in range(16):
                                S.mm(pp[:, tt_ * 128:(tt_ + 1) * 128], hT[:, c, t * 128:(t + 1) * 128], w_[:, c, 0:128],
                                     start=(c == 0), stop=(c == 15), reads=[rw_, rhT], writes=[rpp])
                        if t4 == 0:
                            self.run_deferred()
                        S.copy("act", Vs[:, idx, t4:t4 + 4, 0:128], pp[:].rearrange("p (t n) -> p t n", n=128),
                               reads=[rpp], writes=[rVs[idx]])
                S.op("dve", lambda e: e.memset(kcT[:], 0.0), writes=[rkc])
                S.op("dve", lambda e: e.memset(vca[:], 0.0), writes=[rvc])
                S.op("dve", lambda e: e.memset(vca[:, 128:129], 1.0), writes=[rvc])
                S.copy("dve", vca[:, 129:161], cover[:], reads=[rcst], writes=[rvc])
                for j in range(2):
                    S.dma("pool", w1[:], self.din[p + "b_cmp_w1"][j].rearrange("(l d) o -> d l o", d=128), writes=[rw1])
                    pp, rpp = pproj.next()
                    for l in range(32):
                        S.mm(pp[:, 0:1], w1[:, l, :], posb[:, j, l:l + 1], start=(l == 0), stop=(l == 31),
                             reads=[rcw, rw1], writes=[rpp])
                    S.copy("dve", cb[:, j:j + 1], pp[:, 0:1], reads=[rpp], writes=[rcw])
                    pp, rpp = pproj.next()
                    for l in range(32):
                        S.mm(pp[:, 0:127], w1[:, l, :], kv[:, 2 + j, l:l + 16 * 126 + 1:16], start=(l == 0), stop=(l == 31),
                             reads=[rw1] + rkv[2 + j], writes=[rpp])
                    S.act(cu[:, 0:127], pp[:, 0:127], AF.Identity, reads=[rpp, rcw], writes=[rcu], bias=cb[:, j:j + 1], scale=1.0)
                    S.tt("dve", ct[:, 0:127], cu[:, 0:127], cu[:, 0:127], ALU.mult, reads=[rcu], writes=[rcu])
                    S.ts("dve", ct[:, 0:127], ct[:, 0:127], 0.044715, 1.0, ALU.mult, ALU.add, reads=[rcu], writes=[rcu])
                    S.tt("dve", ct[:, 0:127], ct[:, 0:127], cu[:, 0:127], ALU.mult, reads=[rcu], writes=[rcu])
                    S.act(ct[:, 0:127], ct[:, 0:127], AF.Sigmoid, reads=[rcu], writes=[rcu], scale=2.0 * 0.7978845608028654)
                    S.tt("dve", cg[:, 0:127], cu[:, 0:127], ct[:, 0:127], ALU.mult, reads=[rcu], writes=[rcu])
                    pp, rpp = pproj.next()
                    if j == 0:
                        S.mm(pp[:, 0:127], w2[:, 0, :], cg[:, 0:127], start=True, stop=True, reads=[rcw, rcu], writes=[rpp])
                        S.copy("act", kcT[:, 0:127], pp[:, 0:127], reads=[rpp], writes=[rkc])
                        pr, rpr = pproj.next()
                        tmp, rtmp = tring.next()
                        S.mm(pr[:, 0:128], perm[:], kcT[:], start=True, stop=True, reads=[rkc, rr], writes=[rpr])
                        S.tt("pool", tmp[:, 0:128], kcT[:], ropekc[:, 0, :], ALU.mult, reads=[rkc, rcst], writes=[rtmp])
                        S.tt("dve", kcT[:], pr[:, 0:128], ropekc[:, 1, :], ALU.mult, reads=[rpr, rcst], writes=[rkc])
                        S.tt("dve", kcT[:], kcT[:], tmp[:, 0:128], ALU.add, reads=[rtmp], writes=[rkc])
                    else:
                        S.mm(pp[0:127, 0:128], cg[:, 0:127], w2[:, 1, :], start=True, stop=True, reads=[rcw, rcu], writes=[rpp])
                        S.copy("act", vca[0:127, 0:128], pp[0:127, 0:128], reads=[rpp], writes=[rvc])
                Obank = [(Oa, rOa), (Ob, rOb)]
                amTs = [(self.sb(st, "amTB%d_%d" % (g, k_), [32, 128], BF16), Res()) for k_ in range(2)]
                for i in range(NT):
                    oT, roT = oTr.next()
                    amT, ramT = amTs[i % 2]
                    S.op("dve", lambda e: e.memset(imp[:, 0, :], 0.0), writes=[rimp])
                    items = []
                    for r in range(4):
                        O_, rO_ = Obank[r // 2]
                        items.append(dict(k=kcT[:], kr=[rkc], q=qT[:, r, i * 128:(i + 1) * 128], qr=[rq[r][i // 4]],
                                          v=vca[:, 0:161], vr=[rvc], mask=MASK_IDS["cmp_%d" % i],
                                          O=O_[:, (r % 2) * 256:(r % 2) * 256 + 161], rO=rO_, start=True, stop=True))
                    pipe.run(items, sc)
                    pipe.flush()
                    for r in range(4):
                        hq = g * 4 + r
                        O_, rO_ = Obank[r // 2]
                        OC = O_[:, (r % 2) * 256:(r % 2) * 256 + 161]
                        sm_, rsm = sm.next()
                        S.ts("dve", sm_[:, 0:1], OC[:, 128:129], 1e-30, None, ALU.max, None, reads=[rO_], writes=[rsm])
                        S.op("dve", lambda e: e.reciprocal(sm_[:, 0:1], sm_[:, 0:1]), reads=[rsm], writes=[rsm])
                        S.stt(imp[:, 0, :], OC[:, 129:161], sm_[:, 0:1], imp[:, 0, :], ALU.mult, ALU.add,
                              reads=[rO_, rsm, rimp], writes=[rimp])
                        S.tt("dve", sm_[:, 1:2], sm_[:, 0:1], gate[:, i, hq * 3:hq * 3 + 1], ALU.mult, reads=[rsm, rgate], writes=[rsm])
                        S.ts("dve", oacc[:, r, :], OC[:, 0:128], sm_[:, 1:2], None, ALU.mult, None, reads=[rO_, rsm], writes=[roacc[r]])
                    S.tt("dve", imp[:, 1, :], imp[:, 0, :], seladd[:, i, :], ALU.add, reads=[rimp, rcst], writes=[rimp])
                    S.op("dve", lambda e: e.max(m8[:, 0:8], imp[:, 1, :]), reads=[rimp], writes=[rimp])
                    S.op("dve", lambda e: e.match_replace(imp[:, 2, :], m8[:, 0:8], imp[:, 1, :], -3.0e38), reads=[rimp], writes=[rimp])
                    S.op("dve", lambda e: e.max(m8[:, 8:16], imp[:, 2, :]), reads=[rimp], writes=[rimp])
                    S.ts("dve", imp[:, 3, :], imp[:, 1, :], m8[:, 15:16], None, ALU.is_ge, None, reads=[rimp], writes=[rimp])
                    S.ts("dve", amb[:], imp[:, 3, :], -1.0, -NEG, ALU.add, ALU.mult, reads=[rimp], writes=[rimp])
                    S.tr(ptr[0:32, 256:384], amb[:], self.ident_bf[:], reads=[rimp, self.rC], writes=[rptr[2]])
                    S.copy("dve", amT[:], ptr[0:32, 256:384], reads=[rptr[2]], writes=[ramT])

                    def epi(r, i, O_, rO_, oT, roT, par):
                        hq = g * 4 + r
                        sm_, rsm = sm.next()
                        ob_, rob = ob.next()
                        Osel, Owin = O_[:, 0:129], O_[:, 256:385]
                        S.op("dve", lambda e: e.reciprocal(sm_[:, 0:1], Osel[:, 128:129]), reads=[rO_], writes=[rsm])
                        S.op("dve", lambda e: e.reciprocal(sm_[:, 1:2], Owin[:, 128:129]), reads=[rO_], writes=[rsm])
                        S.tt("dve", sm_[:, 0:2], sm_[:, 0:2], gate[:, i, hq * 3 + 1:hq * 3 + 3], ALU.mult, reads=[rsm, rgate], writes=[rsm])
                        S.stt(oacc[:, r, :], Osel[:, 0:128], sm_[:, 0:1], oacc[:, r, :], ALU.mult, ALU.add,
                              reads=[rO_, rsm, roacc[r]], writes=[roacc[r]])
                        S.stt(ob_[:], Owin[:, 0:128], sm_[:, 1:2], oacc[:, r, :], ALU.mult, ALU.add,
                              reads=[rO_, rsm, roacc[r]], writes=[rob])

                        def tr_():
                            pt = ptr[:, par * 128:(par + 1) * 128]
                            S.tr(pt, ob_[:], self.ident_bf[:], reads=[rob, self.rC], writes=[rptr[par]])
                            S.copy("dve", oT[:, r, :], pt, reads=[rptr[par]], writes=[roT])
                        self.defer(tr_, key="tr")

                    for r in range(4):
                        O_, rO_ = Obank[r % 2]
                        q_ap = qT[:, r, i * 128:(i + 1) * 128]
                        items = [dict(k=kv[:, 0, j * 128:(j + 1) * 128], kr=[rkv[0][j // 4]], q=q_ap, qr=[rq[r][i // 4]],
                                      v=Vs[:, 0, j, 0:129], vr=[rVs[0]],
                                      mask=MASK_IDS["causal"] if j == i else None,
                                      extra=(expand[:, j * 128:(j + 1) * 128], amT[:], [rcst, ramT]),
                                      O=O_[:, 0:129], rO=rO_, start=(j == 0), stop=(j == i)) for j in range(i + 1)]
                        j0_ = max(0, i - 4)
                        items += [dict(k=kv[:, 1, j * 128:(j + 1) * 128], kr=[rkv[1][j // 4]], q=q_ap, qr=[rq[r][i // 4]],
                                       v=Vs[:, 1, j, 0:129], vr=[rVs[1]],
                                       mask=MASK_IDS["causal"] if j == i else (MASK_IDS["win4"] if j == i - 4 else None),
                                       O=O_[:, 256:385], rO=rO_, start=(j == j0_), stop=(j == i)) for j in range(j0_, i + 1)]
                        pipe.run(items, sc, after=lambda r=r, i=i, O_=O_, rO_=rO_, oT=oT, roT=roT, par=r % 2:
                                 epi(r, i, O_, rO_, oT, roT, par))
                    pipe.flush()
                    self.run_deferred("tr")
                    S.dma("sp", self.OT[s, :, g * 4:(g + 1) * 4, i * 128:(i + 1) * 128], oT[:], reads=[roT], writes=[self.rOT[s]])
            S.barrier()

    def mixer_A(self, L, s):
        S = self.S
        S.barrier()
        p = "l%d_" % L
        w_in = self.din[p + "a_w_in"]
        with ExitStack() as st:
            hT, rhT = self.load_hT(st, s)
            rope, perm, rr = self.load_rope(st, "A")
            slots = Ring([(self.sb(st, "wslot%d" % i, [128, 16, 256], BF16), Res()) for i in range(5)])
            pproj = Ring([(self.ps(st, "pproj%d" % i, [128, 512], F32), Res()) for i in range(3)])
            pipe = AttnPipe(self, st, extra_banks=pproj.items)
            Ob = [(self.ps(st, "OA%d" % i, [128, 512], F32), Res()) for i in range(2)]
            ptr = self.ps(st, "ptrA", [128, 1024], BF16)
            rptr = [Res(), Res()]
            qT = self.sb(st, "qT", [128, 2, SEQ], BF16)
            kT = self.sb(st, "kT", [128, 2, SEQ], BF16)
            rqk = [[[Res() for _ in range(4)] for _ in range(2)] for _ in range(2)]
            V = self.sb(st, "Vaug", [128, NT, 2, 130], BF16)
            rV = Res()
            tring = Ring([(self.sb(st, "ropetmp%d" % i, [128, 512], F32), Res()) for i in range(2)])
            oT = self.sb(st, "oTg", [128, 2, SEQ], BF16)
            roT = Res()
            lam = self.sb(st, "lam", [128, 256], F32)
            lsc = self.sb(st, "lsc", [128, 8], F32)
            gv = self.sb(st, "gv", [128, 128], F32)
            rl = Res()
            S.dma("sp", lam[:], self.din[p + "a_lam"].rearrange("a b -> (a b)").partition_broadcast(128), writes=[rl])
            S.dma("sp", gv[:], self.din[p + "a_subln"].partition_broadcast(128), writes=[rl])
            S.tt("dve", lam[:, 0:64], lam[:, 0:64], lam[:, 64:128], ALU.mult, reads=[rl], writes=[rl])
            S.tt("dve", lam[:, 128:192], lam[:, 128:192], lam[:, 192:256], ALU.mult, reads=[rl], writes=[rl])
            S.op("dve", lambda e: e.reduce_sum(lsc[:, 0:1], lam[:, 0:64], axis=AX.X), reads=[rl], writes=[rl])
            S.op("dve", lambda e: e.reduce_sum(lsc[:, 1:2], lam[:, 128:192], axis=AX.X), reads=[rl], writes=[rl])
            S.act(lsc[:, 2:4], lsc[:, 0:2], AF.Exp, reads=[rl], writes=[rl])
            S.tt("dve", lsc[:, 4:5], lsc[:, 3:4], lsc[:, 2:3], ALU.subtract, reads=[rl], writes=[rl])
            S.ts("dve", lsc[:, 4:5], lsc[:, 4:5], -LAM_INIT[L], None, ALU.add, None, reads=[rl], writes=[rl])
            S.ts("dve", gv[:], gv[:], 1.0 - LAM_INIT[L], None, ALU.mult, None, reads=[rl], writes=[rl])
            S.op("dve", lambda e: e.memset(V[:, :, :, 128:129], 1.0), writes=[rV])
            osb = Ring([(self.sb(st, "osb%d" % i, [128, 128], F32), Res()) for i in range(2)])
            obf = Ring([(self.sb(st, "obf%d" % i, [128, 128], BF16), Res()) for i in range(3)])
            sm = Ring([(self.sb(st, "smA%d" % i, [128, 8], F32), Res()) for i in range(2)])
            junk = Ring([(self.sb(st, "junkA%d" % i, [128, 128], F32), Res()) for i in range(2)])
            cnt = [0]

            def epilogue(hh, i, O, rO, par):
                sm_, rsm = sm.next()
                o_, ro = osb.next()
                ob, rob = obf.next()
                O1, O2 = O[:, 0:129], O[:, 256:385]
                S.op("dve", lambda e: e.reciprocal(sm_[:, 0:1], O1[:, 128:129]), reads=[rO], writes=[rsm])
                S.op("dve", lambda e: e.reciprocal(sm_[:, 1:2], O2[:, 128:129]), reads=[rO], writes=[rsm])
                S.tt("dve", sm_[:, 1:2], sm_[:, 1:2], lsc[:, 4:5], ALU.mult, reads=[rsm, rl], writes=[rsm])
                S.ts("dve", o_[:], O1[:, 0:128], sm_[:, 0:1], None, ALU.mult, None, reads=[rO, rsm], writes=[ro])
                S.stt(o_[:], O2[:, 0:128], sm_[:, 1:2], o_[:], ALU.mult, ALU.add, reads=[rO, rsm, ro], writes=[ro])
                jk, rjk = junk.next()
                S.tt("dve", jk[:], o_[:], o_[:], ALU.mult, reads=[ro], writes=[rjk])
                S.op("dve", lambda e: e.reduce_sum(sm_[:, 2:3], jk[:], axis=AX.X), reads=[rjk], writes=[rsm])
                S.ts("dve", sm_[:, 2:3], sm_[:, 2:3], 1.0 / 128.0, EPS, ALU.mult, ALU.add, reads=[rsm], writes=[rsm])
                S.tt("pool", sm_[:, 3:4], sm_[:, 2:3], self.nhalf[:, 0:1], ALU.pow, reads=[rsm, self.rC], writes=[rsm])
                S.stt(ob[:], o_[:], sm_[:, 3:4], gv[:], ALU.mult, ALU.mult, reads=[ro, rsm, rl], writes=[rob])

                def tr_():
                    pt = ptr[:, par * 128:(par + 1) * 128]
                    S.tr(pt, ob[:], self.ident_bf[:], reads=[rob, self.rC], writes=[rptr[par]])
                    S.copy("dve", oT[:, hh, i * 128:(i + 1) * 128], pt, reads=[rptr[par]], writes=[roT])
                self.defer(tr_, key="tr")

            for hg in range(8):
                wq, rwq = slots.next(); wk, rwk = slots.next(); wv, rwv = slots.next()
                self.load_w(wq, rwq, w_in, hg * 256, 256)
                self.load_w(wk, rwk, w_in, 2048 + hg * 256, 256)
                self.load_w(wv, rwv, w_in, 4096 + hg * 256, 256)
                for hh in range(2):
                    for (dst, w_, rw_, which) in ((qT, wq, rwq, 0), (kT, wk, rwk, 1)):
                        for tc4 in range(4):
                            pp, rpp = pproj.next()
                            self.proj_fm(pp, rpp, w_, rw_, hh * 128, hT, rhT, tc4 * 512)
                            d_ = dst[:, hh, tc4 * 512:(tc4 + 1) * 512]
                            S.copy("act", d_, pp[:], reads=[rpp], writes=[rqk[which][hh][tc4]])
                            self.rope_fm2(d_, rqk[which][hh][tc4], pproj, perm[:], rope, rr, tc4 * 512, tring)
                for t in range(NT):
                    pp, rpp = pproj.next()
                    self.proj_tm(pp, rpp, wv, rwv, 0, 256, hT, rhT, t)
                    if t == 0:
                        self.run_deferred()
                    S.copy("act", V[:, t, :, 0:128], pp[:, 0:256].rearrange("p (h n) -> p h n", n=128),
                           reads=[rpp], writes=[rV])
                for hh in range(2):
                    for i in range(NT):
                        par = cnt[0] % 2
                        cnt[0] += 1
                        O, rO = Ob[par]
                        for m in range(2):
                            items = []
                            for j in range(i + 1):
                                items.append(dict(k=kT[m * 64:(m + 1) * 64, hh, j * 128:(j + 1) * 128], kr=[rqk[1][hh][j // 4]],
                                                  q=qT[m * 64:(m + 1) * 64, hh, i * 128:(i + 1) * 128], qr=[rqk[0][hh][i // 4]],
                                                  v=V[:, j, hh, 0:129], vr=[rV],
                                                  mask=MASK_IDS["causal"] if j == i else None,
                                                  O=O[:, m * 256:m * 256 + 129], rO=rO, start=(j == 0), stop=(j == i)))
                            pipe.run(items, 0.125, after=(None if m == 0 else
                                     (lambda hh=hh, i=i, O=O, rO=rO, par=par: epilogue(hh, i, O, rO, par))))
                pipe.flush()
                self.run_deferred("tr")
                S.dma("sp", self.OT[s, :, hg * 2:hg * 2 + 2, :], oT[:], reads=[roT], writes=[self.rOT[s]])
            S.barrier()

    def mixer_C(self, L, s):
        S = self.S
        S.barrier()
        p = "l%d_" % L
        w_in = self.din[p + "c_w_in"]
        with ExitStack() as st:
            hT, rhT = self.load_hT(st, s)
            rope, perm, rr = self.load_rope(st, "B")
            slots = Ring([(self.sb(st, "wslotC%d" % i, [128, 16, 128], BF16), Res()) for i in range(12)])
            pproj = Ring([(self.ps(st, "pproj%d" % i, [128, 512], F32), Res()) for i in range(3)])
            pipe = AttnPipe(self, st, extra_banks=pproj.items)
            Ob = [(self.ps(st, "OC%d" % i, [128, 512], F32), Res()) for i in range(2)]
            ptr = self.ps(st, "ptrC", [128, 1024], BF16)
            rptr = [Res(), Res()]
            qT = self.sb(st, "qTC", [128, 3, SEQ], BF16)
            kT = self.sb(st, "kTC", [128, 3, SEQ], BF16)
            rq = [[Res() for _ in range(4)] for _ in range(3)]
            rk = [[Res() for _ in range(4)] for _ in range(3)]
            V = self.sb(st, "VaugC", [128, 3, NT, 130], BF16)
            rV = [Res() for _ in range(3)]
            tring = Ring([(self.sb(st, "ropetmp%d" % i, [128, 512], F32), Res()) for i in range(2)])
            oT, roT = self.sb(st, "oTC", [128, SEQ], BF16), Res()
            S.op("dve", lambda e: e.memset(V[:, :, :, 128:129], 1.0), writes=rV)
            ob = Ring([(self.sb(st, "obC%d" % i, [128, 128], BF16), Res()) for i in range(3)])
            sm = Ring([(self.sb(st, "smC%d" % i, [128, 2], F32), Res()) for i in range(2)])
            cnt = [0]

            def epilogue(i, O, rO, par):
                sm_, rsm = sm.next()
                ob_, rob = ob.next()
                S.op("dve", lambda e: e.reciprocal(sm_[:, 0:1], O[:, 128:129]), reads=[rO], writes=[rsm])
                S.ts("dve", ob_[:], O[:, 0:128], sm_[:, 0:1], None, ALU.mult, None, reads=[rO, rsm], writes=[rob])

                def tr_():
                    pt = ptr[:, par * 128:(par + 1) * 128]
                    S.tr(pt, ob_[:], self.ident_bf[:], reads=[rob, self.rC], writes=[rptr[par]])
                    S.copy("dve", oT[:, i * 128:(i + 1) * 128], pt, reads=[rptr[par]], writes=[roT])
                self.defer(tr_, key="tr")

            for h in range(8):
                for g in range(3):
                    for which, (dst, rd) in enumerate(((qT, rq), (kT, rk))):
                        w_, rw_ = slots.next()
                        self.load_w(w_, rw_, w_in, which * 3072 + g * 1024 + h * 128, 128)
                        for tc4 in range(4):
                            pp, rpp = pproj.next()
                            self.proj_fm(pp, rpp, w_, rw_, 0, hT, rhT, tc4 * 512)
                            d_ = dst[:, g, tc4 * 512:(tc4 + 1) * 512]
                            S.copy("act", d_, pp[:], reads=[rpp], writes=[rd[g][tc4]])
                            self.rope_fm2(d_, rd[g][tc4], pproj, perm[:], rope, rr, tc4 * 512, tring)
                    w_, rw_ = slots.next()
                    self.load_w(w_, rw_, w_in, 2 * 3072 + g * 1024 + h * 128, 128)
                    for t4 in range(0, NT, 4):
                        pp, rpp = pproj.next()
                        for tt_ in range(4):
                            t = t4 + tt_
                            for c in range(16):
                                S.mm(pp[:, tt_ * 128:(tt_ + 1) * 128], hT[:, c, t * 128:(t + 1) * 128], w_[:, c, 0:128],
                                     start=(c == 0), stop=(c == 15), reads=[rw_, rhT], writes=[rpp])
                        if t4 == 0:
                            self.run_deferred()
                        S.copy("act", V[:, g, t4:t4 + 4, 0:128], pp[:].rearrange("p (t n) -> p t n", n=128),
                               reads=[rpp], writes=[rV[g]])
                for i in range(NT):
                    par = cnt[0] % 2
                    cnt[0] += 1
                    O, rO = Ob[par]
                    items = []
                    for g, (nm, maxd) in enumerate((("d1_", 1), ("d4_", 4), ("d16_", 15))):
                        for j in range(max(0, i - maxd), i + 1):
                            items.append(dict(k=kT[:, g, j * 128:(j + 1) * 128], kr=[rk[g][j // 4]],
                                              q=qT[:, g, i * 128:(i + 1) * 128], qr=[rq[g][i // 4]],
                                              v=V[:, g, j, 0:129], vr=[rV[g]], mask=MASK_IDS[nm + str(i - j)],
                                              O=O[:, 0:129], rO=rO, start=False, stop=False))
                    items[0]["start"] = True
                    items[-1]["stop"] = True
                    pipe.run(items, 128 ** -0.5, after=lambda i=i, O=O, rO=rO, par=par: epilogue(i, O, rO, par))
                pipe.flush()
                self.run_deferred("tr")
                S.dma("sp", self.OT[s, :, h, :], oT[:], reads=[roT], writes=[self.rOT[s]])
            S.barrier()

    def phase_outproj_ln1(self, L, s, src_is_x):
        S = self.S
        S.barrier()
        p = "l%d_" % L
        kind = KINDS[L]
        w_out = self.din[p + {"A": "a_w_out", "B": "b_w_out", "C": "c_w_out"}[kind]]
        kc = 8 if kind == "C" else 16
        hsrc = self.din["x"] if src_is_x else self.H
        with ExitStack() as st:
            oTr = Ring([(self.sb(st, "oTres%d" % i, [128, kc, 128], BF16), Res()) for i in range(2)])
            wo = self.sb(st, "wo", [128, kc, D], BF16)
            rwo = Res()
            g_ = {}
            for c in range(0, kc, 4):
                S.dma("pool", wo[:, c:c + 4, :], w_out.rearrange("(c p) n -> p c n", p=128)[:, c:c + 4, :], writes=[rwo], grp=g_)
            gb, rgb, stats, rstats = self.alloc_ln(st, L, "1", 0)
            wr = self.sb(st, "wr", [128, 16, 36], F32)
            rb = self.sb(st, "rb", [128, 36], F32)
            rwr = Res()
            S.dma("sp", wr[:], self.din[p + "router_w"].rearrange("(c p) n -> p c n", p=128), writes=[rwr])
            S.dma("sp", rb[:], self.din[p + "router_b"].partition_broadcast(128), writes=[rwr])
            pso = [(self.ps(st, "pso%d" % i, [128, 512], F32), Res()) for i in range(4)]
            ptf = Ring([(self.ps(st, "ptf%d" % i, [128, 512], F32), Res()) for i in range(2)])
            plg, rplg = self.ps(st, "plg", [128, 64], F32), Res()
            hres = Ring([(self.sb(st, "hres%d" % i, [128, D], F32), Res()) for i in range(2)])
            a_t, ra = self.sb(st, "a_t", [128, D], F32), Res()
            tmp, rtmp = self.sb(st, "lntmp", [128, D], F32), Res()
            h1 = Ring([(self.sb(st, "h1_%d" % i, [128, D], F32), Res()) for i in range(2)])
            xb = Ring([(self.sb(st, "xb_%d" % i, [128, D], BF16), Res()) for i in range(2)])
            hTf, rhTf = self.sb(st, "hTf", [128, 16, 128], F32), Res()
            g = self.sb(st, "gate", [128, 64], F32)
            g8 = self.sb(st, "gate8", [128, 16], F32)
            rg = Res()
            for t in range(NT):
                gt = s * NT + t
                hr, rhr = hres.next()
                S.dma("sp", hr[:], hsrc[gt * 128:(gt + 1) * 128, :], reads=[] if src_is_x else [self.rH[gt]], writes=[rhr])
                oT, roT = oTr.next()
                S.dma("sp", oT[:], self.OT[s, :, 0:kc, t * 128:(t + 1) * 128], reads=[self.rOT[s]], writes=[roT])
                for cc in range(4):
                    po, rpo = pso[cc]
                    for c in range(kc):
                        S.mm(po[:], oT[:, c, :], wo[:, c, cc * 512:(cc + 1) * 512],
                             start=(c == 0), stop=(c == kc - 1), reads=[roT, rwo], writes=[rpo])
                    S.stt(a_t[:, cc * 512:(cc + 1) * 512], hr[:, cc * 512:(cc + 1) * 512], ALPHA, po[:],
                          ALU.mult, ALU.add, reads=[rhr, rpo], writes=[ra])
                h1t, rh1 = h1.next()
                self.layer_norm_tile(a_t, ra, gb, rgb, 0, h1t, rh1, stats, rstats, tmp, rtmp)
                S.dma("sp", self.H1[gt * 128:(gt + 1) * 128, :], h1t[:], reads=[rh1], writes=[self.rH1[gt]])
                xbt, rxb = xb.next()
                S.copy("act", xbt[:], h1t[:], reads=[rh1], writes=[rxb])
                S.dma("sp", self.XB[gt * 128:(gt + 1) * 128, :], xbt[:], reads=[rxb], writes=[self.rXB[gt]])
                for q4 in range(4):
                    pt, rpt = ptf.next()
                    for j in range(4):
                        c = q4 * 4 + j
                        S.tr(pt[:, j * 128:(j + 1) * 128], h1t[:, c * 128:(c + 1) * 128], self.ident_f[:],
                             reads=[rh1, self.rC], writes=[rpt])
                    S.copy("dve" if q4 % 2 == 0 else "act", hTf[:, q4 * 4:(q4 + 1) * 4, :],
                           pt[:].rearrange("p (j n) -> p j n", n=128), reads=[rpt], writes=[rhTf])
                for c in range(16):
                    S.mm(plg[:, 0:36], hTf[:, c, :], wr[:, c, :], start=(c == 0), stop=(c == 15),
                         reads=[rhTf, rwr], writes=[rplg])
                self.gating(gt, plg, rplg, rb, rwr, g, g8, rg)
            S.barrier()

    def gating(self, gt, plg, rplg, rb, rwr, g, g8, rg):
        S = self.S
        R = [rg]
        lg = g[:, 0:36]
        S.tt("dve", lg, plg[:, 0:36], rb[:, 0:36], ALU.add, reads=[rplg, rwr], writes=R)
        gm = g[:, 36:37]
        S.op("dve", lambda e: e.reduce_max(gm, g[:, 0:4], axis=AX.X), reads=R, writes=R)
        ngm = g[:, 37:38]
        S.ts("dve", ngm, gm, -1.0, None, ALU.mult, None, reads=R, writes=R)
        ex = g[:, 40:44]
        S.act(ex, g[:, 0:4], AF.Exp, reads=R, writes=R, bias=ngm, scale=1.0)
        gs = g[:, 38:39]
        S.op("dve", lambda e: e.reduce_sum(gs, ex, axis=AX.X), reads=R, writes=R)
        gw = g[:, 39:40]
        S.op("dve", lambda e: e.reciprocal(gw, gs), reads=R, writes=R)
        gmask = g[:, 44:48]
        S.ts("dve", gmask, g[:, 0:4], gm, None, ALU.is_ge, None, reads=R, writes=R)
        S.ts("dve", gmask, gmask, -1.0, 1e30, ALU.add, ALU.mult, reads=R, writes=R)
        el = g[:, 4:36]
        S.tt("dve", el.rearrange("p (a b) -> p a b", b=8), el.rearrange("p (a b) -> p a b", b=8),
             gmask.unsqueeze(2).broadcast_to([128, 4, 8]), ALU.add, reads=R, writes=R)
        S.op("dve", lambda e: e.max(g8[:, 0:8], el), reads=R, writes=R)
        d = g8[:, 8:9]
        S.tt("dve", d, g8[:, 1:2], g8[:, 0:1], ALU.subtract, reads=R, writes=R)
        S.act(d, d, AF.Exp, reads=R, writes=R)
        S.ts("dve", d, d, 1.0, None, ALU.add, None, reads=R, writes=R)
        S.op("dve", lambda e: e.reciprocal(d, d), reads=R, writes=R)
        Wt = [self.rOH[gt]]
        S.tt("dve", self.GW[:, gt, 0:1], d, gw, ALU.mult, reads=R, writes=Wt)
        S.tt("dve", self.GW[:, gt, 1:2], gw, self.GW[:, gt, 0:1], ALU.subtract, reads=R + Wt, writes=Wt)
        S.ts("dve", self.OH[:, gt, 0, :], el, g8[:, 0:1], None, ALU.is_equal, None, reads=R, writes=Wt)
        S.ts("dve", self.OH[:, gt, 1, :], el, g8[:, 1:2], None, ALU.is_equal, None, reads=R, writes=Wt)

    def phase_dispatch(self, L):
        S = self.S
        S.barrier()
        NTT, NBLK = self.NTT, self.NBLK
        with ExitStack() as st:
            A = self.sb(st, "dA", [128, NTT + 1, 32], BF16)
            Ac = self.sb(st, "dAc", [128, NTT + 1, 32], BF16)
            rA = Res()
            prk = Ring([(self.ps(st, "prk%d" % i, [128, 32], F32), Res()) for i in range(2)])
            psz, rpsz = self.ps(st, "psz", [128, 32], F32), Res()
            sz = self.sb(st, "dsz", [128, 8, 32], F32)
            rsz = Res()
            S.op("dve", lambda e: e.memset(Ac[:, 0, :], 0.0), writes=[rA])
            for t in range(NTT):
                S.tt("dve", A[:, t, :], self.OH[:, t, 0, :], self.OH[:, t, 1, :], ALU.add, reads=[self.rOH[t]], writes=[rA])
                S.tt("dve", Ac[:, t + 1, :], Ac[:, t, :], A[:, t, :], ALU.add, reads=[rA], writes=[rA])
            S.mm(psz[:], self.ones_bf[:], Ac[:, NTT, :], start=True, stop=True, reads=[rA, self.rC], writes=[rpsz])
            sizes, padded, pend, pstart, t0, t1 = (sz[:, i, :] for i in range(6))
            S.copy("dve", sizes, psz[:], reads=[rpsz], writes=[rsz])
            S.op("dve", lambda e: e.memset(padded, 0.0), writes=[rsz])
            for m in range(self.NTOK * 2 // self.RB):
                S.stt(padded, sizes, float(self.RB * m), padded, ALU.is_gt, ALU.add, reads=[rsz], writes=[rsz])
            S.ts("dve", padded, padded, float(self.RB), None, ALU.mult, None, reads=[rsz], writes=[rsz])
            cur, oth = pend, t0
            S.copy("dve", cur, padded, reads=[rsz], writes=[rsz])
            for sh in (1, 2, 4, 8, 16):
                S.copy("dve", oth, cur, reads=[rsz], writes=[rsz])
                S.tt("dve", oth[:, sh:32], cur[:, sh:32], cur[:, 0:32 - sh], ALU.add, reads=[rsz], writes=[rsz])
                cur, oth = oth, cur
            S.copy("dve", t1, cur, reads=[rsz], writes=[rsz])
            pend = t1
            S.tt("dve", pstart, pend, padded, ALU.subtract, reads=[rsz], writes=[rsz])
            be = self.sb(st, "dbe", [128, NBLK], F32)
            io = self.sb(st, "dio", [128, NBLK], F32)
            rbe = Res()
            S.dma("sp", io[:], self.din["c_blk"], writes=[rbe])
            S.op("dve", lambda e: e.memset(be[:], 0.0), writes=[rbe])
            for e_ in range(32):
                S.stt(be[:], io[:], pend[:, e_:e_ + 1], be[:], ALU.is_ge, ALU.add, reads=[rsz, rbe], writes=[rbe])
            S.ts("dve", be[:], be[:], 31.0, None, ALU.min, None, reads=[rbe], writes=[rbe])
            S.ts("dve", io[:], be[:], 2048.0, self.iota[:, 0:1], ALU.mult, ALU.add, reads=[rbe, self.rC], writes=[rbe])
            S.copy("dve", self.IDX[:, 0, :], io[:], reads=[rbe], writes=[self.rIDX])
            S.ts("dve", io[:], be[:], 512.0, self.iota[:, 0:1], ALU.mult, ALU.add, reads=[rbe, self.rC], writes=[rbe])
            S.copy("dve", self.IDX[:, 1, :], io[:], reads=[rbe], writes=[self.rIDX])
            dtmp = self.sb(st, "dtmp", [128, 4, 32], F32)
            dd = self.sb(st, "ddest", [128, 2], F32)
            rd = Res()
            xbr = Ring([(self.sb(st, "dxb%d" % i, [128, D], BF16), Res()) for i in range(3)])
            for t in range(NTT):
                pr_, rpr = prk.next()
                S.mm(pr_[:], self.tri[:], A[:, t, :], start=True, stop=False, reads=[rA, self.rC], writes=[rpr])
                S.mm(pr_[:], self.ones_bf[:], Ac[:, t, :], start=False, stop=True, reads=[rA, self.rC], writes=[rpr])
                S.tt("dve", dtmp[:, 0, :], pr_[:], pstart, ALU.add, reads=[rpr, rsz], writes=[rd])
                for k in range(2):
                    S.tt("dve", dtmp[:, 1 + k, :], dtmp[:, 0, :], self.OH[:, t, k, :], ALU.mult,
                         reads=[rd, self.rOH[t]], writes=[rd])
                    S.op("dve", lambda e, k=k: e.reduce_sum(dd[:, k:k + 1], dtmp[:, 1 + k, :], axis=AX.X),
                         reads=[rd], writes=[rd])
                S.copy("dve", self.DST[:, t, :], dd[:], reads=[rd], writes=[self.rDST[t]])
                xb_, rxb = xbr.next()
                S.dma("sp", xb_[:], self.XB[t * 128:(t + 1) * 128, :], reads=[self.rXB[t]], writes=[rxb])
                for k in range(2):
                    S.dma("pool", self.XS, xb_[:], reads=[rxb, self.rDST[t]], writes=[self.rXS],
                          indirect={"out_offset": bass.IndirectOffsetOnAxis(ap=self.DST[:, t, k:k + 1], axis=0)})
            S.barrier()

    def phase_blocks(self, L):
        S = self.S
        S.barrier()
        p = "l%d_" % L
        w_in = self.din[p + "moe_w_in"].rearrange("e d n -> (e d) n")
        w_out = self.din[p + "moe_w_out"].rearrange("e f n -> (e f) n")
        with ExitStack() as st:
            win = Ring([(self.sb(st, "win%d" % i, [128, 16, 1024], BF16), Res()) for i in range(3)])
            wout = Ring([(self.sb(st, "wout%d" % i, [128, 4, D], BF16), Res()) for i in range(3)])
            xs = Ring([(self.sb(st, "bxs%d" % i, [128, D], BF16), Res()) for i in range(2)])
            xsT = Ring([(self.sb(st, "bxsT%d" % i, [128, 16, 128], BF16), Res()) for i in range(2)])
            ptr = Ring([(self.ps(st, "bptr%d" % i, [128, 1024], BF16), Res()) for i in range(2)])
            pgu = [(self.ps(st, "bpgu%d" % i, [128, 512], F32), Res()) for i in range(2)]
            py = [(self.ps(st, "bpy%d" % i, [128, 512], F32), Res()) for i in range(4)]
            sg, rsg = self.sb(st, "bsg", [128, 512], F32), Res()
            hm, rhm = self.sb(st, "bhm", [128, 512], BF16), Res()
            hmT, rhmT = self.sb(st, "bhmT", [128, 4, 128], BF16), Res()
            ysb = Ring([(self.sb(st, "bys%d" % i, [128, D], F32), Res()) for i in range(2)])
            for b in range(self.NBLK):
                wi, rwi = win.next()
                wo, rwo = wout.next()
                gi, go = {}, {}
                for c in range(16):
                    S.dma("pool", wi[:, c, :], w_in, reads=[self.rIDX], writes=[rwi], grp=gi,
                          indirect={"in_offset": bass.IndirectOffsetOnAxis(ap=self.IDX[:, 0, b:b + 1], axis=0)},
                          element_offset=c * 128 * 1024)
                for c in range(4):
                    S.dma("pool", wo[:, c, :], w_out, reads=[self.rIDX], writes=[rwo], grp=go,
                          indirect={"in_offset": bass.IndirectOffsetOnAxis(ap=self.IDX[:, 1, b:b + 1], axis=0)},
                          element_offset=c * 128 * D)
                for sub in range(self.RB // 128):
                    row0 = b * self.RB + sub * 128
                    x_, rx = xs.next()
                    S.dma("sp", x_[:], self.XS[row0:row0 + 128, :], reads=[self.rXS], writes=[rx])
                    xt, rxt = xsT.next()
                    for half in range(2):
                        pt, rpt = ptr.next()
                        for j in range(8):
                            c = half * 8 + j
                            S.tr(pt[:, j * 128:(j + 1) * 128], x_[:, c * 128:(c + 1) * 128], self.ident_bf[:],
                                 reads=[rx, self.rC], writes=[rpt])
                        S.copy("dve" if half == 0 else "act", xt[:, half * 8:(half + 1) * 8, :],
                               pt[:].rearrange("p (j n) -> p j n", n=128), reads=[rpt], writes=[rxt])
                    for hf in range(2):
                        pg, rpg = pgu[hf]
                        for c in range(16):
                            S.mm(pg[:], xt[:, c, :], wi[:, c, hf * 512:(hf + 1) * 512], start=(c == 0), stop=(c == 15),
                                 reads=[rxt, rwi], writes=[rpg])
                    S.act(sg[:], pgu[0][0][:], AF.Silu, reads=[pgu[0][1]], writes=[rsg])
                    S.tt("dve", hm[:], sg[:], pgu[1][0][:], ALU.mult, reads=[rsg, pgu[1][1]], writes=[rhm])
                    pt, rpt = ptr.next()
                    for j in range(4):
                        S.tr(pt[:, j * 128:(j + 1) * 128], hm[:, j * 128:(j + 1) * 128], self.ident_bf[:],
                             reads=[rhm, self.rC], writes=[rpt])
                    S.copy("dve", hmT[:], pt[:, 0:512].rearrange("p (j n) -> p j n", n=128), reads=[rpt], writes=[rhmT])
                    y_, ry = ysb.next()
                    for cc in range(4):
                        pyy, rpy = py[cc]
                        for c in range(4):
                            S.mm(pyy[:], hmT[:, c, :], wo[:, c, cc * 512:(cc + 1) * 512], start=(c == 0), stop=(c == 3),
                                 reads=[rhmT, rwo], writes=[rpy])
                        S.copy("act" if cc % 2 == 0 else "dve", y_[:, cc * 512:(cc + 1) * 512], pyy[:], reads=[rpy], writes=[ry])
                    S.dma("sp", self.YS[row0:row0 + 128, :], y_[:], reads=[ry], writes=[self.rYS])
            S.barrier()

    def phase_combine(self, L, last):
        S = self.S
        S.barrier()
        with ExitStack() as st:
            gb, rgb, stats, rstats = self.alloc_ln(st, L, "2", 2)
            em = None if last else self.alloc_hT_emit(st)
            y0 = Ring([(self.sb(st, "cy0_%d" % i, [128, D], F32), Res()) for i in range(2)])
            y1 = Ring([(self.sb(st, "cy1_%d" % i, [128, D], F32), Res()) for i in range(2)])
            hr = Ring([(self.sb(st, "chr_%d" % i, [128, D], F32), Res()) for i in range(2)])
            a_t, ra = self.sb(st, "ca_t", [128, D], F32), Res()
            tmp, rtmp = self.sb(st, "ctmp", [128, D], F32), Res()
            ho = Ring([(self.sb(st, "cho_%d" % i, [128, D], F32), Res()) for i in range(2)])
            xb = Ring([(self.sb(st, "cxb_%d" % i, [128, D], BF16), Res()) for i in range(2)])
            dst = self.out if last else self.H
            for t in range(self.NTT):
                a0, r0 = y0.next()
                a1, r1 = y1.next()
                h_, rh = hr.next()
                S.dma("pool", a0[:], self.YS, reads=[self.rYS, self.rDST[t]], writes=[r0],
                      indirect={"in_offset": bass.IndirectOffsetOnAxis(ap=self.DST[:, t, 0:1], axis=0)})
                S.dma("pool", a1[:], self.YS, reads=[self.rYS, self.rDST[t]], writes=[r1],
                      indirect={"in_offset": bass.IndirectOffsetOnAxis(ap=self.DST[:, t, 1:2], axis=0)})
                S.dma("sp", h_[:], self.H1[t * 128:(t + 1) * 128, :], reads=[self.rH1[t]], writes=[rh])
                S.act(a_t[:], h_[:], AF.Copy, reads=[rh], writes=[ra], scale=ALPHA)
                S.stt(a_t[:], a0[:], self.GW[:, t, 0:1], a_t[:], ALU.mult, ALU.add, reads=[r0, self.rOH[t], ra], writes=[ra])
                S.stt(a_t[:], a1[:], self.GW[:, t, 1:2], a_t[:], ALU.mult, ALU.add, reads=[r1, self.rOH[t], ra], writes=[ra])
                o_, ro = ho.next()
                self.layer_norm_tile(a_t, ra, gb, rgb, 0, o_, ro, stats, rstats, tmp, rtmp)
                S.dma("sp", dst[t * 128:(t + 1) * 128, :], o_[:], reads=[ro],
                      writes=[self.rOUT if last else self.rH[t]])
                if not last:
                    xbt, rxb = xb.next()
                    S.copy("act", xbt[:], o_[:], reads=[ro], writes=[rxb])
                    self.emit_hT(em, xbt, rxb, t // NT, t % NT)
            S.barrier()


_CACHE = {}


def _get_prog(n_seq, layers, dbg=False):
    key = (n_seq, tuple(layers), dbg)
    if key not in _CACHE:
        pr = Prog(n_seq, list(layers), dbg)
        pr.build()
        _CACHE[key] = pr
    return _CACHE[key]


def kernel(**inputs):
    n_cores = 8
    x = np.asarray(inputs["x"], dtype=np.float32)
    B = x.shape[0]
    n_seq = B // n_cores
    pr = _get_prog(n_seq, list(range(DEPTH)))
    in_maps = []
    for c in range(n_cores):
        m = {"x": np.ascontiguousarray(x[c * n_seq:(c + 1) * n_seq].reshape(n_seq * SEQ, D))}
        for i in range(DEPTH):
            for nm in weight_names(i):
                m[nm] = np.ascontiguousarray(np.asarray(inputs[nm], dtype=np.float32))
        m.update(pr.consts)
        in_maps.append(m)
    res = run_bass_kernel_spmd(pr.nc, in_maps, core_ids=list(range(n_cores)))
    out = np.concatenate([np.asarray(r["y"]).reshape(n_seq, SEQ, D) for r in res.results], axis=0)
    return out.astype(np.float32)
```

```python
import math
import numpy as np
from contextlib import ExitStack
import concourse.bass as bass
import concourse.mybir as mybir
from concourse.bass_utils import run_bass_kernel_spmd

F32 = mybir.dt.float32
BF16 = mybir.dt.bfloat16
I32 = mybir.dt.int32
AF = mybir.ActivationFunctionType
ALU = mybir.AluOpType
AX = mybir.AxisListType

D = 2048
SEQ = 2048
NT = 16
DEPTH = 4
ALPHA = (2 * DEPTH) ** 0.25
EPS = 1e-5
THETA = 500000.0
NEG = -30000.0


class Res:
    __slots__ = ("w", "r", "nowaw")

    def __init__(self, nowaw=False):
        self.w = None
        self.r = {}
        self.nowaw = nowaw


class Sched:
    SEM_LIMIT = 30000

    def __init__(self, nc, stack, n_dma_sems=12):
        self.nc = nc
        self.stack = stack
        self.engs = {"pe": nc.tensor, "dve": nc.vector, "act": nc.scalar,
                     "pool": nc.gpsimd, "sp": nc.sync}
        self.sems = {}
        self.cur = {}
        self.epoch = {}
        for e in self.engs:
            self.epoch[e] = 0
            self._new_epoch(e)
        self.waited = {e: {} for e in self.engs}
        self.dma_pool = {}
        self.n_dma_sems = n_dma_sems
        self.dma_rr = {}
        self.n_instr = 0

    def _new_epoch(self, e):
        k = "%s_%d" % (e, self.epoch[e])
        self.epoch[e] += 1
        self.sems[k] = self.stack.enter_context(self.nc.semaphore(k))
        self.cur[e] = [k, 0]

    def _wait(self, e, tok):
        if tok is None:
            return
        k, v = tok
        if self.waited[e].get(k, 0) >= v:
            return
        if e == "pe" and k.startswith("pe_"):
            return
        self.engs[e].wait_ge(self.sems[k], v)
        self.waited[e][k] = v

    def _deps(self, e, reads, writes, skip_key=None):
        for t in reads:
            self._wait(e, t.w)
        for t in writes:
            if t.nowaw and t.w is not None and t.w[0].startswith("d"):
                pass
            elif not (skip_key is not None and t.w is not None and t.w[0] == skip_key):
                self._wait(e, t.w)
            for k, v in t.r.items():
                self._wait(e, (k, v))

    def _mark(self, tok, reads, writes):
        for t in reads:
            t.r[tok[0]] = tok[1]
        for t in writes:
            t.w = tok
            t.r = {}

    def op(self, e, ins_fn, reads=(), writes=()):
        self._deps(e, reads, writes)
        ins = ins_fn(self.engs[e])
        c = self.cur[e]
        c[1] += 1
        ins.then_inc(self.sems[c[0]], 1)
        tok = (c[0], c[1])
        self._mark(tok, reads, writes)
        if c[1] >= self.SEM_LIMIT:
            self._new_epoch(e)
        self.n_instr += 1
        return tok

    def dma(self, q, out, in_, reads=(), writes=(), indirect=None, grp=None, **kw):
        pool = self.dma_pool.setdefault(q, [])
        if grp is not None and grp.get("ent") is not None:
            ent = grp["ent"]
        elif len(pool) < (self.n_dma_sems if q != "pool" else 20):
            k = "d%s_%d" % (q, len(pool))
            self.sems[k] = self.stack.enter_context(self.nc.semaphore(k))
            pool.append([k, 0])
            ent = pool[-1]
        else:
            i = self.dma_rr.get(q, 0)
            ent = pool[i]
            self.dma_rr[q] = (i + 1) % len(pool)
            self._wait(q, (ent[0], ent[1]))
        skip_key = None
        if grp is not None:
            if grp.get("ent") is not None:
                skip_key = ent[0]
            grp["ent"] = ent
        self._deps(q, reads, writes, skip_key)
        if indirect is not None:
            ins = self.engs[q].indirect_dma_start(out, indirect.get("out_offset"), in_,
                                                  indirect.get("in_offset"), **kw)
        else:
            ins = self.engs[q].dma_start(out=out, in_=in_, **kw)
        ent[1] += 16
        ins.then_inc(self.sems[ent[0]], 16)
        tok = (ent[0], ent[1])
        self._mark(tok, reads, writes)
        self.n_instr += 1
        return tok

    def barrier(self):
        toks = []
        for e, c in self.cur.items():
            if c[1] > 0:
                toks.append((c[0], c[1]))
        for q, pool in self.dma_pool.items():
            for ent in pool:
                if ent[1] > 0:
                    toks.append((ent[0], ent[1]))
        for e in self.engs:
            for tok in toks:
                self._wait(e, tok)

    def mm(self, out, lhsT, rhs, start, stop, reads, writes):
        return self.op("pe", lambda e: e.matmul(out, lhsT, rhs, start=start, stop=stop), reads, writes)

    def tr(self, out, in_, ident, reads, writes):
        return self.op("pe", lambda e: e.transpose(out, in_, ident), reads, writes)

    def act(self, out, in_, func, reads, writes, **kw):
        return self.op("act", lambda e: e.activation(out, in_, func, **kw), reads, writes)

    def tt(self, eng, out, in0, in1, op, reads, writes):
        return self.op(eng, lambda e: e.tensor_tensor(out, in0, in1, op), reads, writes)

    def ts(self, eng, out, in0, s1, s2, op0, op1, reads, writes, **kw):
        if s2 is None:
            return self.op(eng, lambda e: e.tensor_scalar(out, in0, s1, None, op0, **kw), reads, writes)
        return self.op(eng, lambda e: e.tensor_scalar(out, in0, s1, s2, op0, op1, **kw), reads, writes)

    def stt(self, out, in0, scalar, in1, op0, op1, reads, writes):
        return self.op("dve", lambda e: e.scalar_tensor_tensor(out, in0, scalar, in1, op0, op1), reads, writes)

    def copy(self, eng, out, in_, reads, writes):
        if eng == "act":
            return self.op("act", lambda e: e.copy(out, in_), reads, writes)
        return self.op(eng, lambda e: e.tensor_copy(out, in_), reads, writes)


class Ring:
    def __init__(self, items):
        self.items = items
        self.i = 0

    def next(self):
        it = self.items[self.i]
        self.i = (self.i + 1) % len(self.items)
        return it


MASK_IDS = {}


def _build_masks():
    k = np.arange(128)[:, None]
    q = np.arange(128)[None, :]
    masks = []

    def add(name, m):
        MASK_IDS[name] = len(masks)
        masks.append(m.astype(np.float32))

    add("causal", k <= q)
    add("win4", k > q)
    add("d1_0", k <= q)
    add("d1_1", (128 + q - k) <= 128)
    for t in range(5):
        dd = 128 * t + q - k
        add("d4_%d" % t, (dd >= 0) & (dd % 4 == 0) & (dd <= 512))
    for t in range(16):
        dd = 128 * t + q - k
        add("d16_%d" % t, (dd >= 0) & (dd % 16 == 0) & (dd <= 2048))
    for i in range(16):
        add("cmp_%d" % i, (16 * k + 31 <= 128 * i + q) & (k < 127))
    return np.stack(masks)


def _rope_tables(kind):
    pos = np.arange(SEQ, dtype=np.float32)
    C = np.ones((128, SEQ), np.float32)
    Sn = np.zeros((128, SEQ), np.float32)
    P = np.zeros((128, 128), np.float32)
    if kind == "A":
        dim, bases = 64, (0, 64)
    else:
        dim, bases = 128, (0,)
    rot = dim // 4
    half = rot // 2
    inv = (THETA ** (-np.arange(0, rot, 2, dtype=np.float32) / rot)).astype(np.float32)
    ang = pos[None, :] * inv[:, None]
    for b in bases:
        for i in range(half):
            C[b + i] = np.cos(ang[i]); C[b + half + i] = np.cos(ang[i])
            Sn[b + i] = np.sin(ang[i]); Sn[b + half + i] = np.sin(ang[i])
            P[b + half + i, b + i] = -1.0
            P[b + i, b + half + i] = 1.0
    return C, Sn, P


def _host_consts():
    c = {}
    c["c_ident"] = np.eye(128, dtype=np.float32)
    c["c_masks"] = np.ascontiguousarray(_build_masks().transpose(1, 0, 2))
    CA, SA, PA = _rope_tables("A")
    CB, SB, PB = _rope_tables("B")
    c["c_ropeA"] = np.stack([CA, SA], axis=1)
    c["c_ropeB"] = np.stack([CB, SB], axis=1)
    c["c_perm"] = np.stack([PA, PB], axis=1)
    cend = np.arange(127) * 16 + 31
    kc = np.zeros((128, 2, 128), np.float32)
    kc[:, 0, :] = 1.0
    kc[:, 0, :127] = CB[:, cend]
    kc[:, 1, :127] = SB[:, cend]
    c["c_ropeKC"] = kc
    cs = np.arange(127) * 16
    ss = np.arange(32) * 64
    cover = ((cs[:, None] < ss[None, :] + 64) & (cs[:, None] + 31 >= ss[None, :])).astype(np.float32)
    cv = np.zeros((128, 32), np.float32)
    cv[:127] = cover
    c["c_cover"] = cv
    kk = np.arange(SEQ)
    c["c_expand"] = (kk[None, :] // 64 == np.arange(32)[:, None]).astype(np.float32)
    posq = np.arange(SEQ)
    cur = posq // 64
    blk = np.arange(32)
    forced = (blk[None, :] == 0) | (blk[None, :] == cur[:, None]) | (blk[None, :] == cur[:, None] - 1)
    valid = ss[None, :] <= posq[:, None]
    sa = np.where(valid, np.where(forced, 1e9, 0.0), -1e9).astype(np.float32)
    c["c_seladd"] = np.ascontiguousarray(sa.reshape(16, 128, 32).transpose(1, 0, 2))
    tri = (np.arange(128)[:, None] < np.arange(128)[None, :]).astype(np.float32)
    c["c_tri"] = tri
    c["c_iota"] = np.stack([np.arange(128, dtype=np.float32)] +
                           [np.full(128, 0.0, np.float32)], axis=1)
    return c


CONST_SHAPES = None


KINDS = ["A", "B", "C", "A"]
LAM_INIT = [0.8 - 0.6 * math.exp(-0.3 * i) for i in range(DEPTH)]


def weight_names(i):
    p = "l%d_" % i
    k = KINDS[i]
    if k == "A":
        m = {p + "a_w_in": [2048, 6144], p + "a_lam": [4, 64], p + "a_subln": [128], p + "a_w_out": [2048, 2048]}
    elif k == "B":
        m = {p + "b_w_in": [2048, 5168], p + "b_cmp_pos": [2, 32, 128], p + "b_cmp_w1": [2, 4096, 128],
             p + "b_cmp_w2": [2, 128, 128], p + "b_w_out": [2048, 2048]}
    else:
        m = {p + "c_w_in": [2048, 9216], p + "c_w_out": [1024, 2048]}
    m.update({p + "ln": [4, 2048], p + "router_w": [2048, 36], p + "router_b": [36],
              p + "moe_w_in": [32, 2048, 1024], p + "moe_w_out": [32, 512, 2048]})
    return m


class AttnPipe:
    def __init__(self, prog, st, n_s=2, n_p=5, depth=3, extra_banks=()):
        self.prog = prog
        self.S = prog.S
        self.sring = Ring([(prog.ps(st, "Sps%d" % i, [128, 512], F32), Res()) for i in range(n_s)] + list(extra_banks))
        self.pring = Ring([(prog.sb(st, "Pt%d" % i, [128, 512], BF16), Res()) for i in range(n_p)])
        self.q = []
        self.depth = depth

    def push(self, items, scale, after=None):
        S = self.S
        sp, rsp = self.sring.next()
        for j, it in enumerate(items):
            ex = it.get("extra")
            S.mm(sp[:, j * 128:(j + 1) * 128], it["k"], it["q"], start=True, stop=(ex is None),
                 reads=list(it["kr"]) + list(it["qr"]), writes=[rsp])
            if ex is not None:
                S.mm(sp[:, j * 128:(j + 1) * 128], ex[0], ex[1], start=False, stop=True, reads=list(ex[2]), writes=[rsp])
        P, rP = self.pring.next()
        w = len(items) * 128
        S.act(P[:, 0:w], sp[:, 0:w], AF.Exp, reads=[rsp], writes=[rP], scale=scale)
        for j, it in enumerate(items):
            if it.get("mask") is not None:
                S.tt("dve", P[:, j * 128:(j + 1) * 128], P[:, j * 128:(j + 1) * 128],
                     self.prog.masks[:, it["mask"], :], ALU.mult, reads=[rP, self.prog.rC], writes=[rP])
        self.q.append((items, P, rP, after))
        while len(self.q) > self.depth:
            self._pv()

    def _pv(self):
        S = self.S
        items, P, rP, after = self.q.pop(0)
        for j, it in enumerate(items):
            S.mm(it["O"], P[:, j * 128:(j + 1) * 128], it["v"], start=it["start"], stop=it["stop"],
                 reads=[rP] + list(it["vr"]), writes=[it["rO"]])
        if after is not None:
            after()

    def run(self, items, scale, after=None):
        n = len(items)
        for i in range(0, n, 4):
            self.push(items[i:i + 4], scale, after if i + 4 >= n else None)

    def flush(self):
        while self.q:
            self._pv()


class Prog:
    def __init__(self, n_seq, layers, dbg=False):
        self.n_seq = n_seq
        self.layers = layers
        self.NTT = n_seq * NT
        self.NTOK = n_seq * SEQ
        self.RB = 256
        self.NBLK = self.NTOK * 2 // self.RB + 32
        self.dbg = dbg
        nc = bass.Bass("TRN2", target_bir_lowering=False)
        self.nc = nc
        self.din = {}
        self.din["x"] = nc.dram_tensor("x", [self.NTOK, D], F32, kind="ExternalInput").ap()
        for i in layers:
            for nm, shp in weight_names(i).items():
                self.din[nm] = nc.dram_tensor(nm, shp, F32, kind="ExternalInput").ap()
        self.consts = _host_consts()
        self.consts["c_blk"] = np.ascontiguousarray(
            np.broadcast_to(float(self.RB) * np.arange(self.NBLK, dtype=np.float32)[None, :], (128, self.NBLK)))
        for nm, arr in self.consts.items():
            self.din[nm] = nc.dram_tensor(nm, list(arr.shape), F32, kind="ExternalInput").ap()
        self.out = nc.dram_tensor("y", [self.NTOK, D], F32, kind="ExternalOutput").ap()
        sk = "ExternalOutput" if dbg else "Internal"
        self.H = nc.dram_tensor("sH", [self.NTOK, D], F32, kind=sk).ap()
        self.H1 = nc.dram_tensor("sH1", [self.NTOK, D], F32, kind=sk).ap()
        self.HT = nc.dram_tensor("sHT", [n_seq, 128, 16, SEQ], BF16, kind=sk).ap()
        self.OT = nc.dram_tensor("sOT", [n_seq, 128, 16, SEQ], BF16, kind=sk).ap()
        self.XB = nc.dram_tensor("sXB", [self.NTOK, D], BF16, kind=sk).ap()
        self.XS = nc.dram_tensor("sXS", [self.NBLK * self.RB, D], BF16, kind=sk).ap()
        self.YS = nc.dram_tensor("sYS", [self.NBLK * self.RB, D], F32, kind=sk).ap()
        self.rH = [Res() for _ in range(self.NTT)]
        self.rH1 = [Res() for _ in range(self.NTT)]
        self.rHT = [Res(True) for _ in range(n_seq)]
        self.rOT = [Res(True) for _ in range(n_seq)]
        self.rXB = [Res() for _ in range(self.NTT)]
        self.rXS = Res(True)
        self.rYS = Res(True)
        self.rOUT = Res(True)

    def sb(self, st, name, shape, dt):
        self.uid = getattr(self, "uid", 0) + 1
        return st.enter_context(self.nc.sbuf_tensor("%s_u%d" % (name, self.uid), shape, dt))

    def ps(self, st, name, shape, dt):
        self.uid = getattr(self, "uid", 0) + 1
        return st.enter_context(self.nc.psum_tensor("%s_u%d" % (name, self.uid), shape, dt))

    def build(self):
        nc = self.nc
        with ExitStack() as st:
            S = Sched(nc, st)
            self.S = S
            self.ident_bf = self.sb(st, "ident_bf", [128, 128], BF16)
            self.ident_f = self.sb(st, "ident_f", [128, 128], F32)
            NM = self.consts["c_masks"].shape[1]
            self.masks = self.sb(st, "masks", [128, NM, 128], BF16)
            self.tri = self.sb(st, "tri", [128, 128], BF16)
            self.ones_bf = self.sb(st, "ones_bf", [128, 128], BF16)
            self.iota = self.sb(st, "iota", [128, 2], F32)
            self.OH = self.sb(st, "OH", [128, self.NTT, 2, 32], BF16)
            self.GW = self.sb(st, "GW", [128, self.NTT, 2], F32)
            self.DST = self.sb(st, "DST", [128, self.NTT, 2], I32)
            self.IDX = self.sb(st, "IDX", [128, 2, self.NBLK], I32)
            self.rIDX = Res()
            self.rC = Res()
            self.rOH = [Res() for _ in range(self.NTT)]
            self.rDST = [Res() for _ in range(self.NTT)]
            S.dma("pool", self.ident_bf[:], self.din["c_ident"], writes=[self.rC])
            S.dma("sp", self.ident_f[:], self.din["c_ident"], writes=[self.rC])
            S.dma("pool", self.masks[:], self.din["c_masks"], writes=[self.rC])
            S.dma("pool", self.tri[:], self.din["c_tri"], writes=[self.rC])
            S.dma("sp", self.iota[:], self.din["c_iota"], writes=[self.rC])
            S.op("dve", lambda e: e.memset(self.ones_bf[:], 1.0), writes=[self.rC])
            self.nhalf = self.sb(st, "nhalf", [128, 1], F32)
            S.op("dve", lambda e: e.memset(self.nhalf[:], -0.5), writes=[self.rC])
            self.eps_t = self.sb(st, "eps_t", [128, 1], F32)
            S.op("dve", lambda e: e.memset(self.eps_t[:], EPS), writes=[self.rC])
            S.barrier()

            first = True
            for li, L in enumerate(self.layers):
                kind = KINDS[L]
                last = (li == len(self.layers) - 1)
                for s in range(self.n_seq):
                    if first:
                        self.phase_x_to_ht(s)
                    if kind == "A":
                        self.mixer_A(L, s)
                    elif kind == "B":
                        self.mixer_B(L, s)
                    else:
                        self.mixer_C(L, s)
                    self.phase_outproj_ln1(L, s, src_is_x=first)
                first = False
                self.phase_dispatch(L)
                self.phase_blocks(L)
                self.phase_combine(L, last)
            S.barrier()
        return nc

    def emit_hT(self, st_res, xb, rxb, s, t):
        S = self.S
        ptA, rptA, ptB, rptB, ht_ring = st_res
        ht, rht = ht_ring.next()
        for half, (pt, rpt) in enumerate(((ptA, rptA), (ptB, rptB))):
            for j in range(8):
                c = half * 8 + j
                S.tr(pt[:, j * 128:(j + 1) * 128], xb[:, c * 128:(c + 1) * 128], self.ident_bf[:],
                     reads=[rxb, self.rC], writes=[rpt])
            S.copy("dve" if half == 0 else "act", ht[:, half * 8:(half + 1) * 8, :],
                   pt[:].rearrange("p (j n) -> p j n", n=128), reads=[rpt], writes=[rht])
        S.dma("sp", self.HT[s, :, :, t * 128:(t + 1) * 128], ht[:], reads=[rht], writes=[self.rHT[s]])

    def alloc_hT_emit(self, st):
        ptA = self.ps(st, "ptA", [128, 1024], BF16)
        ptB = self.ps(st, "ptB", [128, 1024], BF16)
        hts = [(self.sb(st, "htile%d" % i, [128, 16, 128], BF16), Res()) for i in range(2)]
        return (ptA, Res(), ptB, Res(), Ring(hts))

    def phase_x_to_ht(self, s):
        S = self.S
        S.barrier()
        with ExitStack() as st:
            em = self.alloc_hT_emit(st)
            xs = Ring([(self.sb(st, "p0xs%d" % i, [128, D], F32), Res()) for i in range(2)])
            xb = Ring([(self.sb(st, "p0xb%d" % i, [128, D], BF16), Res()) for i in range(2)])
            for t in range(NT):
                g = s * NT + t
                x_, rx = xs.next()
                b_, rb = xb.next()
                S.dma("sp", x_[:], self.din["x"][g * 128:(g + 1) * 128, :], writes=[rx])
                S.copy("act", b_[:], x_[:], reads=[rx], writes=[rb])
                self.emit_hT(em, b_, rb, s, t)
            S.barrier()

    def layer_norm_tile(self, a, ra, gb, rgb, gi, outt, rout, stats, rstats, tmp, rtmp):
        S = self.S
        st6, mv, rstd = stats
        for j in range(4):
            S.op("dve", lambda e, j=j: e.bn_stats(st6[:, j, :], a[:, j * 512:(j + 1) * 512]),
                 reads=[ra], writes=[rstats])
        S.op("dve", lambda e: e.bn_aggr(mv[:], st6[:].rearrange("p a b -> p (a b)")), reads=[rstats], writes=[rstats])
        S.act(rstd[:], mv[:, 1:2], AF.Sqrt, reads=[rstats], writes=[rstats], bias=self.eps_t[:, 0:1], scale=1.0)
        S.op("dve", lambda e: e.reciprocal(rstd[:], rstd[:]), reads=[rstats], writes=[rstats])
        S.ts("dve", tmp[:], a[:], mv[:, 0:1], rstd[:, 0:1], ALU.subtract, ALU.mult, reads=[ra, rstats], writes=[rtmp])
        S.tt("pool", tmp[:], tmp[:], gb[:, gi, :], ALU.mult, reads=[rtmp, rgb], writes=[rtmp])
        S.tt("pool", outt[:], tmp[:], gb[:, gi + 1, :], ALU.add, reads=[rtmp, rgb], writes=[rout])

    def alloc_ln(self, st, L, tag, row0):
        S = self.S
        gb = self.sb(st, "lngb" + tag, [128, 2, D], F32)
        rgb = Res()
        ln = self.din["l%d_ln" % L]
        src = ln[row0:row0 + 2, :].partition_broadcast(128)
        S.dma("sp", gb[:], src, writes=[rgb])
        stats = (self.sb(st, "lnst6" + tag, [128, 4, 6], F32), self.sb(st, "lnmv" + tag, [128, 2], F32),
                 self.sb(st, "lnrstd" + tag, [128, 1], F32))
        return gb, rgb, stats, Res()

    def alloc_attn(self, st):
        sb = [(self.ps(st, "Sps%d" % i, [128, 512], F32), Res()) for i in range(2)]
        pt = [(self.sb(st, "Pt%d" % i, [128, 512], BF16), Res()) for i in range(3)]
        return Ring(sb), Ring(pt)

    def attn(self, actx, q_ap, q_reads, items, O, rO, ncols, scale):
        S = self.S
        sring, pring = actx
        groups = [items[i:i + 4] for i in range(0, len(items), 4)]
        n_items = len(items)
        done = [0]

        def pv(grp, P, rP):
            for j, it in enumerate(grp):
                S.mm(O[:, 0:ncols], P[:, j * 128:(j + 1) * 128], it[2], start=(done[0] == 0),
                     stop=(done[0] == n_items - 1), reads=[rP] + list(it[3]), writes=[rO])
                done[0] += 1

        prev = None
        for grp in groups:
            sp, rsp = sring.next()
            for j, it in enumerate(grp):
                ex = it[5]
                S.mm(sp[:, j * 128:(j + 1) * 128], it[0], q_ap, start=True, stop=(ex is None),
                     reads=list(it[1]) + list(q_reads), writes=[rsp])
                if ex is not None:
                    S.mm(sp[:, j * 128:(j + 1) * 128], ex[0], ex[1], start=False, stop=True,
                         reads=list(ex[2]), writes=[rsp])
            P, rP = pring.next()
            w = len(grp) * 128
            S.act(P[:, 0:w], sp[:, 0:w], AF.Exp, reads=[rsp], writes=[rP], scale=scale)
            for j, it in enumerate(grp):
                if it[4] is not None:
                    S.tt("dve", P[:, j * 128:(j + 1) * 128], P[:, j * 128:(j + 1) * 128],
                         self.masks[:, it[4], :], ALU.mult, reads=[rP, self.rC], writes=[rP])
            if prev is not None:
                pv(*prev)
            prev = (grp, P, rP)
        pv(*prev)

    def load_w(self, slot, rslot, w_ap, col0, ncols, kc=16, dst_col=0):
        src = w_ap.rearrange("(c p) n -> p c n", p=128)[:, :, col0:col0 + ncols]
        self.S.dma("pool", slot[:, 0:kc, dst_col:dst_col + ncols], src, writes=[rslot])

    def proj_fm(self, pp, rpp, slot, rslot, c0, hT, rhT, tok0, ntok=512, kc=16):
        for c in range(kc):
            self.S.mm(pp[:, 0:ntok], slot[:, c, c0:c0 + 128], hT[:, c, tok0:tok0 + ntok],
                      start=(c == 0), stop=(c == kc - 1), reads=[rslot, rhT], writes=[rpp])

    def proj_tm(self, pp, rpp, slot, rslot, c0, ncols, hT, rhT, t, kc=16):
        for c in range(kc):
            self.S.mm(pp[:, 0:ncols], hT[:, c, t * 128:(t + 1) * 128], slot[:, c, c0:c0 + ncols],
                      start=(c == 0), stop=(c == kc - 1), reads=[rslot, rhT], writes=[rpp])

    def defer(self, fn, key="rope"):
        d = self.__dict__.setdefault("_deferred", {})
        prev = d.get(key)
        d[key] = fn
        if prev is not None:
            prev()

    def run_deferred(self, key="rope"):
        d = self.__dict__.setdefault("_deferred", {})
        prev = d.get(key)
        d[key] = None
        if prev is not None:
            prev()

    def rope_fm2(self, dst, rdst, pring, perm_ap, rope, rrope, tok0, tring, ntok=512):
        def go():
            pr, rpr = pring.next()
            tmp, rtmp = tring.next()
            self.rope_fm(dst, rdst, pr, rpr, perm_ap, rope, rrope, tok0, tmp, rtmp, ntok)
        self.defer(go)

    def rope_fm(self, dst, rdst, pr, rpr, perm_ap, rope, rrope, tok0, tmp, rtmp, ntok=512):
        S = self.S
        S.mm(pr[:, 0:ntok], perm_ap, dst, start=True, stop=True, reads=[rdst, rrope], writes=[rpr])
        S.tt("pool", tmp[:, 0:ntok], dst, rope[:, 0, tok0:tok0 + ntok], ALU.mult, reads=[rdst, rrope], writes=[rtmp])
        S.tt("dve", dst, pr[:, 0:ntok], rope[:, 1, tok0:tok0 + ntok], ALU.mult, reads=[rpr, rrope], writes=[rdst])
        S.tt("dve", dst, dst, tmp[:, 0:ntok], ALU.add, reads=[rtmp], writes=[rdst])

    def load_hT(self, st, s):
        hT = self.sb(st, "hT", [128, 16, SEQ], BF16)
        rhT = Res()
        g_ = {}
        for c in range(0, 16, 4):
            self.S.dma("sp", hT[:, c:c + 4, :], self.HT[s, :, c:c + 4, :], reads=[self.rHT[s]], writes=[rhT], grp=g_)
        return hT, rhT

    def load_rope(self, st, kind):
        S = self.S
        rope = self.sb(st, "rope", [128, 2, SEQ], F32)
        perm = self.sb(st, "perm", [128, 128], BF16)
        rr = Res()
        S.dma("sp", rope[:], self.din["c_rope" + ("A" if kind == "A" else "B")], writes=[rr])
        S.dma("pool", perm[:], self.din["c_perm"][:, 0 if kind == "A" else 1, :], writes=[rr])
        return rope, perm, rr

    def mixer_A_old(self, L, s):
        S = self.S
        S.barrier()
        p = "l%d_" % L
        w_in = self.din[p + "a_w_in"]
        with ExitStack() as st:
            hT, rhT = self.load_hT(st, s)
            rope, perm, rr = self.load_rope(st, "A")
            actx = self.alloc_attn(st)
            slots = Ring([(self.sb(st, "wslot%d" % i, [128, 16, 256], BF16), Res()) for i in range(5)])
            pproj = Ring([(self.ps(st, "pproj%d" % i, [128, 512], F32), Res()) for i in range(2)])
            pr, rpr = self.ps(st, "prope", [128, 512], F32), Res()
            O1, rO1 = self.ps(st, "O1", [128, 512], F32), Res()
            O2, rO2 = self.ps(st, "O2", [128, 512], F32), Res()
            ptr, rptr = self.ps(st, "ptrA", [128, 1024], BF16), Res()
            qT = self.sb(st, "qT", [128, 2, SEQ], BF16)
            kT = self.sb(st, "kT", [128, 2, SEQ], BF16)
            rqk = [[Res() for _ in range(2)] for _ in range(2)]
            V = self.sb(st, "Vaug", [128, NT, 2, 132], BF16)
            rV = Res()
            tmp, rtmp = self.sb(st, "ropetmp", [128, 512], F32), Res()
            oT = self.sb(st, "oTg", [128, 2, SEQ], BF16)
            roT = Res()
            lam = self.sb(st, "lam", [128, 256], F32)
            lsc = self.sb(st, "lsc", [128, 8], F32)
            gv = self.sb(st, "gv", [128, 128], F32)
            rl = Res()
            S.dma("sp", lam[:], self.din[p + "a_lam"].rearrange("a b -> (a b)").partition_broadcast(128), writes=[rl])
            S.dma("sp", gv[:], self.din[p + "a_subln"].partition_broadcast(128), writes=[rl])
            S.tt("dve", lam[:, 0:64], lam[:, 0:64], lam[:, 64:128], ALU.mult, reads=[rl], writes=[rl])
            S.tt("dve", lam[:, 128:192], lam[:, 128:192], lam[:, 192:256], ALU.mult, reads=[rl], writes=[rl])
            S.op("dve", lambda e: e.reduce_sum(lsc[:, 0:1], lam[:, 0:64], axis=AX.X), reads=[rl], writes=[rl])
            S.op("dve", lambda e: e.reduce_sum(lsc[:, 1:2], lam[:, 128:192], axis=AX.X), reads=[rl], writes=[rl])
            S.act(lsc[:, 2:4], lsc[:, 0:2], AF.Exp, reads=[rl], writes=[rl])
            S.tt("dve", lsc[:, 4:5], lsc[:, 3:4], lsc[:, 2:3], ALU.subtract, reads=[rl], writes=[rl])
            S.ts("dve", lsc[:, 4:5], lsc[:, 4:5], -LAM_INIT[L], None, ALU.add, None, reads=[rl], writes=[rl])
            S.ts("dve", gv[:], gv[:], 1.0 - LAM_INIT[L], None, ALU.mult, None, reads=[rl], writes=[rl])
            S.op("dve", lambda e: e.memset(V[:, :, :, 128:129], 1.0), writes=[rV])
            osb = Ring([(self.sb(st, "osb%d" % i, [128, 128], F32), Res()) for i in range(2)])
            obf = Ring([(self.sb(st, "obf%d" % i, [128, 128], BF16), Res()) for i in range(2)])
            sm = Ring([(self.sb(st, "smA%d" % i, [128, 8], F32), Res()) for i in range(2)])
            junk = self.sb(st, "junkA", [128, 128], F32)
            for hg in range(8):
                wq, rwq = slots.next(); wk, rwk = slots.next(); wv, rwv = slots.next()
                self.load_w(wq, rwq, w_in, hg * 256, 256)
                self.load_w(wk, rwk, w_in, 2048 + hg * 256, 256)
                self.load_w(wv, rwv, w_in, 4096 + hg * 256, 256)
                for hh in range(2):
                    for (dst, w_, rw_, which) in ((qT, wq, rwq, 0), (kT, wk, rwk, 1)):
                        for tc4 in range(4):
                            pp, rpp = pproj.next()
                            self.proj_fm(pp, rpp, w_, rw_, hh * 128, hT, rhT, tc4 * 512)
                            d_ = dst[:, hh, tc4 * 512:(tc4 + 1) * 512]
                            S.copy("act", d_, pp[:], reads=[rpp], writes=[rqk[which][hh]])
                            self.rope_fm(d_, rqk[which][hh], pr, rpr, perm[:], rope, rr, tc4 * 512, tmp, rtmp)
                for t in range(NT):
                    pp, rpp = pproj.next()
                    self.proj_tm(pp, rpp, wv, rwv, 0, 256, hT, rhT, t)
                    S.copy("act", V[:, t, :, 0:128], pp[:, 0:256].rearrange("p (h n) -> p h n", n=128),
                           reads=[rpp], writes=[rV])
                for hh in range(2):
                    for i in range(NT):
                        for m, (O, rO) in enumerate(((O1, rO1), (O2, rO2))):
                            items = []
                            for j in range(i + 1):
                                items.append((kT[m * 64:(m + 1) * 64, hh, j * 128:(j + 1) * 128], [rqk[1][hh]],
                                              V[:, j, hh, 0:129], [rV],
                                              MASK_IDS["causal"] if j == i else None, None))
                            self.attn(actx, qT[m * 64:(m + 1) * 64, hh, i * 128:(i + 1) * 128], [rqk[0][hh]],
                                      items, O, rO, 129, 0.125)
                        sm_, rsm = sm.next()
                        o_, ro = osb.next()
                        ob, rob = obf.next()
                        S.op("dve", lambda e: e.reciprocal(sm_[:, 0:1], O1[:, 128:129]), reads=[rO1], writes=[rsm])
                        S.op("dve", lambda e: e.reciprocal(sm_[:, 1:2], O2[:, 128:129]), reads=[rO2], writes=[rsm])
                        S.tt("dve", sm_[:, 1:2], sm_[:, 1:2], lsc[:, 4:5], ALU.mult, reads=[rsm, rl], writes=[rsm])
                        S.ts("dve", o_[:], O1[:, 0:128], sm_[:, 0:1], None, ALU.mult, None, reads=[rO1, rsm], writes=[ro])
                        S.stt(o_[:], O2[:, 0:128], sm_[:, 1:2], o_[:], ALU.mult, ALU.add, reads=[rO2, rsm, ro], writes=[ro])
                        S.op("dve", lambda e: e.memset(sm_[:, 2:3], 0.0), writes=[rsm])
                        S.act(junk[:], o_[:], AF.Square, reads=[ro], writes=[rsm], accum_out=sm_[:, 2:3])
                        S.act(sm_[:, 3:4], sm_[:, 2:3], AF.Sqrt, reads=[rsm], writes=[rsm], bias=self.eps_t[:, 0:1],
                              scale=1.0 / 128.0)
                        S.op("dve", lambda e: e.reciprocal(sm_[:, 3:4], sm_[:, 3:4]), reads=[rsm], writes=[rsm])
                        S.stt(ob[:], o_[:], sm_[:, 3:4], gv[:], ALU.mult, ALU.mult, reads=[ro, rsm, rl], writes=[rob])
                        S.tr(ptr[:, 0:128], ob[:], self.ident_bf[:], reads=[rob, self.rC], writes=[rptr])
                        S.copy("dve", oT[:, hh, i * 128:(i + 1) * 128], ptr[:, 0:128], reads=[rptr], writes=[roT])
                S.dma("sp", self.OT[s, :, hg * 2:hg * 2 + 2, :], oT[:], reads=[roT], writes=[self.rOT[s]])
            S.barrier()

    def mixer_C_old(self, L, s):
        S = self.S
        S.barrier()
        p = "l%d_" % L
        w_in = self.din[p + "c_w_in"]
        with ExitStack() as st:
            hT, rhT = self.load_hT(st, s)
            rope, perm, rr = self.load_rope(st, "B")
            actx = self.alloc_attn(st)
            slots = Ring([(self.sb(st, "wslotC%d" % i, [128, 16, 128], BF16), Res()) for i in range(12)])
            pproj = Ring([(self.ps(st, "pproj%d" % i, [128, 512], F32), Res()) for i in range(2)])
            pr, rpr = self.ps(st, "prope", [128, 512], F32), Res()
            O, rO = self.ps(st, "OC_", [128, 512], F32), Res()
            ptr, rptr = self.ps(st, "ptrC", [128, 1024], BF16), Res()
            qT = self.sb(st, "qTC", [128, 3, SEQ], BF16)
            kT = self.sb(st, "kTC", [128, 3, SEQ], BF16)
            rq = [Res() for _ in range(3)]
            rk = [Res() for _ in range(3)]
            V = self.sb(st, "VaugC", [128, 3, NT, 132], BF16)
            rV = [Res() for _ in range(3)]
            tmp, rtmp = self.sb(st, "ropetmp", [128, 512], F32), Res()
            oT, roT = self.sb(st, "oTC", [128, SEQ], BF16), Res()
            S.op("dve", lambda e: e.memset(V[:, :, :, 128:129], 1.0), writes=rV)
            ob = Ring([(self.sb(st, "obC%d" % i, [128, 128], BF16), Res()) for i in range(2)])
            sm = Ring([(self.sb(st, "smC%d" % i, [128, 2], F32), Res()) for i in range(2)])
            for h in range(8):
                for g in range(3):
                    for which, (dst, rd) in enumerate(((qT, rq), (kT, rk))):
                        w_, rw_ = slots.next()
                        self.load_w(w_, rw_, w_in, which * 3072 + g * 1024 + h * 128, 128)
                        for tc4 in range(4):
                            pp, rpp = pproj.next()
                            self.proj_fm(pp, rpp, w_, rw_, 0, hT, rhT, tc4 * 512)
                            d_ = dst[:, g, tc4 * 512:(tc4 + 1) * 512]
                            S.copy("act", d_, pp[:], reads=[rpp], writes=[rd[g]])
                            self.rope_fm(d_, rd[g], pr, rpr, perm[:], rope, rr, tc4 * 512, tmp, rtmp)
                    w_, rw_ = slots.next()
                    self.load_w(w_, rw_, w_in, 2 * 3072 + g * 1024 + h * 128, 128)
                    for t4 in range(0, NT, 4):
                        pp, rpp = pproj.next()
                        for tt_ in range(4):
                            t = t4 + tt_
                            for c in range(16):
                                S.mm(pp[:, tt_ * 128:(tt_ + 1) * 128], hT[:, c, t * 128:(t + 1) * 128], w_[:, c, 0:128],
                                     start=(c == 0), stop=(c == 15), reads=[rw_, rhT], writes=[rpp])
                        S.copy("act", V[:, g, t4:t4 + 4, 0:128], pp[:].rearrange("p (t n) -> p t n", n=128),
                               reads=[rpp], writes=[rV[g]])
                for i in range(NT):
                    items = []
                    for g, (nm, maxd) in enumerate((("d1_", 1), ("d4_", 4), ("d16_", 15))):
                        for j in range(max(0, i - maxd), i + 1):
                            items.append((kT[:, g, j * 128:(j + 1) * 128], [rk[g]], V[:, g, j, 0:129], [rV[g]],
                                          MASK_IDS[nm + str(i - j)], None))
                    first = True
                    self.attn_multi(actx, [(qT[:, g, i * 128:(i + 1) * 128], [rq[g]]) for g in range(3)],
                                    items, [len([1 for j in range(max(0, i - md), i + 1)]) for md in (1, 4, 15)],
                                    O, rO, 129, 128 ** -0.5)
                    sm_, rsm = sm.next()
                    ob_, rob = ob.next()
                    S.op("dve", lambda e: e.reciprocal(sm_[:, 0:1], O[:, 128:129]), reads=[rO], writes=[rsm])
                    S.ts("dve", ob_[:], O[:, 0:128], sm_[:, 0:1], None, ALU.mult, None, reads=[rO, rsm], writes=[rob])
                    S.tr(ptr[:, 0:128], ob_[:], self.ident_bf[:], reads=[rob, self.rC], writes=[rptr])
                    S.copy("dve", oT[:, i * 128:(i + 1) * 128], ptr[:, 0:128], reads=[rptr], writes=[roT])
                S.dma("sp", self.OT[s, :, h, :], oT[:], reads=[roT], writes=[self.rOT[s]])
            S.barrier()

    def attn_multi(self, actx, qs, items, counts, O, rO, ncols, scale):
        S = self.S
        sring, pring = actx
        qidx = []
        for gi, cnt in enumerate(counts):
            qidx += [gi] * cnt
        n_items = len(items)
        groups = [list(range(i, min(i + 4, n_items))) for i in range(0, n_items, 4)]
        done = [0]

        def pv(grp, P, rP):
            for j, ii in enumerate(grp):
                it = items[ii]
                S.mm(O[:, 0:ncols], P[:, j * 128:(j + 1) * 128], it[2], start=(done[0] == 0),
                     stop=(done[0] == n_items - 1), reads=[rP] + list(it[3]), writes=[rO])
                done[0] += 1

        prev = None
        for grp in groups:
            sp, rsp = sring.next()
            for j, ii in enumerate(grp):
                it = items[ii]
                q_ap, q_reads = qs[qidx[ii]]
                S.mm(sp[:, j * 128:(j + 1) * 128], it[0], q_ap, start=True, stop=True,
                     reads=list(it[1]) + list(q_reads), writes=[rsp])
            P, rP = pring.next()
            w = len(grp) * 128
            S.act(P[:, 0:w], sp[:, 0:w], AF.Exp, reads=[rsp], writes=[rP], scale=scale)
            for j, ii in enumerate(grp):
                it = items[ii]
                if it[4] is not None:
                    S.tt("dve", P[:, j * 128:(j + 1) * 128], P[:, j * 128:(j + 1) * 128],
                         self.masks[:, it[4], :], ALU.mult, reads=[rP, self.rC], writes=[rP])
            if prev is not None:
                pv(*prev)
            prev = (grp, P, rP)
        pv(*prev)

    def mixer_B(self, L, s):
        S = self.S
        S.barrier()
        p = "l%d_" % L
        w_in = self.din[p + "b_w_in"]
        sc = 128 ** -0.5
        with ExitStack() as st:
            hT, rhT = self.load_hT(st, s)
            rope, perm, rr = self.load_rope(st, "B")
            big = Ring([(self.sb(st, "wbig%d" % i, [128, 16, 512], BF16), Res()) for i in range(1)])
            small = Ring([(self.sb(st, "wsm%d" % i, [128, 16, 128], BF16), Res()) for i in range(4)])
            pproj = Ring([(self.ps(st, "pproj%d" % i, [128, 512], F32), Res()) for i in range(3)])
            tring = Ring([(self.sb(st, "ropetmpB%d" % i, [128, 512], F32), Res()) for i in range(2)])
            pipe = AttnPipe(self, st, extra_banks=pproj.items)
            Oa, rOa = self.ps(st, "Oa", [128, 512], F32), Res()
            Ob, rOb = self.ps(st, "Ob", [128, 512], F32), Res()
            ptr = self.ps(st, "ptrB", [128, 1024], BF16)
            rptr = [Res(), Res(), Res()]
            qT = self.sb(st, "qTB", [128, 4, SEQ], BF16)
            rq = [[Res() for _ in range(4)] for _ in range(4)]
            kv = self.sb(st, "kvT", [128, 4, SEQ], BF16)
            rkv = [[Res() for _ in range(4)] for _ in range(4)]
            Vs = self.sb(st, "VsB", [128, 2, NT, 132], BF16)
            rVs = [Res() for _ in range(2)]
            oTr = Ring([(self.sb(st, "oTB%d" % i, [128, 4, 128], BF16), Res()) for i in range(2)])
            gate = self.sb(st, "gateB", [128, NT, 48], F32)
            rgate = Res()
            ropekc = self.sb(st, "ropekc", [128, 2, 128], F32)
            cover = self.sb(st, "coverB", [128, 32], BF16)
            expand = self.sb(st, "expandB", [32, SEQ], BF16)
            seladd = self.sb(st, "seladdB", [128, 16, 32], F32)
            rcst = Res()
            S.dma("sp", ropekc[:], self.din["c_ropeKC"], writes=[rcst])
            S.dma("pool", cover[:], self.din["c_cover"], writes=[rcst])
            S.dma("pool", expand[:], self.din["c_expand"], writes=[rcst])
            S.dma("sp", seladd[:], self.din["c_seladd"], writes=[rcst])
            S.op("dve", lambda e: e.memset(Vs[:, :, :, 128:129], 1.0), writes=rVs)
            w1 = self.sb(st, "cw1", [128, 32, 128], BF16)
            rw1 = Res()
            w2 = self.sb(st, "cw2", [128, 2, 128], BF16)
            posf = self.sb(st, "cposf", [128, 2, 32], F32)
            posb = self.sb(st, "cposb", [128, 2, 32], BF16)
            cb = self.sb(st, "cbias", [128, 2], F32)
            rcw = Res()
            for j in range(2):
                S.dma("pool", w2[:, j, :], self.din[p + "b_cmp_w2"][j], writes=[rcw])
                S.dma("sp", posf[:, j, :], self.din[p + "b_cmp_pos"][j].rearrange("l d -> d l"), writes=[rcw],
                      allow_slow_non_contiguous=True)
            S.copy("dve", posb[:], posf[:], reads=[rcw], writes=[rcw])
            wg, rwg = small.next()
            self.load_w(wg, rwg, w_in, 5120, 48)
            for t in range(NT):
                pp, rpp = pproj.next()
                self.proj_tm(pp, rpp, wg, rwg, 0, 48, hT, rhT, t)
                S.act(gate[:, t, :], pp[:, 0:48], AF.Sigmoid, reads=[rpp], writes=[rgate])
            kcT = self.sb(st, "kcT", [128, 128], BF16)
            vca = self.sb(st, "vcaug", [128, 192], BF16)
            rkc, rvc = Res(), Res()
            cu = self.sb(st, "cmp_u", [128, 128], F32)
            ct = self.sb(st, "cmp_t", [128, 128], F32)
            cg = self.sb(st, "cmp_g", [128, 128], BF16)
            rcu = Res()
            oacc = self.sb(st, "oaccB", [128, 4, 128], F32)
            roacc = [Res() for _ in range(4)]
            imp = self.sb(st, "impB", [128, 4, 32], F32)
            m8 = self.sb(st, "m8B", [128, 16], F32)
            amb = self.sb(st, "ambB", [128, 32], BF16)
            rimp = Res()
            sm = Ring([(self.sb(st, "smB%d" % i, [128, 4], F32), Res()) for i in range(3)])
            ob = Ring([(self.sb(st, "obB%d" % i, [128, 128], BF16), Res()) for i in range(3)])
            for g in range(4):
                wq, rwq = big.next()
                self.load_w(wq, rwq, w_in, g * 512, 512)
                for r in range(4):
                    for tc4 in range(4):
                        pp, rpp = pproj.next()
                        self.proj_fm(pp, rpp, wq, rwq, r * 128, hT, rhT, tc4 * 512)
                        d_ = qT[:, r, tc4 * 512:(tc4 + 1) * 512]
                        S.copy("act", d_, pp[:], reads=[rpp], writes=[rq[r][tc4]])
                        self.rope_fm2(d_, rq[r][tc4], pproj, perm[:], rope, rr, tc4 * 512, tring)
                for idx, (jj, do_rope) in enumerate(((2, True), (4, True), (0, False), (1, False))):
                    w_, rw_ = small.next()
                    self.load_w(w_, rw_, w_in, 2048 + jj * 512 + g * 128, 128)
                    for tc4 in range(4):
                        pp, rpp = pproj.next()
                        self.proj_fm(pp, rpp, w_, rw_, 0, hT, rhT, tc4 * 512)
                        d_ = kv[:, idx, tc4 * 512:(tc4 + 1) * 512]
                        S.copy("act", d_, pp[:], reads=[rpp], writes=[rkv[idx][tc4]])
                        if do_rope:
                            self.rope_fm2(d_, rkv[idx][tc4], pproj, perm[:], rope, rr, tc4 * 512, tring)
                for idx, jj in enumerate((3, 5)):
                    w_, rw_ = small.next()
                    self.load_w(w_, rw_, w_in, 2048 + jj * 512 + g * 128, 128)
                    for t4 in range(0, NT, 4):
                        pp, rpp = pproj.next()
                        for tt_ in range(4):
                            t = t4 + tt_
                            for c in range(16):
                                S.mm(pp[:, tt_ * 128:(tt_ + 1) * 128], hT[:, c, t * 128:(t + 1) * 128], w_[:, c, 0:128],
                                     start=(c == 0), stop=(c == 15), reads=[rw_, rhT], writes=[rpp])
                        if t4 == 0:
                            self.run_deferred()
                        S.copy("act", Vs[:, idx, t4:t4 + 4, 0:128], pp[:].rearrange("p (t n) -> p t n", n=128),
                               reads=[rpp], writes=[rVs[idx]])
                S.op("dve", lambda e: e.memset(kcT[:], 0.0), writes=[rkc])
                S.op("dve", lambda e: e.memset(vca[:], 0.0), writes=[rvc])
                S.op("dve", lambda e: e.memset(vca[:, 128:129], 1.0), writes=[rvc])
                S.copy("dve", vca[:, 129:161], cover[:], reads=[rcst], writes=[rvc])
                for j in range(2):
                    S.dma("pool", w1[:], self.din[p + "b_cmp_w1"][j].rearrange("(l d) o -> d l o", d=128), writes=[rw1])
                    pp, rpp = pproj.next()
                    for l in range(32):
                        S.mm(pp[:, 0:1], w1[:, l, :], posb[:, j, l:l + 1], start=(l == 0), stop=(l == 31),
                             reads=[rcw, rw1], writes=[rpp])
                    S.copy("dve", cb[:, j:j + 1], pp[:, 0:1], reads=[rpp], writes=[rcw])
                    pp, rpp = pproj.next()
                    for l in range(32):
                        S.mm(pp[:, 0:127], w1[:, l, :], kv[:, 2 + j, l:l + 16 * 126 + 1:16], start=(l == 0), stop=(l == 31),
                             reads=[rw1] + rkv[2 + j], writes=[rpp])
                    S.act(cu[:, 0:127], pp[:, 0:127], AF.Identity, reads=[rpp, rcw], writes=[rcu], bias=cb[:, j:j + 1], scale=1.0)
                    S.tt("dve", ct[:, 0:127], cu[:, 0:127], cu[:, 0:127], ALU.mult, reads=[rcu], writes=[rcu])
                    S.ts("dve", ct[:, 0:127], ct[:, 0:127], 0.044715, 1.0, ALU.mult, ALU.add, reads=[rcu], writes=[rcu])
                    S.tt("dve", ct[:, 0:127], ct[:, 0:127], cu[:, 0:127], ALU.mult, reads=[rcu], writes=[rcu])
                    S.act(ct[:, 0:127], ct[:, 0:127], AF.Sigmoid, reads=[rcu], writes=[rcu], scale=2.0 * 0.7978845608028654)
                    S.tt("dve", cg[:, 0:127], cu[:, 0:127], ct[:, 0:127], ALU.mult, reads=[rcu], writes=[rcu])
                    pp, rpp = pproj.next()
                    if j == 0:
                        S.mm(pp[:, 0:127], w2[:, 0, :], cg[:, 0:127], start=True, stop=True, reads=[rcw, rcu], writes=[rpp])
                        S.copy("act", kcT[:, 0:127], pp[:, 0:127], reads=[rpp], writes=[rkc])
                        pr, rpr = pproj.next()
                        tmp, rtmp = tring.next()
                        S.mm(pr[:, 0:128], perm[:], kcT[:], start=True, stop=True, reads=[rkc, rr], writes=[rpr])
                        S.tt("pool", tmp[:, 0:128], kcT[:], ropekc[:, 0, :], ALU.mult, reads=[rkc, rcst], writes=[rtmp])
                        S.tt("dve", kcT[:], pr[:, 0:128], ropekc[:, 1, :], ALU.mult, reads=[rpr, rcst], writes=[rkc])
                        S.tt("dve", kcT[:], kcT[:], tmp[:, 0:128], ALU.add, reads=[rtmp], writes=[rkc])
                    else:
                        S.mm(pp[0:127, 0:128], cg[:, 0:127], w2[:, 1, :], start=True, stop=True, reads=[rcw, rcu], writes=[rpp])
                        S.copy("act", vca[0:127, 0:128], pp[0:127, 0:128], reads=[rpp], writes=[rvc])
                Obank = [(Oa, rOa), (Ob, rOb)]
                amTs = [(self.sb(st, "amTB%d_%d" % (g, k_), [32, 128], BF16), Res()) for k_ in range(2)]
                for i in range(NT):
                    oT, roT = oTr.next()
                    amT, ramT = amTs[i % 2]
                    S.op("dve", lambda e: e.memset(imp[:, 0, :], 0.0), writes=[rimp])
                    items = []
                    for r in range(4):
                        O_, rO_ = Obank[r // 2]
                        items.append(dict(k=kcT[:], kr=[rkc], q=qT[:, r, i * 128:(i + 1) * 128], qr=[rq[r][i // 4]],
                                          v=vca[:, 0:161], vr=[rvc], mask=MASK_IDS["cmp_%d" % i],
                                          O=O_[:, (r % 2) * 256:(r % 2) * 256 + 161], rO=rO_, start=True, stop=True))
                    pipe.run(items, sc)
                    pipe.flush()
                    for r in range(4):
                        hq = g * 4 + r
                        O_, rO_ = Obank[r // 2]
                        OC = O_[:, (r % 2) * 256:(r % 2) * 256 + 161]
                        sm_, rsm = sm.next()
                        S.ts("dve", sm_[:, 0:1], OC[:, 128:129], 1e-30, None, ALU.max, None, reads=[rO_], writes=[rsm])
                        S.op("dve", lambda e: e.reciprocal(sm_[:, 0:1], sm_[:, 0:1]), reads=[rsm], writes=[rsm])
                        S.stt(imp[:, 0, :], OC[:, 129:161], sm_[:, 0:1], imp[:, 0, :], ALU.mult, ALU.add,
                              reads=[rO_, rsm, rimp], writes=[rimp])
                        S.tt("dve", sm_[:, 1:2], sm_[:, 0:1], gate[:, i, hq * 3:hq * 3 + 1], ALU.mult, reads=[rsm, rgate], writes=[rsm])
                        S.ts("dve", oacc[:, r, :], OC[:, 0:128], sm_[:, 1:2], None, ALU.mult, None, reads=[rO_, rsm], writes=[roacc[r]])
                    S.tt("dve", imp[:, 1, :], imp[:, 0, :], seladd[:, i, :], ALU.add, reads=[rimp, rcst], writes=[rimp])
                    S.op("dve", lambda e: e.max(m8[:, 0:8], imp[:, 1, :]), reads=[rimp], writes=[rimp])
                    S.op("dve", lambda e: e.match_replace(imp[:, 2, :], m8[:, 0:8], imp[:, 1, :], -3.0e38), reads=[rimp], writes=[rimp])
                    S.op("dve", lambda e: e.max(m8[:, 8:16], imp[:, 2, :]), reads=[rimp], writes=[rimp])
                    S.ts("dve", imp[:, 3, :], imp[:, 1, :], m8[:, 15:16], None, ALU.is_ge, None, reads=[rimp], writes=[rimp])
                    S.ts("dve", amb[:], imp[:, 3, :], -1.0, -NEG, ALU.add, ALU.mult, reads=[rimp], writes=[rimp])
                    S.tr(ptr[0:32, 256:384], amb[:], self.ident_bf[:], reads=[rimp, self.rC], writes=[rptr[2]])
                    S.copy("dve", amT[:], ptr[0:32, 256:384], reads=[rptr[2]], writes=[ramT])

                    def epi(r, i, O_, rO_, oT, roT, par):
                        hq = g * 4 + r
                        sm_, rsm = sm.next()
                        ob_, rob = ob.next()
                        Osel, Owin = O_[:, 0:129], O_[:, 256:385]
                        S.op("dve", lambda e: e.reciprocal(sm_[:, 0:1], Osel[:, 128:129]), reads=[rO_], writes=[rsm])
                        S.op("dve", lambda e: e.reciprocal(sm_[:, 1:2], Owin[:, 128:129]), reads=[rO_], writes=[rsm])
                        S.tt("dve", sm_[:, 0:2], sm_[:, 0:2], gate[:, i, hq * 3 + 1:hq * 3 + 3], ALU.mult, reads=[rsm, rgate], writes=[rsm])
                        S.stt(oacc[:, r, :], Osel[:, 0:128], sm_[:, 0:1], oacc[:, r, :], ALU.mult, ALU.add,
                              reads=[rO_, rsm, roacc[r]], writes=[roacc[r]])
                        S.stt(ob_[:], Owin[:, 0:128], sm_[:, 1:2], oacc[:, r, :], ALU.mult, ALU.add,
                              reads=[rO_, rsm, roacc[r]], writes=[rob])

                        def tr_():
                            pt = ptr[:, par * 128:(par + 1) * 128]
                            S.tr(pt, ob_[:], self.ident_bf[:], reads=[rob, self.rC], writes=[rptr[par]])
                            S.copy("dve", oT[:, r, :], pt, reads=[rptr[par]], writes=[roT])
                        self.defer(tr_, key="tr")

                    for r in range(4):
                        O_, rO_ = Obank[r % 2]
                        q_ap = qT[:, r, i * 128:(i + 1) * 128]
                        items = [dict(k=kv[:, 0, j * 128:(j + 1) * 128], kr=[rkv[0][j // 4]], q=q_ap, qr=[rq[r][i // 4]],
                                      v=Vs[:, 0, j, 0:129], vr=[rVs[0]],
                                      mask=MASK_IDS["causal"] if j == i else None,
                                      extra=(expand[:, j * 128:(j + 1) * 128], amT[:], [rcst, ramT]),
                                      O=O_[:, 0:129], rO=rO_, start=(j == 0), stop=(j == i)) for j in range(i + 1)]
                        j0_ = max(0, i - 4)
                        items += [dict(k=kv[:, 1, j * 128:(j + 1) * 128], kr=[rkv[1][j // 4]], q=q_ap, qr=[rq[r][i // 4]],
                                       v=Vs[:, 1, j, 0:129], vr=[rVs[1]],
                                       mask=MASK_IDS["causal"] if j == i else (MASK_IDS["win4"] if j == i - 4 else None),
                                       O=O_[:, 256:385], rO=rO_, start=(j == j0_), stop=(j == i)) for j in range(j0_, i + 1)]
                        pipe.run(items, sc, after=lambda r=r, i=i, O_=O_, rO_=rO_, oT=oT, roT=roT, par=r % 2:
                                 epi(r, i, O_, rO_, oT, roT, par))
                    pipe.flush()
                    self.run_deferred("tr")
                    S.dma("sp", self.OT[s, :, g * 4:(g + 1) * 4, i * 128:(i + 1) * 128], oT[:], reads=[roT], writes=[self.rOT[s]])
            S.barrier()

    def mixer_A(self, L, s):
        S = self.S
        S.barrier()
        p = "l%d_" % L
        w_in = self.din[p + "a_w_in"]
        with ExitStack() as st:
            hT, rhT = self.load_hT(st, s)
            rope, perm, rr = self.load_rope(st, "A")
            slots = Ring([(self.sb(st, "wslot%d" % i, [128, 16, 256], BF16), Res()) for i in range(5)])
            pproj = Ring([(self.ps(st, "pproj%d" % i, [128, 512], F32), Res()) for i in range(3)])
            pipe = AttnPipe(self, st, extra_banks=pproj.items)
            Ob = [(self.ps(st, "OA%d" % i, [128, 512], F32), Res()) for i in range(2)]
            ptr = self.ps(st, "ptrA", [128, 1024], BF16)
            rptr = [Res(), Res()]
            qT = self.sb(st, "qT", [128, 2, SEQ], BF16)
            kT = self.sb(st, "kT", [128, 2, SEQ], BF16)
            rqk = [[[Res() for _ in range(4)] for _ in range(2)] for _ in range(2)]
            V = self.sb(st, "Vaug", [128, NT, 2, 130], BF16)
            rV = Res()
            tring = Ring([(self.sb(st, "ropetmp%d" % i, [128, 512], F32), Res()) for i in range(2)])
            oT = self.sb(st, "oTg", [128, 2, SEQ], BF16)
            roT = Res()
            lam = self.sb(st, "lam", [128, 256], F32)
            lsc = self.sb(st, "lsc", [128, 8], F32)
            gv = self.sb(st, "gv", [128, 128], F32)
            rl = Res()
            S.dma("sp", lam[:], self.din[p + "a_lam"].rearrange("a b -> (a b)").partition_broadcast(128), writes=[rl])
            S.dma("sp", gv[:], self.din[p + "a_subln"].partition_broadcast(128), writes=[rl])
            S.tt("dve", lam[:, 0:64], lam[:, 0:64], lam[:, 64:128], ALU.mult, reads=[rl], writes=[rl])
            S.tt("dve", lam[:, 128:192], lam[:, 128:192], lam[:, 192:256], ALU.mult, reads=[rl], writes=[rl])
            S.op("dve", lambda e: e.reduce_sum(lsc[:, 0:1], lam[:, 0:64], axis=AX.X), reads=[rl], writes=[rl])
            S.op("dve", lambda e: e.reduce_sum(lsc[:, 1:2], lam[:, 128:192], axis=AX.X), reads=[rl], writes=[rl])
            S.act(lsc[:, 2:4], lsc[:, 0:2], AF.Exp, reads=[rl], writes=[rl])
            S.tt("dve", lsc[:, 4:5], lsc[:, 3:4], lsc[:, 2:3], ALU.subtract, reads=[rl], writes=[rl])
            S.ts("dve", lsc[:, 4:5], lsc[:, 4:5], -LAM_INIT[L], None, ALU.add, None, reads=[rl], writes=[rl])
            S.ts("dve", gv[:], gv[:], 1.0 - LAM_INIT[L], None, ALU.mult, None, reads=[rl], writes=[rl])
            S.op("dve", lambda e: e.memset(V[:, :, :, 128:129], 1.0), writes=[rV])
            osb = Ring([(self.sb(st, "osb%d" % i, [128, 128], F32), Res()) for i in range(2)])
            obf = Ring([(self.sb(st, "obf%d" % i, [128, 128], BF16), Res()) for i in range(3)])
            sm = Ring([(self.sb(st, "smA%d" % i, [128, 8], F32), Res()) for i in range(2)])
            junk = Ring([(self.sb(st, "junkA%d" % i, [128, 128], F32), Res()) for i in range(2)])
            cnt = [0]

            def epilogue(hh, i, O, rO, par):
                sm_, rsm = sm.next()
                o_, ro = osb.next()
                ob, rob = obf.next()
                O1, O2 = O[:, 0:129], O[:, 256:385]
                S.op("dve", lambda e: e.reciprocal(sm_[:, 0:1], O1[:, 128:129]), reads=[rO], writes=[rsm])
                S.op("dve", lambda e: e.reciprocal(sm_[:, 1:2], O2[:, 128:129]), reads=[rO], writes=[rsm])
                S.tt("dve", sm_[:, 1:2], sm_[:, 1:2], lsc[:, 4:5], ALU.mult, reads=[rsm, rl], writes=[rsm])
                S.ts("dve", o_[:], O1[:, 0:128], sm_[:, 0:1], None, ALU.mult, None, reads=[rO, rsm], writes=[ro])
                S.stt(o_[:], O2[:, 0:128], sm_[:, 1:2], o_[:], ALU.mult, ALU.add, reads=[rO, rsm, ro], writes=[ro])
                jk, rjk = junk.next()
                S.tt("dve", jk[:], o_[:], o_[:], ALU.mult, reads=[ro], writes=[rjk])
                S.op("dve", lambda e: e.reduce_sum(sm_[:, 2:3], jk[:], axis=AX.X), reads=[rjk], writes=[rsm])
                S.ts("dve", sm_[:, 2:3], sm_[:, 2:3], 1.0 / 128.0, EPS, ALU.mult, ALU.add, reads=[rsm], writes=[rsm])
                S.tt("pool", sm_[:, 3:4], sm_[:, 2:3], self.nhalf[:, 0:1], ALU.pow, reads=[rsm, self.rC], writes=[rsm])
                S.stt(ob[:], o_[:], sm_[:, 3:4], gv[:], ALU.mult, ALU.mult, reads=[ro, rsm, rl], writes=[rob])

                def tr_():
                    pt = ptr[:, par * 128:(par + 1) * 128]
                    S.tr(pt, ob[:], self.ident_bf[:], reads=[rob, self.rC], writes=[rptr[par]])
                    S.copy("dve", oT[:, hh, i * 128:(i + 1) * 128], pt, reads=[rptr[par]], writes=[roT])
                self.defer(tr_, key="tr")

            for hg in range(8):
                wq, rwq = slots.next(); wk, rwk = slots.next(); wv, rwv = slots.next()
                self.load_w(wq, rwq, w_in, hg * 256, 256)
                self.load_w(wk, rwk, w_in, 2048 + hg * 256, 256)
                self.load_w(wv, rwv, w_in, 4096 + hg * 256, 256)
                for hh in range(2):
                    for (dst, w_, rw_, which) in ((qT, wq, rwq, 0), (kT, wk, rwk, 1)):
                        for tc4 in range(4):
                            pp, rpp = pproj.next()
                            self.proj_fm(pp, rpp, w_, rw_, hh * 128, hT, rhT, tc4 * 512)
                            d_ = dst[:, hh, tc4 * 512:(tc4 + 1) * 512]
                            S.copy("act", d_, pp[:], reads=[rpp], writes=[rqk[which][hh][tc4]])
                            self.rope_fm2(d_, rqk[which][hh][tc4], pproj, perm[:], rope, rr, tc4 * 512, tring)
                for t in range(NT):
                    pp, rpp = pproj.next()
                    self.proj_tm(pp, rpp, wv, rwv, 0, 256, hT, rhT, t)
                    if t == 0:
                        self.run_deferred()
                    S.copy("act", V[:, t, :, 0:128], pp[:, 0:256].rearrange("p (h n) -> p h n", n=128),
                           reads=[rpp], writes=[rV])
                for hh in range(2):
                    for i in range(NT):
                        par = cnt[0] % 2
                        cnt[0] += 1
                        O, rO = Ob[par]
                        for m in range(2):
                            items = []
                            for j in range(i + 1):
                                items.append(dict(k=kT[m * 64:(m + 1) * 64, hh, j * 128:(j + 1) * 128], kr=[rqk[1][hh][j // 4]],
                                                  q=qT[m * 64:(m + 1) * 64, hh, i * 128:(i + 1) * 128], qr=[rqk[0][hh][i // 4]],
                                                  v=V[:, j, hh, 0:129], vr=[rV],
                                                  mask=MASK_IDS["causal"] if j == i else None,
                                                  O=O[:, m * 256:m * 256 + 129], rO=rO, start=(j == 0), stop=(j == i)))
                            pipe.run(items, 0.125, after=(None if m == 0 else
                                     (lambda hh=hh, i=i, O=O, rO=rO, par=par: epilogue(hh, i, O, rO, par))))
                pipe.flush()
                self.run_deferred("tr")
                S.dma("sp", self.OT[s, :, hg * 2:hg * 2 + 2, :], oT[:], reads=[roT], writes=[self.rOT[s]])
            S.barrier()

    def mixer_C(self, L, s):
        S = self.S
        S.barrier()
        p = "l%d_" % L
        w_in = self.din[p + "c_w_in"]
        with ExitStack() as st:
            hT, rhT = self.load_hT(st, s)
            rope, perm, rr = self.load_rope(st, "B")
            slots = Ring([(self.sb(st, "wslotC%d" % i, [128, 16, 128], BF16), Res()) for i in range(12)])
            pproj = Ring([(self.ps(st, "pproj%d" % i, [128, 512], F32), Res()) for i in range(3)])
            pipe = AttnPipe(self, st, extra_banks=pproj.items)
            Ob = [(self.ps(st, "OC%d" % i, [128, 512], F32), Res()) for i in range(2)]
            ptr = self.ps(st, "ptrC", [128, 1024], BF16)
            rptr = [Res(), Res()]
            qT = self.sb(st, "qTC", [128, 3, SEQ], BF16)
            kT = self.sb(st, "kTC", [128, 3, SEQ], BF16)
            rq = [[Res() for _ in range(4)] for _ in range(3)]
            rk = [[Res() for _ in range(4)] for _ in range(3)]
            V = self.sb(st, "VaugC", [128, 3, NT, 130], BF16)
            rV = [Res() for _ in range(3)]
            tring = Ring([(self.sb(st, "ropetmp%d" % i, [128, 512], F32), Res()) for i in range(2)])
            oT, roT = self.sb(st, "oTC", [128, SEQ], BF16), Res()
            S.op("dve", lambda e: e.memset(V[:, :, :, 128:129], 1.0), writes=rV)
            ob = Ring([(self.sb(st, "obC%d" % i, [128, 128], BF16), Res()) for i in range(3)])
            sm = Ring([(self.sb(st, "smC%d" % i, [128, 2], F32), Res()) for i in range(2)])
            cnt = [0]

            def epilogue(i, O, rO, par):
                sm_, rsm = sm.next()
                ob_, rob = ob.next()
                S.op("dve", lambda e: e.reciprocal(sm_[:, 0:1], O[:, 128:129]), reads=[rO], writes=[rsm])
                S.ts("dve", ob_[:], O[:, 0:128], sm_[:, 0:1], None, ALU.mult, None, reads=[rO, rsm], writes=[rob])

                def tr_():
                    pt = ptr[:, par * 128:(par + 1) * 128]
                    S.tr(pt, ob_[:], self.ident_bf[:], reads=[rob, self.rC], writes=[rptr[par]])
                    S.copy("dve", oT[:, i * 128:(i + 1) * 128], pt, reads=[rptr[par]], writes=[roT])
                self.defer(tr_, key="tr")

            for h in range(8):
                for g in range(3):
                    for which, (dst, rd) in enumerate(((qT, rq), (kT, rk))):
                        w_, rw_ = slots.next()
                        self.load_w(w_, rw_, w_in, which * 3072 + g * 1024 + h * 128, 128)
                        for tc4 in range(4):
                            pp, rpp = pproj.next()
                            self.proj_fm(pp, rpp, w_, rw_, 0, hT, rhT, tc4 * 512)
                            d_ = dst[:, g, tc4 * 512:(tc4 + 1) * 512]
                            S.copy("act", d_, pp[:], reads=[rpp], writes=[rd[g][tc4]])
                            self.rope_fm2(d_, rd[g][tc4], pproj, perm[:], rope, rr, tc4 * 512, tring)
                    w_, rw_ = slots.next()
                    self.load_w(w_, rw_, w_in, 2 * 3072 + g * 1024 + h * 128, 128)
                    for t4 in range(0, NT, 4):
                        pp, rpp = pproj.next()
                        for tt_ in range(4):
                            t = t4 + tt_
                            for c in range(16):
                                S.mm(pp[:, tt_ * 128:(tt_ + 1) * 128], hT[:, c, t * 128:(t + 1) * 128], w_[:, c, 0:128],
                                     start=(c == 0), stop=(c == 15), reads=[rw_, rhT], writes=[rpp])
                        if t4 == 0:
                            self.run_deferred()
                        S.copy("act", V[:, g, t4:t4 + 4, 0:128], pp[:].rearrange("p (t n) -> p t n", n=128),
                               reads=[rpp], writes=[rV[g]])
                for i in range(NT):
                    par = cnt[0] % 2
                    cnt[0] += 1
                    O, rO = Ob[par]
                    items = []
                    for g, (nm, maxd) in enumerate((("d1_", 1), ("d4_", 4), ("d16_", 15))):
                        for j in range(max(0, i - maxd), i + 1):
                            items.append(dict(k=kT[:, g, j * 128:(j + 1) * 128], kr=[rk[g][j // 4]],
                                              q=qT[:, g, i * 128:(i + 1) * 128], qr=[rq[g][i // 4]],
                                              v=V[:, g, j, 0:129], vr=[rV[g]], mask=MASK_IDS[nm + str(i - j)],
                                              O=O[:, 0:129], rO=rO, start=False, stop=False))
                    items[0]["start"] = True
                    items[-1]["stop"] = True
                    pipe.run(items, 128 ** -0.5, after=lambda i=i, O=O, rO=rO, par=par: epilogue(i, O, rO, par))
                pipe.flush()
                self.run_deferred("tr")
                S.dma("sp", self.OT[s, :, h, :], oT[:], reads=[roT], writes=[self.rOT[s]])
            S.barrier()

    def phase_outproj_ln1(self, L, s, src_is_x):
        S = self.S
        S.barrier()
        p = "l%d_" % L
        kind = KINDS[L]
        w_out = self.din[p + {"A": "a_w_out", "B": "b_w_out", "C": "c_w_out"}[kind]]
        kc = 8 if kind == "C" else 16
        hsrc = self.din["x"] if src_is_x else self.H
        with ExitStack() as st:
            oTr = Ring([(self.sb(st, "oTres%d" % i, [128, kc, 128], BF16), Res()) for i in range(2)])
            wo = self.sb(st, "wo", [128, kc, D], BF16)
            rwo = Res()
            g_ = {}
            for c in range(0, kc, 4):
                S.dma("pool", wo[:, c:c + 4, :], w_out.rearrange("(c p) n -> p c n", p=128)[:, c:c + 4, :], writes=[rwo], grp=g_)
            gb, rgb, stats, rstats = self.alloc_ln(st, L, "1", 0)
            wr = self.sb(st, "wr", [128, 16, 36], F32)
            rb = self.sb(st, "rb", [128, 36], F32)
            rwr = Res()
            S.dma("sp", wr[:], self.din[p + "router_w"].rearrange("(c p) n -> p c n", p=128), writes=[rwr])
            S.dma("sp", rb[:], self.din[p + "router_b"].partition_broadcast(128), writes=[rwr])
            pso = [(self.ps(st, "pso%d" % i, [128, 512], F32), Res()) for i in range(4)]
            ptf = Ring([(self.ps(st, "ptf%d" % i, [128, 512], F32), Res()) for i in range(2)])
            plg, rplg = self.ps(st, "plg", [128, 64], F32), Res()
            hres = Ring([(self.sb(st, "hres%d" % i, [128, D], F32), Res()) for i in range(2)])
            a_t, ra = self.sb(st, "a_t", [128, D], F32), Res()
            tmp, rtmp = self.sb(st, "lntmp", [128, D], F32), Res()
            h1 = Ring([(self.sb(st, "h1_%d" % i, [128, D], F32), Res()) for i in range(2)])
            xb = Ring([(self.sb(st, "xb_%d" % i, [128, D], BF16), Res()) for i in range(2)])
            hTf, rhTf = self.sb(st, "hTf", [128, 16, 128], F32), Res()
            g = self.sb(st, "gate", [128, 64], F32)
            g8 = self.sb(st, "gate8", [128, 16], F32)
            rg = Res()
            for t in range(NT):
                gt = s * NT + t
                hr, rhr = hres.next()
                S.dma("sp", hr[:], hsrc[gt * 128:(gt + 1) * 128, :], reads=[] if src_is_x else [self.rH[gt]], writes=[rhr])
                oT, roT = oTr.next()
                S.dma("sp", oT[:], self.OT[s, :, 0:kc, t * 128:(t + 1) * 128], reads=[self.rOT[s]], writes=[roT])
                for cc in range(4):
                    po, rpo = pso[cc]
                    for c in range(kc):
                        S.mm(po[:], oT[:, c, :], wo[:, c, cc * 512:(cc + 1) * 512],
                             start=(c == 0), stop=(c == kc - 1), reads=[roT, rwo], writes=[rpo])
                    S.stt(a_t[:, cc * 512:(cc + 1) * 512], hr[:, cc * 512:(cc + 1) * 512], ALPHA, po[:],
                          ALU.mult, ALU.add, reads=[rhr, rpo], writes=[ra])
                h1t, rh1 = h1.next()
                self.layer_norm_tile(a_t, ra, gb, rgb, 0, h1t, rh1, stats, rstats, tmp, rtmp)
                S.dma("sp", self.H1[gt * 128:(gt + 1) * 128, :], h1t[:], reads=[rh1], writes=[self.rH1[gt]])
                xbt, rxb = xb.next()
                S.copy("act", xbt[:], h1t[:], reads=[rh1], writes=[rxb])
                S.dma("sp", self.XB[gt * 128:(gt + 1) * 128, :], xbt[:], reads=[rxb], writes=[self.rXB[gt]])
                for q4 in range(4):
                    pt, rpt = ptf.next()
                    for j in range(4):
                        c = q4 * 4 + j
                        S.tr(pt[:, j * 128:(j + 1) * 128], h1t[:, c * 128:(c + 1) * 128], self.ident_f[:],
                             reads=[rh1, self.rC], writes=[rpt])
                    S.copy("dve" if q4 % 2 == 0 else "act", hTf[:, q4 * 4:(q4 + 1) * 4, :],
                           pt[:].rearrange("p (j n) -> p j n", n=128), reads=[rpt], writes=[rhTf])
                for c in range(16):
                    S.mm(plg[:, 0:36], hTf[:, c, :], wr[:, c, :], start=(c == 0), stop=(c == 15),
                         reads=[rhTf, rwr], writes=[rplg])
                self.gating(gt, plg, rplg, rb, rwr, g, g8, rg)
            S.barrier()

    def gating(self, gt, plg, rplg, rb, rwr, g, g8, rg):
        S = self.S
        R = [rg]
        lg = g[:, 0:36]
        S.tt("dve", lg, plg[:, 0:36], rb[:, 0:36], ALU.add, reads=[rplg, rwr], writes=R)
        gm = g[:, 36:37]
        S.op("dve", lambda e: e.reduce_max(gm, g[:, 0:4], axis=AX.X), reads=R, writes=R)
        ngm = g[:, 37:38]
        S.ts("dve", ngm, gm, -1.0, None, ALU.mult, None, reads=R, writes=R)
        ex = g[:, 40:44]
        S.act(ex, g[:, 0:4], AF.Exp, reads=R, writes=R, bias=ngm, scale=1.0)
        gs = g[:, 38:39]
        S.op("dve", lambda e: e.reduce_sum(gs, ex, axis=AX.X), reads=R, writes=R)
        gw = g[:, 39:40]
        S.op("dve", lambda e: e.reciprocal(gw, gs), reads=R, writes=R)
        gmask = g[:, 44:48]
        S.ts("dve", gmask, g[:, 0:4], gm, None, ALU.is_ge, None, reads=R, writes=R)
        S.ts("dve", gmask, gmask, -1.0, 1e30, ALU.add, ALU.mult, reads=R, writes=R)
        el = g[:, 4:36]
        S.tt("dve", el.rearrange("p (a b) -> p a b", b=8), el.rearrange("p (a b) -> p a b", b=8),
             gmask.unsqueeze(2).broadcast_to([128, 4, 8]), ALU.add, reads=R, writes=R)
        S.op("dve", lambda e: e.max(g8[:, 0:8], el), reads=R, writes=R)
        d = g8[:, 8:9]
        S.tt("dve", d, g8[:, 1:2], g8[:, 0:1], ALU.subtract, reads=R, writes=R)
        S.act(d, d, AF.Exp, reads=R, writes=R)
        S.ts("dve", d, d, 1.0, None, ALU.add, None, reads=R, writes=R)
        S.op("dve", lambda e: e.reciprocal(d, d), reads=R, writes=R)
        Wt = [self.rOH[gt]]
        S.tt("dve", self.GW[:, gt, 0:1], d, gw, ALU.mult, reads=R, writes=Wt)
        S.tt("dve", self.GW[:, gt, 1:2], gw, self.GW[:, gt, 0:1], ALU.subtract, reads=R + Wt, writes=Wt)
        S.ts("dve", self.OH[:, gt, 0, :], el, g8[:, 0:1], None, ALU.is_equal, None, reads=R, writes=Wt)
        S.ts("dve", self.OH[:, gt, 1, :], el, g8[:, 1:2], None, ALU.is_equal, None, reads=R, writes=Wt)

    def phase_dispatch(self, L):
        S = self.S
        S.barrier()
        NTT, NBLK = self.NTT, self.NBLK
        with ExitStack() as st:
            A = self.sb(st, "dA", [128, NTT + 1, 32], BF16)
            Ac = self.sb(st, "dAc", [128, NTT + 1, 32], BF16)
            rA = Res()
            prk = Ring([(self.ps(st, "prk%d" % i, [128, 32], F32), Res()) for i in range(2)])
            psz, rpsz = self.ps(st, "psz", [128, 32], F32), Res()
            sz = self.sb(st, "dsz", [128, 8, 32], F32)
            rsz = Res()
            S.op("dve", lambda e: e.memset(Ac[:, 0, :], 0.0), writes=[rA])
            for t in range(NTT):
                S.tt("dve", A[:, t, :], self.OH[:, t, 0, :], self.OH[:, t, 1, :], ALU.add, reads=[self.rOH[t]], writes=[rA])
                S.tt("dve", Ac[:, t + 1, :], Ac[:, t, :], A[:, t, :], ALU.add, reads=[rA], writes=[rA])
            S.mm(psz[:], self.ones_bf[:], Ac[:, NTT, :], start=True, stop=True, reads=[rA, self.rC], writes=[rpsz])
            sizes, padded, pend, pstart, t0, t1 = (sz[:, i, :] for i in range(6))
            S.copy("dve", sizes, psz[:], reads=[rpsz], writes=[rsz])
            S.op("dve", lambda e: e.memset(padded, 0.0), writes=[rsz])
            for m in range(self.NTOK * 2 // self.RB):
                S.stt(padded, sizes, float(self.RB * m), padded, ALU.is_gt, ALU.add, reads=[rsz], writes=[rsz])
            S.ts("dve", padded, padded, float(self.RB), None, ALU.mult, None, reads=[rsz], writes=[rsz])
            cur, oth = pend, t0
            S.copy("dve", cur, padded, reads=[rsz], writes=[rsz])
            for sh in (1, 2, 4, 8, 16):
                S.copy("dve", oth, cur, reads=[rsz], writes=[rsz])
                S.tt("dve", oth[:, sh:32], cur[:, sh:32], cur[:, 0:32 - sh], ALU.add, reads=[rsz], writes=[rsz])
                cur, oth = oth, cur
            S.copy("dve", t1, cur, reads=[rsz], writes=[rsz])
            pend = t1
            S.tt("dve", pstart, pend, padded, ALU.subtract, reads=[rsz], writes=[rsz])
            be = self.sb(st, "dbe", [128, NBLK], F32)
            io = self.sb(st, "dio", [128, NBLK], F32)
            rbe = Res()
            S.dma("sp", io[:], self.din["c_blk"], writes=[rbe])
            S.op("dve", lambda e: e.memset(be[:], 0.0), writes=[rbe])
            for e_ in range(32):
                S.stt(be[:], io[:], pend[:, e_:e_ + 1], be[:], ALU.is_ge, ALU.add, reads=[rsz, rbe], writes=[rbe])
            S.ts("dve", be[:], be[:], 31.0, None, ALU.min, None, reads=[rbe], writes=[rbe])
            S.ts("dve", io[:], be[:], 2048.0, self.iota[:, 0:1], ALU.mult, ALU.add, reads=[rbe, self.rC], writes=[rbe])
            S.copy("dve", self.IDX[:, 0, :], io[:], reads=[rbe], writes=[self.rIDX])
            S.ts("dve", io[:], be[:], 512.0, self.iota[:, 0:1], ALU.mult, ALU.add, reads=[rbe, self.rC], writes=[rbe])
            S.copy("dve", self.IDX[:, 1, :], io[:], reads=[rbe], writes=[self.rIDX])
            dtmp = self.sb(st, "dtmp", [128, 4, 32], F32)
            dd = self.sb(st, "ddest", [128, 2], F32)
            rd = Res()
            xbr = Ring([(self.sb(st, "dxb%d" % i, [128, D], BF16), Res()) for i in range(3)])
            for t in range(NTT):
                pr_, rpr = prk.next()
                S.mm(pr_[:], self.tri[:], A[:, t, :], start=True, stop=False, reads=[rA, self.rC], writes=[rpr])
                S.mm(pr_[:], self.ones_bf[:], Ac[:, t, :], start=False, stop=True, reads=[rA, self.rC], writes=[rpr])
                S.tt("dve", dtmp[:, 0, :], pr_[:], pstart, ALU.add, reads=[rpr, rsz], writes=[rd])
                for k in range(2):
                    S.tt("dve", dtmp[:, 1 + k, :], dtmp[:, 0, :], self.OH[:, t, k, :], ALU.mult,
                         reads=[rd, self.rOH[t]], writes=[rd])
                    S.op("dve", lambda e, k=k: e.reduce_sum(dd[:, k:k + 1], dtmp[:, 1 + k, :], axis=AX.X),
                         reads=[rd], writes=[rd])
                S.copy("dve", self.DST[:, t, :], dd[:], reads=[rd], writes=[self.rDST[t]])
                xb_, rxb = xbr.next()
                S.dma("sp", xb_[:], self.XB[t * 128:(t + 1) * 128, :], reads=[self.rXB[t]], writes=[rxb])
                for k in range(2):
                    S.dma("pool", self.XS, xb_[:], reads=[rxb, self.rDST[t]], writes=[self.rXS],
                          indirect={"out_offset": bass.IndirectOffsetOnAxis(ap=self.DST[:, t, k:k + 1], axis=0)})
            S.barrier()

    def phase_blocks(self, L):
        S = self.S
        S.barrier()
        p = "l%d_" % L
        w_in = self.din[p + "moe_w_in"].rearrange("e d n -> (e d) n")
        w_out = self.din[p + "moe_w_out"].rearrange("e f n -> (e f) n")
        with ExitStack() as st:
            win = Ring([(self.sb(st, "win%d" % i, [128, 16, 1024], BF16), Res()) for i in range(3)])
            wout = Ring([(self.sb(st, "wout%d" % i, [128, 4, D], BF16), Res()) for i in range(3)])
            xs = Ring([(self.sb(st, "bxs%d" % i, [128, D], BF16), Res()) for i in range(3)])
            xsT = Ring([(self.sb(st, "bxsT%d" % i, [128, 16, 128], BF16), Res()) for i in range(2)])
            ptr = Ring([(self.ps(st, "bptr%d" % i, [128, 1024], BF16), Res()) for i in range(2)])
            pgu = [(self.ps(st, "bpgu%d" % i, [128, 512], F32), Res()) for i in range(2)]
            py = [(self.ps(st, "bpy%d" % i, [128, 512], F32), Res()) for i in range(4)]
            sg, rsg = self.sb(st, "bsg", [128, 512], F32), Res()
            hm, rhm = self.sb(st, "bhm", [128, 512], BF16), Res()
            hmT, rhmT = self.sb(st, "bhmT", [128, 4, 128], BF16), Res()
            ysb = Ring([(self.sb(st, "bys%d" % i, [128, D], F32), Res()) for i in range(2)])
            def load_x(b_, sub_):
                r0 = b_ * self.RB + sub_ * 128
                xa, rxa = xs.next()
                S.dma("sp", xa[:], self.XS[r0:r0 + 128, :], reads=[self.rXS], writes=[rxa])
                return xa, rxa

            nsub = self.RB // 128
            pend = load_x(0, 0)
            for b in range(self.NBLK):
                wi, rwi = win.next()
                wo, rwo = wout.next()
                gi, go = {}, {}
                for c in range(16):
                    S.dma("pool", wi[:, c, :], w_in, reads=[self.rIDX], writes=[rwi], grp=gi,
                          indirect={"in_offset": bass.IndirectOffsetOnAxis(ap=self.IDX[:, 0, b:b + 1], axis=0)},
                          element_offset=c * 128 * 1024)
                for c in range(4):
                    S.dma("pool", wo[:, c, :], w_out, reads=[self.rIDX], writes=[rwo], grp=go,
                          indirect={"in_offset": bass.IndirectOffsetOnAxis(ap=self.IDX[:, 1, b:b + 1], axis=0)},
                          element_offset=c * 128 * D)
                for sub in range(self.RB // 128):
                    row0 = b * self.RB + sub * 128
                    x_, rx = pend
                    nb_, ns_ = (b, sub + 1) if sub + 1 < nsub else (b + 1, 0)
                    if nb_ < self.NBLK:
                        pend = load_x(nb_, ns_)
                    xt, rxt = xsT.next()
                    for half in range(2):
                        pt, rpt = ptr.next()
                        for j in range(8):
                            c = half * 8 + j
                            S.tr(pt[:, j * 128:(j + 1) * 128], x_[:, c * 128:(c + 1) * 128], self.ident_bf[:],
                                 reads=[rx, self.rC], writes=[rpt])
                        S.copy("dve" if half == 0 else "act", xt[:, half * 8:(half + 1) * 8, :],
                               pt[:].rearrange("p (j n) -> p j n", n=128), reads=[rpt], writes=[rxt])
                    for hf in range(2):
                        pg, rpg = pgu[hf]
                        for c in range(16):
                            S.mm(pg[:], xt[:, c, :], wi[:, c, hf * 512:(hf + 1) * 512], start=(c == 0), stop=(c == 15),
                                 reads=[rxt, rwi], writes=[rpg])
                    S.act(sg[:], pgu[0][0][:], AF.Silu, reads=[pgu[0][1]], writes=[rsg])
                    S.tt("dve", hm[:], sg[:], pgu[1][0][:], ALU.mult, reads=[rsg, pgu[1][1]], writes=[rhm])
                    pt, rpt = ptr.next()
                    for j in range(4):
                        S.tr(pt[:, j * 128:(j + 1) * 128], hm[:, j * 128:(j + 1) * 128], self.ident_bf[:],
                             reads=[rhm, self.rC], writes=[rpt])
                    S.copy("dve", hmT[:], pt[:, 0:512].rearrange("p (j n) -> p j n", n=128), reads=[rpt], writes=[rhmT])
                    y_, ry = ysb.next()
                    for cc in range(4):
                        pyy, rpy = py[cc]
                        for c in range(4):
                            S.mm(pyy[:], hmT[:, c, :], wo[:, c, cc * 512:(cc + 1) * 512], start=(c == 0), stop=(c == 3),
                                 reads=[rhmT, rwo], writes=[rpy])
                        S.copy("act" if cc % 2 == 0 else "dve", y_[:, cc * 512:(cc + 1) * 512], pyy[:], reads=[rpy], writes=[ry])
                    S.dma("sp", self.YS[row0:row0 + 128, :], y_[:], reads=[ry], writes=[self.rYS])
            S.barrier()

    def phase_combine(self, L, last):
        S = self.S
        S.barrier()
        with ExitStack() as st:
            gb, rgb, stats, rstats = self.alloc_ln(st, L, "2", 2)
            em = None if last else self.alloc_hT_emit(st)
            y0 = Ring([(self.sb(st, "cy0_%d" % i, [128, D], F32), Res()) for i in range(2)])
            y1 = Ring([(self.sb(st, "cy1_%d" % i, [128, D], F32), Res()) for i in range(2)])
            hr = Ring([(self.sb(st, "chr_%d" % i, [128, D], F32), Res()) for i in range(2)])
            a_t, ra = self.sb(st, "ca_t", [128, D], F32), Res()
            tmp, rtmp = self.sb(st, "ctmp", [128, D], F32), Res()
            ho = Ring([(self.sb(st, "cho_%d" % i, [128, D], F32), Res()) for i in range(2)])
            xb = Ring([(self.sb(st, "cxb_%d" % i, [128, D], BF16), Res()) for i in range(2)])
            dst = self.out if last else self.H
            for t in range(self.NTT):
                a0, r0 = y0.next()
                a1, r1 = y1.next()
                h_, rh = hr.next()
                S.dma("pool", a0[:], self.YS, reads=[self.rYS, self.rDST[t]], writes=[r0],
                      indirect={"in_offset": bass.IndirectOffsetOnAxis(ap=self.DST[:, t, 0:1], axis=0)})
                S.dma("pool", a1[:], self.YS, reads=[self.rYS, self.rDST[t]], writes=[r1],
                      indirect={"in_offset": bass.IndirectOffsetOnAxis(ap=self.DST[:, t, 1:2], axis=0)})
                S.dma("sp", h_[:], self.H1[t * 128:(t + 1) * 128, :], reads=[self.rH1[t]], writes=[rh])
                S.act(a_t[:], h_[:], AF.Copy, reads=[rh], writes=[ra], scale=ALPHA)
                S.stt(a_t[:], a0[:], self.GW[:, t, 0:1], a_t[:], ALU.mult, ALU.add, reads=[r0, self.rOH[t], ra], writes=[ra])
                S.stt(a_t[:], a1[:], self.GW[:, t, 1:2], a_t[:], ALU.mult, ALU.add, reads=[r1, self.rOH[t], ra], writes=[ra])
                o_, ro = ho.next()
                self.layer_norm_tile(a_t, ra, gb, rgb, 0, o_, ro, stats, rstats, tmp, rtmp)
                S.dma("sp", dst[t * 128:(t + 1) * 128, :], o_[:], reads=[ro],
                      writes=[self.rOUT if last else self.rH[t]])
                if not last:
                    xbt, rxb = xb.next()
                    S.copy("act", xbt[:], o_[:], reads=[ro], writes=[rxb])
                    self.emit_hT(em, xbt, rxb, t // NT, t % NT)
            S.barrier()


_CACHE = {}


def _get_prog(n_seq, layers, dbg=False):
    key = (n_seq, tuple(layers), dbg)
    if key not in _CACHE:
        pr = Prog(n_seq, list(layers), dbg)
        pr.build()
        _CACHE[key] = pr
    return _CACHE[key]


def kernel(**inputs):
    n_cores = 8
    x = np.asarray(inputs["x"], dtype=np.float32)
    B = x.shape[0]
    n_seq = B // n_cores
    pr = _get_prog(n_seq, list(range(DEPTH)))
    in_maps = []
    for c in range(n_cores):
        m = {"x": np.ascontiguousarray(x[c * n_seq:(c + 1) * n_seq].reshape(n_seq * SEQ, D))}
        for i in range(DEPTH):
            for nm in weight_names(i):
                m[nm] = np.ascontiguousarray(np.asarray(inputs[nm], dtype=np.float32))
        m.update(pr.consts)
        in_maps.append(m)
    res = run_bass_kernel_spmd(pr.nc, in_maps, core_ids=list(range(n_cores)))
    out = np.concatenate([np.asarray(r["y"]).reshape(n_seq, SEQ, D) for r in res.results], axis=0)
    return out.astype(np.float32)
```
